# Optimizing a Trainium2 kernel written in Bass

```python
import jax, jax.numpy as jnp
from jax import lax
import numpy as np

D_MODEL = 1024
BATCH = 8
SEQ = 4096
DEPTH = 1

CHUNK = 64
GLA_HEADS = 4
GLA_DK = D_MODEL // 2 // GLA_HEADS
GLA_DV = D_MODEL // GLA_HEADS
GLA_QK = GLA_HEADS * GLA_DK
GLA_V = GLA_HEADS * GLA_DV
GLA_GATE_RANK = 16
GLA_TAU = 16.0
SGU_GROUPS = 4
SGU_BLOCK = 128
SGU_WIDTH = D_MODEL
SGU_DG = SGU_WIDTH // SGU_GROUPS
D_FF = -(-8 * D_MODEL // (3 * 256)) * 256
EPS = 1e-6

IN_SPLITS = (GLA_QK, GLA_QK, GLA_V, GLA_V, GLA_GATE_RANK,
             SGU_WIDTH, SGU_WIDTH, D_MODEL, D_MODEL)
D_IN = int(sum(IN_SPLITS))
IN_OFFSETS = tuple(int(o) for o in np.cumsum(IN_SPLITS)[:-1])

kernel_name = "hybrid_gla_sgu_swiglu_sandwich"


def rmsnorm(x, g):
    xf = x.astype(jnp.float32)
    y = xf * lax.rsqrt(jnp.mean(xf * xf, axis=-1, keepdims=True) + EPS)
    return (y * g.astype(jnp.float32)).astype(x.dtype)


def gla_branch(q, k, v, r, a_low, w_gate_up, b_gate, gla_norm):
    B, S, _ = q.shape
    N = S // CHUNK
    f32 = jnp.float32

    def heads(t, d):
        return t.reshape(B, N, CHUNK, GLA_HEADS, d)

    qh = heads(q, GLA_DK).astype(f32) * (GLA_DK ** -0.5)
    kh = heads(k, GLA_DK).astype(f32)
    vh = heads(v, GLA_DV).astype(f32)
    logit = jnp.einsum('bsr,rk->bsk', a_low, w_gate_up) + b_gate
    log_a = heads(jax.nn.log_sigmoid(logit.astype(f32)) / GLA_TAU, GLA_DK)
    cum = jnp.cumsum(log_a, axis=2)
    tot = cum[:, :, -1]
    k_dec = kh * jnp.exp(tot[:, :, None] - cum)
    upd = jnp.einsum('bnchk,bnchv->bnhkv', k_dec, vh)
    decay = jnp.exp(tot)

    def step(state, inp):
        a_c, u_c = inp
        state = a_c[..., None] * state + u_c
        return state, state

    s0 = jnp.zeros((B, GLA_HEADS, GLA_DK, GLA_DV), f32)
    _, states = lax.scan(step, s0, (jnp.moveaxis(decay, 1, 0), jnp.moveaxis(upd, 1, 0)))
    states = jnp.moveaxis(states, 0, 1)
    o = jnp.einsum('bnchk,bnhkv->bnchv', qh, states)
    o = o * lax.rsqrt(jnp.mean(o * o, axis=-1, keepdims=True) + EPS) * gla_norm.astype(f32)
    o = o.reshape(B, S, GLA_V).astype(q.dtype)
    return o * jax.nn.silu(r)


def sgu_branch(u, v, ln_g, ln_b, w_spatial, b_spatial):
    B, S, _ = u.shape
    u = jax.nn.gelu(u)
    vf = jax.nn.gelu(v).reshape(B, S, SGU_GROUPS, SGU_DG).astype(jnp.float32)
    mu = jnp.mean(vf, axis=-1, keepdims=True)
    var = jnp.mean(jnp.square(vf - mu), axis=-1, keepdims=True)
    vn = (vf - mu) * lax.rsqrt(var + EPS) * ln_g.astype(jnp.float32) + ln_b.astype(jnp.float32)
    vn = vn.astype(u.dtype).reshape(B, S // SGU_BLOCK, SGU_BLOCK, SGU_GROUPS, SGU_DG)
    pos = np.arange(SGU_BLOCK)
    mask = (pos[None, :] // CHUNK) <= (pos[:, None] // CHUNK)
    w = jnp.where(mask[None], w_spatial, 0.0)
    mixed = jnp.einsum('gij,bnjgc->bnigc', w, vn) + b_spatial.T[:, :, None]
    return u * mixed.reshape(B, S, SGU_WIDTH)


def setup_inputs(seed: int = 0) -> dict:
    key = jax.random.key(seed)
    ks = jax.random.split(key, 20)
    L = DEPTH
    nrm = jax.random.normal

    def gain(k, shape):
        return 1.0 + 0.05 * nrm(k, shape, jnp.float32)

    return {
        "x": nrm(ks[0], (BATCH, SEQ, D_MODEL), jnp.float32),
        "norm_pre_mix": gain(ks[1], (L, D_MODEL)),
        "w_in": nrm(ks[2], (L, D_MODEL, D_IN), jnp.float32) * D_MODEL ** -0.5,
        "w_gate_up": nrm(ks[3], (L, GLA_GATE_RANK, GLA_QK), jnp.float32) * GLA_GATE_RANK ** -0.5,
        "b_gate": 0.01 * nrm(ks[4], (L, GLA_QK), jnp.float32),
        "gla_norm": gain(ks[5], (L, GLA_HEADS, GLA_DV)),
        "sgu_ln_g": gain(ks[6], (L, SGU_GROUPS, SGU_DG)),
        "sgu_ln_b": 0.02 * nrm(ks[7], (L, SGU_GROUPS, SGU_DG), jnp.float32),
        "w_spatial": nrm(ks[8], (L, SGU_GROUPS, SGU_BLOCK, SGU_BLOCK), jnp.float32) * SGU_BLOCK ** -0.5,
        "b_spatial": 1.0 + 0.02 * nrm(ks[9], (L, SGU_GROUPS, SGU_BLOCK), jnp.float32),
        "w_branch_gla": nrm(ks[10], (L, GLA_V, D_MODEL), jnp.float32) * GLA_V ** -0.5,
        "w_branch_sgu": nrm(ks[11], (L, SGU_WIDTH, D_MODEL), jnp.float32) * SGU_WIDTH ** -0.5,
        "w_out": nrm(ks[12], (L, D_MODEL, D_MODEL), jnp.float32) * D_MODEL ** -0.5,
        "norm_post_mix": gain(ks[13], (L, D_MODEL)),
        "norm_pre_ffn": gain(ks[14], (L, D_MODEL)),
        "w_ffn_in": nrm(ks[15], (L, D_MODEL, 2 * D_FF), jnp.float32) * D_MODEL ** -0.5,
        "w_ffn_out": nrm(ks[16], (L, D_FF, D_MODEL), jnp.float32) * D_FF ** -0.5,
        "norm_post_ffn": gain(ks[17], (L, D_MODEL)),
    }


def reference(x, norm_pre_mix, w_in, w_gate_up, b_gate, gla_norm, sgu_ln_g, sgu_ln_b,
              w_spatial, b_spatial, w_branch_gla, w_branch_sgu, w_out, norm_post_mix,
              norm_pre_ffn, w_ffn_in, w_ffn_out, norm_post_ffn):
    for l in range(DEPTH):
        a = rmsnorm(x, norm_pre_mix[l])
        proj = jnp.einsum('bsd,de->bse', a, w_in[l])
        q, k, v, r, a_low, su, sv, g_gla, g_sgu = jnp.split(proj, IN_OFFSETS, axis=-1)
        y_gla = gla_branch(q, k, v, r, a_low, w_gate_up[l], b_gate[l], gla_norm[l])
        y_sgu = sgu_branch(su, sv, sgu_ln_g[l], sgu_ln_b[l], w_spatial[l], b_spatial[l])
        merged = (jax.nn.sigmoid(g_gla) * jnp.einsum('bsv,vd->bsd', y_gla, w_branch_gla[l])
                  + jax.nn.sigmoid(g_sgu) * jnp.einsum('bsv,vd->bsd', y_sgu, w_branch_sgu[l]))
        mix = jnp.einsum('bsd,de->bse', merged, w_out[l])
        x = x + rmsnorm(mix, norm_post_mix[l])
        h = rmsnorm(x, norm_pre_ffn[l])
        gate, up = jnp.split(jnp.einsum('bsd,df->bsf', h, w_ffn_in[l]), 2, axis=-1)
        y = jnp.einsum('bsf,fd->bsd', jax.nn.silu(gate) * up, w_ffn_out[l])
        x = x + rmsnorm(y, norm_post_ffn[l])
    return x
```

```python
import contextlib
import numpy as np
import concourse.bass as bass
import concourse.mybir as mybir
from concourse.bass_utils import run_bass_kernel_spmd

F32 = mybir.dt.float32
BF16 = mybir.dt.bfloat16
AF = mybir.ActivationFunctionType
ALU = mybir.AluOpType

D = 1024
DIN = 7184
DFF = 2816
T = 512
NB = 4
import os
KMODE = int(os.environ.get("KMODE", "0"))
STRICT = int(os.environ.get("STRICT", "0"))
EPS = 1e-6


class _Op:
    __slots__ = ("eng", "fn", "deps", "chan", "seq", "waits")

    def __init__(self, eng, fn, chan):
        self.eng = eng
        self.fn = fn
        self.chan = chan
        self.deps = {}
        self.seq = 0
        self.waits = []


class Sched:
    COMPUTE = ("pe", "act", "dve", "pool")

    def __init__(self, nc):
        self.nc = nc
        self.ops = []
        self.last_w = {}
        self.readers = {}
        self.upto = 99
        self.stopped = False

    def phase(self, k):
        self.stopped = k > self.upto

    def op(self, eng, fn, reads=(), writes=(), chan=None, force=False):
        if self.stopped and not force:
            return -1
        idx = len(self.ops)
        o = _Op(eng, fn, chan)
        for r in reads:
            w = self.last_w.get(r)
            if w is not None:
                o.deps[w] = True
        for r in writes:
            w = self.last_w.get(r)
            if w is not None and w not in o.deps:
                o.deps[w] = False
            for rd in self.readers.get(r, ()):
                if rd not in o.deps:
                    o.deps[rd] = False
        for r in reads:
            self.readers.setdefault(r, []).append(idx)
        for r in writes:
            self.last_w[r] = idx
            self.readers[r] = []
        self.ops.append(o)
        return idx

    def emit(self):
        nc = self.nc
        ops = self.ops
        engs = ("pe", "act", "dve", "pool", "sp")
        streams = {e: [] for e in engs}
        for i, o in enumerate(ops):
            streams[o.eng].append(i)
            o.seq = len(streams[o.eng])
        chan_cnt = {}
        dma_val = {}
        for i, o in enumerate(ops):
            if o.chan is not None:
                chan_cnt[o.chan] = chan_cnt.get(o.chan, 0) + 16
                dma_val[i] = chan_cnt[o.chan]
        need_sig = set()
        for e in engs:
            waited = {}
            for i in streams[e]:
                o = ops[i]
                for d, is_raw in sorted(o.deps.items()):
                    p = ops[d]
                    if p.chan is not None:
                        key = ("dma", p.chan)
                        val = dma_val[d]
                        if p.chan == "const":
                            val = chan_cnt[p.chan]
                            dma_val[d] = val
                    else:
                        if p.eng == e and o.chan is None:
                            if e == "pe" or (not is_raw and not STRICT):
                                continue
                        key = ("eng", p.eng)
                        val = p.seq
                    if waited.get(key, 0) >= val:
                        continue
                    waited[key] = val
                    o.waits.append((key, d))
                    if p.chan is None:
                        need_sig.add(d)
        rank = {}
        for e in self.COMPUTE:
            n = 0
            for i in streams[e]:
                if i in need_sig:
                    n += 1
                    rank[i] = n
        self.stats = {e: len(streams[e]) for e in engs}
        with contextlib.ExitStack() as st:
            esem = {e: st.enter_context(nc.semaphore("s_" + e)) for e in self.COMPUTE}
            csem = {}
            for n_, c in enumerate(chan_cnt):
                csem[c] = st.enter_context(nc.semaphore("c%d" % n_))
            block = st.enter_context(nc.Block())

            def run_stream(e, engobj):
                for i in streams[e]:
                    o = ops[i]
                    for key, d in o.waits:
                        if key[0] == "dma":
                            engobj.wait_ge(csem[key[1]], dma_val[d])
                        else:
                            engobj.wait_ge(esem[key[1]], rank[d])
                    ins = o.fn(engobj)
                    if o.chan is not None:
                        ins.then_inc(csem[o.chan], 16)
                    elif i in need_sig:
                        ins.then_inc(esem[e], 1)
                if e == "sp":
                    for c, v in chan_cnt.items():
                        engobj.wait_ge(csem[c], v)

            @block.tensor
            def _(eng):
                run_stream("pe", eng)

            @block.scalar
            def _(eng):
                run_stream("act", eng)

            @block.vector
            def _(eng):
                run_stream("dve", eng)

            @block.gpsimd
            def _(eng):
                run_stream("pool", eng)

            @block.sync
            def _(eng):
                run_stream("sp", eng)


def weight_tiles():
    tl = []
    tl.append(("w_in", 0, 8, 3072, 16))
    tl.append(("w_in", 0, 8, 0, 512))
    tl.append(("w_in", 0, 8, 1024, 512))
    tl.append(("w_in", 0, 8, 1536, 512))
    tl.append(("w_in", 0, 8, 512, 512))
    tl.append(("w_in", 0, 8, 2048, 512))
    tl.append(("w_in", 0, 8, 2560, 512))
    tl.append(("w_in", 0, 8, 3088, 512))
    tl.append(("w_in", 0, 8, 3600, 512))
    tl.append(("w_in", 0, 8, 4112, 512))
    tl.append(("w_in", 0, 8, 4624, 512))
    tl.append(("w_in", 0, 8, 5136, 512))
    tl.append(("w_in", 0, 8, 5648, 512))
    tl.append(("w_in", 0, 8, 6160, 512))
    tl.append(("w_in", 0, 8, 6672, 512))
    for half in range(2):
        tl.append(("w_bg", 0, 8, half * 512, 512))
        tl.append(("w_bs", 0, 8, half * 512, 512))
    tl.append(("w_out", 0, 8, 0, 512))
    tl.append(("w_out", 0, 8, 512, 512))
    for t in range(6):
        ncw = 512 if t < 5 else 256
        tl.append(("w_fi", 0, 8, t * 512, ncw))
        tl.append(("w_fi", 0, 8, DFF + t * 512, ncw))
    for half in range(2):
        for kg, nk in enumerate((8, 8, 6)):
            tl.append(("w_fo", kg * 8 * 128, nk, half * 512, 512))
    return tl


def build(S_tok, upto=99):
    NT = S_tok // T
    nc = bass.Bass("TRN2", target_bir_lowering=False)

    def din(name, shape, dt=F32):
        return nc.dram_tensor(name, shape, dt, kind="ExternalInput").ap()

    x = din("x", [S_tok, D])
    wsrc = {
        "w_in": din("w_in", [D, DIN]),
        "w_bg": din("w_bg", [D, D]),
        "w_bs": din("w_bs", [D, D]),
        "w_out": din("w_out", [D, D]),
        "w_fi": din("w_fi", [D, 2 * DFF]),
        "w_fo": din("w_fo", [DFF, D]),
    }
    wscr = {k: nc.dram_tensor(k + "_bf", list(v.shape), BF16, kind="Internal").ap() for k, v in wsrc.items()}
    c_gpm = din("g_pm", [D])
    c_gpf = din("g_pf", [D])
    c_gqm = din("g_qm", [D])
    c_gqf = din("g_qf", [D])
    c_bg = din("b_gate", [512])
    c_gn = din("gla_norm", [D])
    c_lng = din("ln_g", [D])
    c_lnb = din("ln_b", [D])
    c_wgu = din("w_gu", [16, 512])
    c_wsp = din("wspT", [128, 4, 128])
    c_bsp = din("b_sp", [4, 128])
    out = nc.dram_tensor("out", [S_tok, D], F32, kind="ExternalOutput").ap()

    S = Sched(nc)
    S.upto = upto
    with contextlib.ExitStack() as st:
        def sb(name, shape, dt):
            return st.enter_context(nc.sbuf_tensor(name, shape, dt))

        xt = sb("xt", [128, 4, D], F32)
        aT = sb("aT", [128, 8, T], BF16)
        xs = sb("xs", [128, D], BF16)
        sr = sb("sr", [128, 4, D], BF16)
        alT = sb("alT", [16, T], F32)
        state = sb("state", [128, 4, 256], F32)
        stbf = sb("stbf", [128, 2, 4, 256], BF16)
        sgg = sb("sgg", [128, 8, T], BF16)
        sgs = sb("sgs", [128, 8, T], BF16)
        ytm = sb("ytm", [128, 2, D], F32)
        ygb = sb("ygb", [128, D], BF16)
        ygb2 = sb("ygb2", [128, D], BF16)
        ygbs = [ygb, ygb2]
        big = sb("big", [128, 24, T], BF16)
        rgate = sb("rgate", [128, 8192], BF16)
        rqk = sb("rqk", [128, 8, T], BF16)
        rtmp = sb("rtmp", [128, 4, T], F32)
        stat = sb("stat", [128, 64], F32)
        dec = sb("dec", [128, 32], F32)
        bnst = sb("bnst", [128, 4, 6], F32)
        bnag = sb("bnag", [128, 4, 2], F32)
        wring = [sb("wr%d" % i, [128, 8, 512], BF16) for i in range(NB)]
        gpm = sb("gpm", [128, D], F32)
        gpf = sb("gpf", [128, D], F32)
        gqm = sb("gqm", [128, D], F32)
        gqf = sb("gqf", [128, D], F32)
        gnb = sb("gnb", [128, D], F32)
        lng = sb("lng", [128, D], F32)
        lnb = sb("lnb", [128, D], F32)
        bgb = sb("bgb", [128, 512], F32)
        wgu = sb("wgu", [16, 512], F32)
        wspf = sb("wspf", [128, 4, 128], F32)
        wsp = sb("wsp", [128, 4, 128], BF16)
        bspb = sb("bspb", [128, 4, 128], F32)
        identf = sb("identf", [128, 128], F32)
        ident = sb("ident", [128, 128], BF16)
        mtri = sb("mtri", [128, 128], F32)
        ind = sb("ind", [128, 2], F32)
        mhalf = sb("mhalf", [128, 1], F32)
        junk = sb("junk", [128, D], BF16)
        fz = sb("fz", [128, 1], F32)
        PP = [st.enter_context(nc.psum_tensor("pp%d" % i, [128, 2, 512], F32)) for i in range(4)]

        def bank(b):
            return PP[b // 2][:, b % 2, :]

        def bres(b):
            return ("ps", b)

        vtm = big[:, 0:8, :].rearrange("p (s a) t -> p s (a t)", s=4)
        vn = big[:, 8:16, :].rearrange("p (s a) t -> p s (a t)", s=4)
        u = big[:, 16:24, :]
        actT = big
        loga = rgate[:, 0:4096].bitcast(F32).rearrange("p (s e) -> p s e", s=4)
        edec = rgate[:, 4096:8192].bitcast(F32).rearrange("p (s e) -> p s e", s=4)
        yglaT = rgate[:, 0:4096].rearrange("p (c t) -> p c t", c=8)
        ysguT = rgate[:, 4096:8192].rearrange("p (c t) -> p c t", c=8)
        qT = rqk[:, 0:4, :]
        kdec = rqk[:, 4:8, :]
        mergedT = rqk
        lg = rtmp[:, 0, :]
        e1 = rtmp[:, 1, :]

        def al(group, side, nsides, write):
            r = [("tok", group, side)]
            w = [("tok", group, j) for j in range(nsides) if j != side] if write else []
            return r, w

        def A(eng, fn, reads=(), writes=(), alias=(), chan=None):
            rr = list(reads)
            ww = list(writes)
            for (g, sd, n, wr) in alias:
                r_, w_ = al(g, sd, n, wr)
                rr += r_
                ww += w_
            return S.op(eng, fn, rr, ww, chan)

        G_BIG, G_GATE, G_QK, G_TMP = "big", "gate", "qk", "tmp"

        tiles = weight_tiles()
        NW = len(tiles)
        cstate = {"n": 0}

        def emit_casts(upto_):
            while cstate["n"] < min(NW, upto_):
                i = cstate["n"]
                wn, r0, nk, c0, ncw = tiles[i]
                S.op("pool", lambda e, wn=wn, r0=r0, nk=nk, c0=c0, ncw=ncw:
                     e.dma_start(out=wscr[wn][r0:r0 + nk * 128, c0:c0 + ncw], in_=wsrc[wn][r0:r0 + nk * 128, c0:c0 + ncw]),
                     writes=[("scr", i)], chan=("cast", i))
                cstate["n"] += 1

        def cload(dst, src, res):
            S.op("sp", lambda e: e.dma_start(out=dst, in_=src), writes=[res], chan="const")

        cload(gpm[:], c_gpm.partition_broadcast(128), "gpm")
        cload(gpf[:], c_gpf.partition_broadcast(128), "gpf")
        cload(gqm[:], c_gqm.partition_broadcast(128), "gqm")
        cload(gqf[:], c_gqf.partition_broadcast(128), "gqf")
        cload(gnb[:], c_gn.partition_broadcast(128), "gnb")
        cload(lng[:], c_lng.partition_broadcast(128), "lng")
        cload(lnb[:], c_lnb.partition_broadcast(128), "lnb")
        cload(bgb[:], c_bg.partition_broadcast(128), "bgb")
        cload(wgu[:], c_wgu, "wgu")
        cload(wspf[:], c_wsp, "wspf")
        cload(bspb[:], c_bsp.partition_broadcast(128), "bspb")

        S.op("pool", lambda e: e.memset(identf[:], 0.0), writes=["identf"])
        S.op("pool", lambda e: e.affine_select(out=identf[:], in_=identf[:], pattern=[[-1, 128]],
                                               compare_op=ALU.not_equal, fill=1.0, base=0, channel_multiplier=1),
             reads=["identf"], writes=["identf"])
        S.op("dve", lambda e: e.tensor_copy(out=ident[:], in_=identf[:]), reads=["identf"], writes=["ident"])
        S.op("pool", lambda e: e.memset(mhalf[:], -0.5), writes=["mhalf"])
        emit_casts(8)
        S.op("pool", lambda e: e.memset(mtri[:], -1.0 / 16), writes=["mtri"])
        S.op("pool", lambda e: e.affine_select(out=mtri[:], in_=mtri[:], pattern=[[-1, 128]],
                                               compare_op=ALU.is_gt, fill=0.0, base=0, channel_multiplier=1),
             reads=["mtri"], writes=["mtri"])
        S.op("pool", lambda e: e.memset(mtri[64:128, 0:64], 0.0), reads=["mtri"], writes=["mtri"])
        S.op("pool", lambda e: e.memset(ind[:], 0.0), writes=["ind"])
        S.op("pool", lambda e: e.memset(ind[0:64, 0:1], -1.0 / 16), reads=["ind"], writes=["ind"])
        S.op("pool", lambda e: e.memset(ind[64:128, 1:2], -1.0 / 16), reads=["ind"], writes=["ind"])
        S.op("pool", lambda e: e.memset(state[:], 0.0), writes=[("state", h) for h in range(4)])
        S.op("pool", lambda e: e.memset(wspf[64:128, :, 0:64], 0.0), reads=["wspf"], writes=["wspf"])
        S.op("dve", lambda e: e.tensor_copy(out=wsp[:], in_=wspf[:]), reads=["wspf"], writes=["wsp"])

        total_w = NT * NW
        wstate = {"loaded": 0}

        def emit_load(j):
            i = j % NW
            wn, r0, nk, c0, ncw = tiles[i]
            slot = j % NB
            S.op("sp", lambda e: e.dma_start(
                out=wring[slot][:, 0:nk, 0:ncw],
                in_=wscr[wn][r0:r0 + nk * 128, c0:c0 + ncw].rearrange("(k p) e -> p k e", p=128)),
                reads=[("scr", i)], writes=[("w", slot)], chan=("w", slot))

        def w_acquire(j):
            if j < NW:
                emit_casts(j + 10)
            while wstate["loaded"] <= j and wstate["loaded"] < total_w:
                assert wstate["loaded"] < j + NB
                emit_load(wstate["loaded"])
                wstate["loaded"] += 1
            return wring[j % NB], ("w", j % NB)

        def w_release(j):
            nxt = j + NB
            if nxt < total_w and wstate["loaded"] == nxt:
                emit_load(nxt)
                wstate["loaded"] += 1


        scnt = [0]

        def scol():
            c = scnt[0] % 64
            scnt[0] += 1
            return c

        def rstd_from(ss_col, n, res_in):
            c1 = scol()
            c2 = scol()
            S.op("dve", lambda e: e.tensor_scalar(out=stat[:, c1:c1 + 1], in0=stat[:, ss_col:ss_col + 1],
                                                  scalar1=1.0 / n, scalar2=EPS, op0=ALU.mult, op1=ALU.add),
                 reads=[res_in], writes=[("stat", c1)])
            S.op("pool", lambda e: e.tensor_tensor(out=stat[:, c2:c2 + 1], in0=stat[:, c1:c1 + 1], in1=mhalf[:], op=ALU.pow),
                 reads=[("stat", c1), "mhalf"], writes=[("stat", c2)])
            return c2, ("stat", c2)

        def prenorm_transposes(s, gtile, gres, tpb):
            c0 = scol()
            S.op("act", lambda e: e.activation(out=junk[:], in_=xt[:, s, :], func=AF.Square, accum_out=stat[:, c0:c0 + 1]),
                 reads=[("xt", s)], writes=[("stat", c0)])
            c2, r2 = rstd_from(c0, D, ("stat", c0))
            S.op("dve", lambda e: e.scalar_tensor_tensor(out=xs[:], in0=xt[:, s, :], scalar=stat[:, c2:c2 + 1], in1=gtile[:],
                                                         op0=ALU.mult, op1=ALU.mult),
                 reads=[("xt", s), r2, gres], writes=["xs"])
            pv = bank(tpb).bitcast(BF16).rearrange("p (k t) -> p k t", k=8)
            for k in range(8):
                S.op("pe", lambda e, k=k: e.transpose(out=pv[:, k, :], in_=xs[:, k * 128:(k + 1) * 128], identity=ident[:]),
                     reads=["xs", "ident"], writes=[bres(tpb)])
            S.op("act", lambda e: e.activation(out=aT[:, :, s * 128:(s + 1) * 128], in_=pv, func=AF.Copy),
                 reads=[bres(tpb)], writes=[("aT", s)])

        def postnorm_residual(s, pp, gtile, gres, final, t):
            src = PP[pp][:].rearrange("p a b -> p (a b)")
            c0 = scol()
            yb = ytm[:, s % 2, :]
            S.op("act", lambda e: e.activation(out=junk[:], in_=src, func=AF.Square, accum_out=stat[:, c0:c0 + 1]),
                 reads=[bres(2 * pp), bres(2 * pp + 1)], writes=[("stat", c0)])
            c2, r2 = rstd_from(c0, D, ("stat", c0))
            S.op("dve", lambda e: e.scalar_tensor_tensor(out=yb, in0=src, scalar=stat[:, c2:c2 + 1], in1=gtile[:],
                                                         op0=ALU.mult, op1=ALU.mult),
                 reads=[bres(2 * pp), bres(2 * pp + 1), r2, gres], writes=[("ytm", s % 2)])
            S.op("pool", lambda e: e.tensor_tensor(out=xt[:, s, :], in0=xt[:, s, :], in1=yb, op=ALU.add),
                 reads=[("xt", s), ("ytm", s % 2)], writes=[("xt", s)])
            if final:
                S.op("sp", lambda e: e.dma_start(out=out[t * T + s * 128:t * T + (s + 1) * 128, :], in_=xt[:, s, :]),
                     reads=[("xt", s)], chan=("st", s), force=True)

        def fence(res):
            S.op("dve", lambda e: e.memset(fz[:], 0.0), writes=list(res) + ["fz"])

        XT_ALL = [("xt", s) for s in range(4)]
        AT_ALL = [("aT", s) for s in range(4)]

        for t in range(NT):
            wj = t * NW
            S.phase(0)
            S.op("sp", lambda e, t=t: e.dma_start(out=xt[:], in_=x[t * T:(t + 1) * T, :].rearrange("(s p) d -> p s d", p=128)),
                 writes=XT_ALL, chan="xld")
            if t == 0:
                for j in range(min(NB, total_w)):
                    emit_load(j)
                    wstate["loaded"] += 1
            S.phase(1)
            for s in range(4):
                prenorm_transposes(s, gpm, "gpm", s % 2)

            S.phase(2)
            fb = [0]

            def fbank():
                b = 4 + (fb[0] % 4)
                fb[0] += 1
                return b

            def fm_items(evac):
                nonlocal wj
                j = wj
                wj += 1
                items = []
                for c4 in range(4):
                    def item(c4=c4, j=j):
                        wt, wr = w_acquire(j)
                        pb = fbank()
                        for kc in range(8):
                            S.op("pe", lambda e, kc=kc, c4=c4, pb=pb, wt=wt: e.matmul(
                                bank(pb), lhsT=wt[:, kc, c4 * 128:(c4 + 1) * 128], rhs=aT[:, kc, :],
                                start=(kc == 0), stop=(kc == 7)),
                                reads=AT_ALL + [wr], writes=[bres(pb)])
                        evac(c4, pb)
                        if c4 == 3:
                            w_release(j)
                    items.append(item)
                return items

            def tm_items(evac):
                nonlocal wj
                j = wj
                wj += 1
                items = []
                for s in range(4):
                    def item(s=s, j=j):
                        wt, wr = w_acquire(j)
                        pb = fbank()
                        for kc in range(8):
                            S.op("pe", lambda e, kc=kc, s=s, pb=pb, wt=wt: e.matmul(
                                bank(pb), lhsT=aT[:, kc, s * 128:(s + 1) * 128], rhs=wt[:, kc, :],
                                start=(kc == 0), stop=(kc == 7)),
                                reads=[("aT", s), wr], writes=[bres(pb)])
                        evac(s, pb)
                        if s == 3:
                            w_release(j)
                    items.append(item)
                return items

            wt, wr = w_acquire(wj)
            for kc in range(8):
                S.op("pe", lambda e, kc=kc, wt=wt: e.matmul(bank(2)[0:16, :], lhsT=wt[:, kc, 0:16], rhs=aT[:, kc, :],
                                                             start=(kc == 0), stop=(kc == 7)),
                     reads=AT_ALL + [wr], writes=[bres(2)])
            w_release(wj)
            wj += 1
            S.op("dve", lambda e: e.tensor_copy(out=alT[:], in_=bank(2)[0:16, :]), reads=[bres(2)], writes=["alT"])

            def gate_logits(s):
                pb = 2 + (s % 2)
                S.op("pe", lambda e, s=s, pb=pb: e.matmul(bank(pb), lhsT=alT[:, s * 128:(s + 1) * 128], rhs=wgu[:], start=True, stop=True),
                     reads=["alT", "wgu"], writes=[bres(pb)])
                lgs = rtmp[:, s % 2, :]
                e1s = rtmp[:, 2 + (s % 2), :]
                A("dve", lambda e, pb=pb, lgs=lgs: e.tensor_tensor(out=lgs, in0=bank(pb), in1=bgb[:], op=ALU.add),
                  reads=[bres(pb), "bgb"], writes=[("lg", s % 2)], alias=[(G_TMP, 0, 4, True)])
                A("act", lambda e, lgs=lgs, e1s=e1s: e.activation(out=e1s, in_=lgs, func=AF.Exp, scale=-1.0),
                  reads=[("lg", s % 2)], writes=[("e1", s % 2)], alias=[(G_TMP, 0, 4, True)])
                A("act", lambda e, s=s, e1s=e1s: e.activation(out=loga[:, s, :], in_=e1s, func=AF.Ln, bias=1.0),
                  reads=[("e1", s % 2)], writes=[("loga", s)], alias=[(G_TMP, 0, 4, False), (G_GATE, 0, 2, True)])

            def gate_cumsum(s):
                pb = 2 + (s % 2)
                S.op("pe", lambda e, s=s, pb=pb: e.matmul(bank(pb), lhsT=mtri[:], rhs=loga[:, s, :], start=True, stop=True),
                     reads=["mtri", ("loga", s), ("tok", G_GATE, 0)], writes=[bres(pb)])
                A("act", lambda e, s=s, pb=pb: e.activation(out=edec[:, s, :], in_=bank(pb), func=AF.Exp),
                  reads=[bres(pb)], writes=[("edec", s)], alias=[(G_GATE, 0, 2, True)])

            def ev_q(h, pb):
                A("act", lambda e: e.activation(out=qT[:, h, :], in_=bank(pb), func=AF.Copy, scale=float(128 ** -0.5)),
                  reads=[bres(pb)], writes=[("qT", h)], alias=[(G_QK, 0, 2, True)])

            def mk_ev_v(half):
                def ev(s, pb):
                    A("dve", lambda e: e.tensor_copy(out=vtm[:, s, half * 512:(half + 1) * 512], in_=bank(pb)),
                      reads=[bres(pb)], writes=[("vtm", s, half)], alias=[(G_BIG, 0, 2, True)])
                return ev

            def ev_k(s, pb):
                A("dve", lambda e: e.tensor_tensor(out=kdec[:, s, :], in0=bank(pb), in1=edec[:, s, :], op=ALU.mult),
                  reads=[bres(pb), ("edec", s)], writes=[("kdec", s)], alias=[(G_QK, 0, 2, True), (G_GATE, 0, 2, False)])

            def mk_ev_r(half):
                def ev(s, pb):
                    S.op("act", lambda e: e.activation(out=sr[:, s, half * 512:(half + 1) * 512], in_=bank(pb), func=AF.Silu),
                         reads=[bres(pb)], writes=[("sr", s, half)])
                return ev

            def mk_ev_su(half):
                def ev(c4, pb):
                    ch = half * 4 + c4
                    A("act", lambda e: e.activation(out=u[:, ch, :], in_=bank(pb), func=AF.Gelu_apprx_tanh),
                      reads=[bres(pb)], writes=[("u", ch)], alias=[(G_BIG, 0, 2, True)])
                return ev

            gvc = [0]

            def mk_ev_sv(half):
                def ev(s, pb):
                    gi = gvc[0] % 4
                    gvc[0] += 1
                    gv = rtmp[:, gi, :]
                    gres = ("gv", gi)
                    A("act", lambda e: e.activation(out=gv, in_=bank(pb), func=AF.Gelu_apprx_tanh),
                      reads=[bres(pb)], writes=[gres], alias=[(G_TMP, 3, 4, True)])
                    cs = []
                    for g2 in range(2):
                        S.op("dve", lambda e, g2=g2: e.bn_stats(out=bnst[:, g2, :], in_=gv[:, g2 * 256:(g2 + 1) * 256]),
                             reads=[gres, ("tok", G_TMP, 3)], writes=[("bnst", g2)])
                        S.op("dve", lambda e, g2=g2: e.bn_aggr(out=bnag[:, g2, :], in_=bnst[:, g2, :]),
                             reads=[("bnst", g2)], writes=[("bnag", g2)])
                        c1 = scol()
                        c2 = scol()
                        c3 = scol()
                        S.op("dve", lambda e, g2=g2, c1=c1, c3=c3: e.tensor_scalar(out=stat[:, c1:c1 + 1], in0=bnag[:, g2, 1:2], scalar1=EPS, scalar2=None, op0=ALU.add),
                             reads=[("bnag", g2)], writes=[("stat", c1)])
                        S.op("dve", lambda e, g2=g2, c3=c3: e.tensor_copy(out=stat[:, c3:c3 + 1], in_=bnag[:, g2, 0:1]),
                             reads=[("bnag", g2)], writes=[("stat", c3)])
                        S.op("pool", lambda e, c1=c1, c2=c2: e.tensor_tensor(out=stat[:, c2:c2 + 1], in0=stat[:, c1:c1 + 1], in1=mhalf[:], op=ALU.pow),
                             reads=[("stat", c1), "mhalf"], writes=[("stat", c2)])
                        cs.append((c2, c3))
                    for g2 in range(2):
                        c2, c3 = cs[g2]
                        A("pool", lambda e, g2=g2, c2=c2, c3=c3: e.tensor_scalar(out=gv[:, g2 * 256:(g2 + 1) * 256], in0=gv[:, g2 * 256:(g2 + 1) * 256],
                                                                             scalar1=stat[:, c3:c3 + 1], scalar2=stat[:, c2:c2 + 1],
                                                                             op0=ALU.subtract, op1=ALU.mult),
                          reads=[gres, ("stat", c3), ("stat", c2)], writes=[gres], alias=[(G_TMP, 3, 4, True)])
                    A("pool", lambda e: e.tensor_tensor(out=gv, in0=gv, in1=lng[:, half * 512:(half + 1) * 512], op=ALU.mult),
                      reads=[gres, "lng"], writes=[gres], alias=[(G_TMP, 3, 4, True)])
                    A("pool", lambda e: e.tensor_tensor(out=vn[:, s, half * 512:(half + 1) * 512], in0=gv, in1=lnb[:, half * 512:(half + 1) * 512], op=ALU.add),
                      reads=[gres, "lnb", ("tok", G_TMP, 3)], writes=[("vn", s, half)], alias=[(G_BIG, 0, 2, True)])
                return ev

            def mk_ev_sig(dst, name, half):
                def ev(c4, pb):
                    ch = half * 4 + c4
                    S.op("act", lambda e: e.activation(out=dst[:, ch, :], in_=bank(pb), func=AF.Sigmoid),
                         reads=[bres(pb)], writes=[(name, ch)])
                return ev

            gate_logits(0)
            gate_logits(1)
            S.phase(3)
            for it in fm_items(ev_q):
                it()
            S.phase(2)
            gate_cumsum(0)
            gate_cumsum(1)
            gate_logits(2)
            gate_logits(3)
            S.phase(3)
            for it in tm_items(mk_ev_v(0)):
                it()
            S.phase(2)
            gate_cumsum(2)
            gate_cumsum(3)
            for s in range(4):
                for h in range(4):
                    cc = (s * 4 + h) * 2
                    S.op("pe", lambda e, s=s, h=h, cc=cc: e.matmul(bank(2)[:, cc:cc + 2], lhsT=loga[:, s, h * 128:(h + 1) * 128], rhs=ind[:],
                                                                    start=True, stop=True),
                         reads=[("loga", s), "ind", ("tok", G_GATE, 0)], writes=[bres(2)])
            S.op("act", lambda e: e.activation(out=dec[:], in_=bank(2)[:, 0:32], func=AF.Exp), reads=[bres(2)], writes=["dec"])
            S.phase(3)
            for it in tm_items(mk_ev_v(1)):
                it()
            for it in tm_items(ev_k):
                it()

            S.phase(4)
            from collections import deque
            filler = deque()
            filler.extend(tm_items(mk_ev_r(0)))
            filler.extend(tm_items(mk_ev_r(1)))
            filler.extend(fm_items(mk_ev_su(0)))
            filler.extend(fm_items(mk_ev_su(1)))
            filler.extend(tm_items(mk_ev_sv(0)))
            filler.extend(tm_items(mk_ev_sv(1)))
            filler.extend(fm_items(mk_ev_sig(sgg, "sgg", 0)))
            filler.extend(fm_items(mk_ev_sig(sgg, "sgg", 1)))
            filler.extend(fm_items(mk_ev_sig(sgs, "sgs", 0)))
            filler.extend(fm_items(mk_ev_sig(sgs, "sgs", 1)))

            def pull(n):
                for _ in range(n):
                    if filler:
                        filler.popleft()()

            gla_on = S.upto >= 5
            if KMODE == 1:
                pull(len(filler))
            UQ = []
            OPS = []
            if gla_on:
                fence(UQ + [bres(0), bres(1)])
                fence(OPS + [bres(2), bres(3)])
            pending = deque()
            for s in range(4):
                opv = PP[1][:].rearrange("p a b -> p (a b)")
                for j in range(2):
                    par = j
                    rows = slice(64 * j, 64 * (j + 1))
                    for h in range(4):
                        if not gla_on:
                            break
                        uoff = (h // 2) * 256
                        upd = bank(h % 2)[:, uoff:uoff + 256]
                        ures = bres(h % 2)
                        A("pe", lambda e, s=s, h=h, rows=rows, upd=upd: e.matmul(
                            upd, lhsT=kdec[rows, s, h * 128:(h + 1) * 128], rhs=vtm[rows, s, h * 256:(h + 1) * 256], start=True, stop=True),
                          reads=[("kdec", s), ("vtm", s, h // 2)], writes=[ures], alias=[(G_QK, 0, 2, False), (G_BIG, 0, 2, False)])
                        dcol = (s * 4 + h) * 2 + j
                        S.op("dve", lambda e, h=h, upd=upd, dcol=dcol: e.scalar_tensor_tensor(
                            out=state[:, h, :], in0=state[:, h, :], scalar=dec[:, dcol:dcol + 1], in1=upd, op0=ALU.mult, op1=ALU.add),
                            reads=[("state", h), "dec", ures], writes=[("state", h)])
                        S.op("act", lambda e, h=h, par=par: e.activation(out=stbf[:, par, h, :], in_=state[:, h, :], func=AF.Copy),
                             reads=[("state", h)], writes=[("stbf", par, h)])
                        if h == 1 and KMODE == 0:
                            pull(1)
                    if KMODE == 0:
                        pull(3)
                    if j == 0 and pending:
                        pending.popleft()()
                    for h in range(4):
                        if not gla_on:
                            break
                        A("pe", lambda e, s=s, h=h, j=j, par=par, rows=rows: e.matmul(
                            PP[1][rows, :, :].rearrange("p a b -> p (a b)")[:, h * 256:(h + 1) * 256],
                            lhsT=qT[:, h, s * 128 + 64 * j:s * 128 + 64 * (j + 1)], rhs=stbf[:, par, h, :], start=True, stop=True),
                          reads=[("qT", h), ("stbf", par, h)], writes=[bres(2 + h // 2)], alias=[(G_QK, 0, 2, False)])
                    if KMODE == 2:
                        pull(4)
                if not gla_on:
                    continue
                yb = ytm[:, s % 2, :]
                c0s = []
                for h in range(4):
                    c0 = scol()
                    c0s.append(c0)
                    S.op("act", lambda e, h=h, c0=c0, opv=opv: e.activation(out=junk[:, h * 256:(h + 1) * 256], in_=opv[:, h * 256:(h + 1) * 256],
                                                               func=AF.Square, accum_out=stat[:, c0:c0 + 1]),
                         reads=[bres(2 + h // 2)], writes=[("stat", c0)])
                for h in range(4):
                    c2, r2 = rstd_from(c0s[h], 256, ("stat", c0s[h]))
                    S.op("dve", lambda e, h=h, c2=c2, yb=yb, opv=opv: e.scalar_tensor_tensor(
                        out=yb[:, h * 256:(h + 1) * 256], in0=opv[:, h * 256:(h + 1) * 256], scalar=stat[:, c2:c2 + 1],
                        in1=gnb[:, h * 256:(h + 1) * 256], op0=ALU.mult, op1=ALU.mult),
                        reads=[bres(2 + h // 2), r2, "gnb"], writes=[("ytm", s % 2)])
                S.op("pool", lambda e, s=s, yb=yb: e.tensor_tensor(out=ygbs[s % 2][:], in0=yb, in1=sr[:, s, :], op=ALU.mult),
                     reads=[("ytm", s % 2), ("sr", s, 0), ("sr", s, 1)], writes=[("ygb", s % 2)])
                def ytrans(s=s):
                    tb = fbank()
                    pv = bank(tb).bitcast(BF16).rearrange("p (k t) -> p k t", k=8)
                    for k in range(8):
                        S.op("pe", lambda e, k=k, pv=pv: e.transpose(out=pv[:, k, :], in_=ygbs[s % 2][:, k * 128:(k + 1) * 128], identity=ident[:]),
                             reads=[("ygb", s % 2), "ident"], writes=[bres(tb)])
                    A("act", lambda e, s=s, pv=pv: e.activation(out=yglaT[:, :, s * 128:(s + 1) * 128], in_=pv, func=AF.Copy),
                      reads=[bres(tb)], writes=[("ygla", s)], alias=[(G_GATE, 1, 2, True)])
                pending.append(ytrans)
            pull(4)
            while pending:
                pending.popleft()()
            pull(len(filler))

            S.phase(6)
            for s in range(4):
                pp = 1 + (s % 2)
                mv = PP[pp][:].rearrange("p a (c i) -> p (a c) i", c=4)
                for g in range(4):
                    for cc in range(2):
                        ch = g * 2 + cc
                        A("pe", lambda e, s=s, g=g, cc=cc, ch=ch, mv=mv: e.matmul(
                            mv[:, ch, :], lhsT=vn[:, s, g * 256 + cc * 128:g * 256 + (cc + 1) * 128], rhs=wsp[:, g, :], start=True, stop=True),
                          reads=[("vn", s, g // 2), "wsp"], writes=[bres(2 * pp + ch // 4)], alias=[(G_BIG, 0, 2, False)])
                yb4 = ytm[:, s % 2, :].rearrange("p (g c i) -> p g c i", g=4, c=2)
                S.op("dve", lambda e, mv=mv, yb4=yb4: e.tensor_tensor(
                    out=yb4, in0=mv.rearrange("p (g c) i -> p g c i", g=4),
                    in1=bspb[:].unsqueeze(2).broadcast_to([128, 4, 2, 128]), op=ALU.add),
                    reads=[bres(2 * pp), bres(2 * pp + 1), "bspb"], writes=[("ytm", s % 2)])
                A("dve", lambda e, s=s: e.tensor_tensor(out=ysguT[:, :, s * 128:(s + 1) * 128],
                                                        in0=ytm[:, s % 2, :].rearrange("p (c i) -> p c i", c=8),
                                                        in1=u[:, :, s * 128:(s + 1) * 128], op=ALU.mult),
                  reads=[("ytm", s % 2)] + [("u", ch) for ch in range(8)], writes=[("ysgu", s)],
                  alias=[(G_GATE, 1, 2, True), (G_BIG, 0, 2, False)])

            S.phase(7)
            YG = [("ygla", s) for s in range(4)]
            YS = [("ysgu", s) for s in range(4)]
            for half in range(2):
                wtg, wrg = w_acquire(wj)
                wts, wrs = w_acquire(wj + 1)
                for c4 in range(4):
                    ch = half * 4 + c4
                    pp = 1 + (ch % 3)
                    for kc in range(8):
                        A("pe", lambda e, kc=kc, c4=c4, pp=pp, wtg=wtg: e.matmul(
                            PP[pp][:, 0, :], lhsT=wtg[:, kc, c4 * 128:(c4 + 1) * 128], rhs=yglaT[:, kc, :], start=(kc == 0), stop=(kc == 7)),
                          reads=YG + [wrg], writes=[bres(2 * pp)], alias=[(G_GATE, 1, 2, False)])
                    for kc in range(8):
                        A("pe", lambda e, kc=kc, c4=c4, pp=pp, wts=wts: e.matmul(
                            PP[pp][:, 1, :], lhsT=wts[:, kc, c4 * 128:(c4 + 1) * 128], rhs=ysguT[:, kc, :], start=(kc == 0), stop=(kc == 7)),
                          reads=YS + [wrs], writes=[bres(2 * pp + 1)], alias=[(G_GATE, 1, 2, False)])
                    t1 = rtmp[:, (ch % 2) * 2, :]
                    t2 = rtmp[:, (ch % 2) * 2 + 1, :]
                    A("dve", lambda e, ch=ch, pp=pp, t1=t1: e.tensor_tensor(out=t1, in0=PP[pp][:, 0, :], in1=sgg[:, ch, :], op=ALU.mult),
                      reads=[bres(2 * pp), ("sgg", ch)], writes=[("t1", ch % 2)], alias=[(G_TMP, 1, 4, True)])
                    A("dve", lambda e, ch=ch, pp=pp, t2=t2: e.tensor_tensor(out=t2, in0=PP[pp][:, 1, :], in1=sgs[:, ch, :], op=ALU.mult),
                      reads=[bres(2 * pp + 1), ("sgs", ch)], writes=[("t2", ch % 2)], alias=[(G_TMP, 1, 4, True)])
                    A("pool", lambda e, ch=ch, t1=t1, t2=t2: e.tensor_tensor(out=mergedT[:, ch, :], in0=t1, in1=t2, op=ALU.add),
                      reads=[("t1", ch % 2), ("t2", ch % 2)], writes=[("merged", ch)], alias=[(G_TMP, 1, 4, False), (G_QK, 1, 2, True)])
                w_release(wj)
                w_release(wj + 1)
                wj += 2

            S.phase(8)
            MG = [("merged", ch) for ch in range(8)]
            wt0, wr0 = w_acquire(wj)
            wt1, wr1 = w_acquire(wj + 1)
            for s in range(4):
                pp = 1 + (s % 3)
                for half, (wt, wr) in enumerate(((wt0, wr0), (wt1, wr1))):
                    for kc in range(8):
                        A("pe", lambda e, kc=kc, s=s, pp=pp, half=half, wt=wt: e.matmul(
                            PP[pp][:, half, :], lhsT=mergedT[:, kc, s * 128:(s + 1) * 128], rhs=wt[:, kc, :], start=(kc == 0), stop=(kc == 7)),
                          reads=MG + [wr], writes=[bres(2 * pp + half)], alias=[(G_QK, 1, 2, False)])
                postnorm_residual(s, pp, gqm, "gqm", False, t)
            w_release(wj)
            w_release(wj + 1)
            wj += 2

            S.phase(9)
            for s in range(4):
                prenorm_transposes(s, gpf, "gpf", s % 2)

            S.phase(10)
            fc = 0
            for tt in range(6):
                wtg, wrg = w_acquire(wj)
                wtu, wru = w_acquire(wj + 1)
                for c4 in range(4 if tt < 5 else 2):
                    pp = 1 + (fc % 3)
                    for kc in range(8):
                        S.op("pe", lambda e, kc=kc, c4=c4, pp=pp, wtg=wtg: e.matmul(
                            PP[pp][:, 0, :], lhsT=wtg[:, kc, c4 * 128:(c4 + 1) * 128], rhs=aT[:, kc, :], start=(kc == 0), stop=(kc == 7)),
                            reads=AT_ALL + [wrg], writes=[bres(2 * pp)])
                    for kc in range(8):
                        S.op("pe", lambda e, kc=kc, c4=c4, pp=pp, wtu=wtu: e.matmul(
                            PP[pp][:, 1, :], lhsT=wtu[:, kc, c4 * 128:(c4 + 1) * 128], rhs=aT[:, kc, :], start=(kc == 0), stop=(kc == 7)),
                            reads=AT_ALL + [wru], writes=[bres(2 * pp + 1)])
                    sgt = rtmp[:, fc % 2, :]
                    A("act", lambda e, pp=pp, sgt=sgt: e.activation(out=sgt, in_=PP[pp][:, 0, :], func=AF.Silu),
                      reads=[bres(2 * pp)], writes=[("sgt", fc % 2)], alias=[(G_TMP, 2, 4, True)])
                    A("dve", lambda e, pp=pp, sgt=sgt, fc=fc: e.tensor_tensor(out=actT[:, fc, :], in0=PP[pp][:, 1, :], in1=sgt, op=ALU.mult),
                      reads=[bres(2 * pp + 1), ("sgt", fc % 2)], writes=[("act", fc)], alias=[(G_TMP, 2, 4, False), (G_BIG, 1, 2, True)])
                    fc += 1
                w_release(wj)
                w_release(wj + 1)
                wj += 2

            S.phase(11)
            for half in range(2):
                for kg, nk in enumerate((8, 8, 6)):
                    wt, wr = w_acquire(wj)
                    for s in range(4):
                        for k in range(nk):
                            fcc = kg * 8 + k
                            A("pe", lambda e, s=s, k=k, fcc=fcc, half=half, wt=wt: e.matmul(
                                PP[s][:, half, :], lhsT=actT[:, fcc, s * 128:(s + 1) * 128], rhs=wt[:, k, :],
                                start=(fcc == 0), stop=(fcc == 21)),
                              reads=[("act", fcc), wr], writes=[bres(2 * s + half)], alias=[(G_BIG, 1, 2, False)])
                    w_release(wj)
                    wj += 1
            for s in range(4):
                postnorm_residual(s, s, gqf, "gqf", True, t)

        S.emit()
    return nc, S


_CACHE = {}


def _prep_consts(inp):
    f = lambda a: np.ascontiguousarray(np.asarray(a, dtype=np.float32))
    return {
        "w_in": f(inp["w_in"][0]),
        "w_bg": f(inp["w_branch_gla"][0]),
        "w_bs": f(inp["w_branch_sgu"][0]),
        "w_out": f(inp["w_out"][0]),
        "w_fi": f(inp["w_ffn_in"][0]),
        "w_fo": f(inp["w_ffn_out"][0]),
        "g_pm": f(inp["norm_pre_mix"][0]),
        "g_pf": f(inp["norm_pre_ffn"][0]),
        "g_qm": f(inp["norm_post_mix"][0]),
        "g_qf": f(inp["norm_post_ffn"][0]),
        "b_gate": f(inp["b_gate"][0]),
        "gla_norm": f(np.asarray(inp["gla_norm"][0]).reshape(-1)),
        "ln_g": f(np.asarray(inp["sgu_ln_g"][0]).reshape(-1)),
        "ln_b": f(np.asarray(inp["sgu_ln_b"][0]).reshape(-1)),
        "w_gu": f(inp["w_gate_up"][0]),
        "wspT": f(np.transpose(np.asarray(inp["w_spatial"][0]), (2, 0, 1))),
        "b_sp": f(np.asarray(inp["b_spatial"][0])),
    }


def kernel(**inputs):
    x = np.asarray(inputs["x"], dtype=np.float32)
    B, S_tok, _ = x.shape
    key = S_tok
    if key not in _CACHE:
        _CACHE[key] = build(S_tok)[0]
    nc = _CACHE[key]
    consts = _prep_consts(inputs)
    in_maps = []
    for b in range(B):
        m = dict(consts)
        m["x"] = np.ascontiguousarray(x[b])
        in_maps.append(m)
    res = run_bass_kernel_spmd(nc, in_maps, core_ids=list(range(B)))
    return np.stack([np.asarray(r["out"]) for r in res.results], axis=0).astype(np.float32)


def _simulate(S):
    ops = S.ops
    engs = ("pe", "act", "dve", "pool", "sp")
    streams = {e: [i for i, o in enumerate(ops) if o.eng == e] for e in engs}
    pos = {e: 0 for e in engs}
    done = set()
    progress = True
    while progress:
        progress = False
        for e in engs:
            while pos[e] < len(streams[e]):
                i = streams[e][pos[e]]
                if all(d in done for (_k, d) in ops[i].waits):
                    done.add(i)
                    pos[e] += 1
                    progress = True
                else:
                    break
    stuck = {e: (pos[e], len(streams[e])) for e in engs if pos[e] < len(streams[e])}
    return stuck
```

```python
import contextlib
import numpy as np
import concourse.bass as bass
import concourse.mybir as mybir
from concourse.bass_utils import run_bass_kernel_spmd

F32 = mybir.dt.float32
BF16 = mybir.dt.bfloat16
AF = mybir.ActivationFunctionType
ALU = mybir.AluOpType

D = 1024
DIN = 7184
DFF = 2816
T = 512
NB = 4
import os
KMODE = int(os.environ.get("KMODE", "0"))
STRICT = int(os.environ.get("STRICT", "0"))
EPS = 1e-6


class _Op:
    __slots__ = ("eng", "fn", "deps", "chan", "seq", "waits")

    def __init__(self, eng, fn, chan):
        self.eng = eng
        self.fn = fn
        self.chan = chan
        self.deps = {}
        self.seq = 0
        self.waits = []


class Sched:
    COMPUTE = ("pe", "act", "dve", "pool")

    def __init__(self, nc):
        self.nc = nc
        self.ops = []
        self.last_w = {}
        self.readers = {}
        self.upto = 99
        self.stopped = False

    def phase(self, k):
        self.stopped = k > self.upto

    def op(self, eng, fn, reads=(), writes=(), chan=None, force=False):
        if self.stopped and not force:
            return -1
        idx = len(self.ops)
        o = _Op(eng, fn, chan)
        for r in reads:
            w = self.last_w.get(r)
            if w is not None:
                o.deps[w] = True
        for r in writes:
            w = self.last_w.get(r)
            if w is not None and w not in o.deps:
                o.deps[w] = False
            for rd in self.readers.get(r, ()):
                if rd not in o.deps:
                    o.deps[rd] = False
        for r in reads:
            self.readers.setdefault(r, []).append(idx)
        for r in writes:
            self.last_w[r] = idx
            self.readers[r] = []
        self.ops.append(o)
        return idx

    def emit(self):
        nc = self.nc
        ops = self.ops
        engs = ("pe", "act", "dve", "pool", "sp")
        streams = {e: [] for e in engs}
        for i, o in enumerate(ops):
            streams[o.eng].append(i)
            o.seq = len(streams[o.eng])
        chan_cnt = {}
        dma_val = {}
        for i, o in enumerate(ops):
            if o.chan is not None:
                chan_cnt[o.chan] = chan_cnt.get(o.chan, 0) + 16
                dma_val[i] = chan_cnt[o.chan]
        need_sig = set()
        for e in engs:
            waited = {}
            for i in streams[e]:
                o = ops[i]
                for d, is_raw in sorted(o.deps.items()):
                    p = ops[d]
                    if p.chan is not None:
                        key = ("dma", p.chan)
                        val = dma_val[d]
                        if p.chan == "const":
                            val = chan_cnt[p.chan]
                            dma_val[d] = val
                    else:
                        if p.eng == e and o.chan is None:
                            if e == "pe" or (not is_raw and not STRICT):
                                continue
                        key = ("eng", p.eng)
                        val = p.seq
                    if waited.get(key, 0) >= val:
                        continue
                    waited[key] = val
                    o.waits.append((key, d))
                    if p.chan is None:
                        need_sig.add(d)
        rank = {}
        for e in self.COMPUTE:
            n = 0
            for i in streams[e]:
                if i in need_sig:
                    n += 1
                    rank[i] = n
        self.stats = {e: len(streams[e]) for e in engs}
        with contextlib.ExitStack() as st:
            esem = {e: st.enter_context(nc.semaphore("s_" + e)) for e in self.COMPUTE}
            csem = {}
            for n_, c in enumerate(chan_cnt):
                csem[c] = st.enter_context(nc.semaphore("c%d" % n_))
            block = st.enter_context(nc.Block())

            def run_stream(e, engobj):
                for i in streams[e]:
                    o = ops[i]
                    for key, d in o.waits:
                        if key[0] == "dma":
                            engobj.wait_ge(csem[key[1]], dma_val[d])
                        else:
                            engobj.wait_ge(esem[key[1]], rank[d])
                    ins = o.fn(engobj)
                    if o.chan is not None:
                        ins.then_inc(csem[o.chan], 16)
                    elif i in need_sig:
                        ins.then_inc(esem[e], 1)
                if e == "sp":
                    for c, v in chan_cnt.items():
                        engobj.wait_ge(csem[c], v)

            @block.tensor
            def _(eng):
                run_stream("pe", eng)

            @block.scalar
            def _(eng):
                run_stream("act", eng)

            @block.vector
            def _(eng):
                run_stream("dve", eng)

            @block.gpsimd
            def _(eng):
                run_stream("pool", eng)

            @block.sync
            def _(eng):
                run_stream("sp", eng)


def weight_tiles():
    tl = []
    tl.append(("w_in", 0, 8, 3072, 16))
    tl.append(("w_in", 0, 8, 0, 512))
    tl.append(("w_in", 0, 8, 1024, 512))
    tl.append(("w_in", 0, 8, 1536, 512))
    tl.append(("w_in", 0, 8, 512, 512))
    tl.append(("w_in", 0, 8, 2048, 512))
    tl.append(("w_in", 0, 8, 2560, 512))
    tl.append(("w_in", 0, 8, 3088, 512))
    tl.append(("w_in", 0, 8, 3600, 512))
    tl.append(("w_in", 0, 8, 4112, 512))
    tl.append(("w_in", 0, 8, 4624, 512))
    tl.append(("w_in", 0, 8, 5136, 512))
    tl.append(("w_in", 0, 8, 5648, 512))
    tl.append(("w_in", 0, 8, 6160, 512))
    tl.append(("w_in", 0, 8, 6672, 512))
    for half in range(2):
        tl.append(("w_bg", 0, 8, half * 512, 512))
        tl.append(("w_bs", 0, 8, half * 512, 512))
    tl.append(("w_out", 0, 8, 0, 512))
    tl.append(("w_out", 0, 8, 512, 512))
    for t in range(6):
        ncw = 512 if t < 5 else 256
        tl.append(("w_fi", 0, 8, t * 512, ncw))
        tl.append(("w_fi", 0, 8, DFF + t * 512, ncw))
    for half in range(2):
        for kg, nk in enumerate((8, 8, 6)):
            tl.append(("w_fo", kg * 8 * 128, nk, half * 512, 512))
    return tl


def build(S_tok, upto=99):
    NT = S_tok // T
    nc = bass.Bass("TRN2", target_bir_lowering=False)

    def din(name, shape, dt=F32):
        return nc.dram_tensor(name, shape, dt, kind="ExternalInput").ap()

    x = din("x", [S_tok, D])
    wsrc = {
        "w_in": din("w_in", [D, DIN]),
        "w_bg": din("w_bg", [D, D]),
        "w_bs": din("w_bs", [D, D]),
        "w_out": din("w_out", [D, D]),
        "w_fi": din("w_fi", [D, 2 * DFF]),
        "w_fo": din("w_fo", [DFF, D]),
    }
    wscr = {k: nc.dram_tensor(k + "_bf", list(v.shape), BF16, kind="Internal").ap() for k, v in wsrc.items()}
    c_gpm = din("g_pm", [D])
    c_gpf = din("g_pf", [D])
    c_gqm = din("g_qm", [D])
    c_gqf = din("g_qf", [D])
    c_bg = din("b_gate", [512])
    c_gn = din("gla_norm", [D])
    c_lng = din("ln_g", [D])
    c_lnb = din("ln_b", [D])
    c_wgu = din("w_gu", [16, 512])
    c_wsp = din("wspT", [128, 4, 128])
    c_bsp = din("b_sp", [4, 128])
    out = nc.dram_tensor("out", [S_tok, D], F32, kind="ExternalOutput").ap()

    S = Sched(nc)
    S.upto = upto
    with contextlib.ExitStack() as st:
        def sb(name, shape, dt):
            return st.enter_context(nc.sbuf_tensor(name, shape, dt))

        xt = sb("xt", [128, 4, D], F32)
        aT = sb("aT", [128, 8, T], BF16)
        xs = sb("xs", [128, D], BF16)
        sr = sb("sr", [128, 4, D], BF16)
        alT = sb("alT", [16, T], F32)
        state = sb("state", [128, 4, 256], F32)
        stbf = sb("stbf", [128, 2, 4, 256], BF16)
        sgg = sb("sgg", [128, 8, T], BF16)
        sgs = sb("sgs", [128, 8, T], BF16)
        ytm = sb("ytm", [128, 2, D], F32)
        ygb = sb("ygb", [128, D], BF16)
        ygb2 = sb("ygb2", [128, D], BF16)
        ygbs = [ygb, ygb2]
        big = sb("big", [128, 24, T], BF16)
        rgate = sb("rgate", [128, 8192], BF16)
        rqk = sb("rqk", [128, 8, T], BF16)
        rtmp = sb("rtmp", [128, 4, T], F32)
        stat = sb("stat", [128, 64], F32)
        dec = sb("dec", [128, 32], F32)
        bnst = sb("bnst", [128, 4, 6], F32)
        bnag = sb("bnag", [128, 4, 2], F32)
        wring = [sb("wr%d" % i, [128, 8, 512], BF16) for i in range(NB)]
        gpm = sb("gpm", [128, D], F32)
        gpf = sb("gpf", [128, D], F32)
        gqm = sb("gqm", [128, D], F32)
        gqf = sb("gqf", [128, D], F32)
        gnb = sb("gnb", [128, D], F32)
        lng = sb("lng", [128, D], F32)
        lnb = sb("lnb", [128, D], F32)
        bgb = sb("bgb", [128, 512], F32)
        wgu = sb("wgu", [16, 512], F32)
        wspf = sb("wspf", [128, 4, 128], F32)
        wsp = sb("wsp", [128, 4, 128], BF16)
        bspb = sb("bspb", [128, 4, 128], F32)
        identf = sb("identf", [128, 128], F32)
        ident = sb("ident", [128, 128], BF16)
        mtri = sb("mtri", [128, 128], F32)
        ind = sb("ind", [128, 2], F32)
        mhalf = sb("mhalf", [128, 1], F32)
        junk = sb("junk", [128, D], BF16)
        fz = sb("fz", [128, 1], F32)
        PP = [st.enter_context(nc.psum_tensor("pp%d" % i, [128, 2, 512], F32)) for i in range(4)]

        def bank(b):
            return PP[b // 2][:, b % 2, :]

        def bres(b):
            return ("ps", b)

        vtm = big[:, 0:8, :].rearrange("p (s a) t -> p s (a t)", s=4)
        vn = big[:, 8:16, :].rearrange("p (s a) t -> p s (a t)", s=4)
        u = big[:, 16:24, :]
        actT = big
        loga = rgate[:, 0:4096].bitcast(F32).rearrange("p (s e) -> p s e", s=4)
        edec = rgate[:, 4096:8192].bitcast(F32).rearrange("p (s e) -> p s e", s=4)
        yglaT = rgate[:, 0:4096].rearrange("p (c t) -> p c t", c=8)
        ysguT = rgate[:, 4096:8192].rearrange("p (c t) -> p c t", c=8)
        qT = rqk[:, 0:4, :]
        kdec = rqk[:, 4:8, :]
        mergedT = rqk
        lg = rtmp[:, 0, :]
        e1 = rtmp[:, 1, :]

        def al(group, side, nsides, write):
            r = [("tok", group, side)]
            w = [("tok", group, j) for j in range(nsides) if j != side] if write else []
            return r, w

        def A(eng, fn, reads=(), writes=(), alias=(), chan=None):
            rr = list(reads)
            ww = list(writes)
            for (g, sd, n, wr) in alias:
                r_, w_ = al(g, sd, n, wr)
                rr += r_
                ww += w_
            return S.op(eng, fn, rr, ww, chan)

        G_BIG, G_GATE, G_QK, G_TMP = "big", "gate", "qk", "tmp"

        tiles = weight_tiles()
        NW = len(tiles)
        cstate = {"n": 0}

        def emit_casts(upto_):
            while cstate["n"] < min(NW, upto_):
                i = cstate["n"]
                wn, r0, nk, c0, ncw = tiles[i]
                S.op("pool", lambda e, wn=wn, r0=r0, nk=nk, c0=c0, ncw=ncw:
                     e.dma_start(out=wscr[wn][r0:r0 + nk * 128, c0:c0 + ncw], in_=wsrc[wn][r0:r0 + nk * 128, c0:c0 + ncw]),
                     writes=[("scr", i)], chan=("cast", i))
                cstate["n"] += 1

        def cload(dst, src, res):
            S.op("sp", lambda e: e.dma_start(out=dst, in_=src), writes=[res], chan="const")

        cload(gpm[:], c_gpm.partition_broadcast(128), "gpm")
        cload(gpf[:], c_gpf.partition_broadcast(128), "gpf")
        cload(gqm[:], c_gqm.partition_broadcast(128), "gqm")
        cload(gqf[:], c_gqf.partition_broadcast(128), "gqf")
        cload(gnb[:], c_gn.partition_broadcast(128), "gnb")
        cload(lng[:], c_lng.partition_broadcast(128), "lng")
        cload(lnb[:], c_lnb.partition_broadcast(128), "lnb")
        cload(bgb[:], c_bg.partition_broadcast(128), "bgb")
        cload(wgu[:], c_wgu, "wgu")
        cload(wspf[:], c_wsp, "wspf")
        cload(bspb[:], c_bsp.partition_broadcast(128), "bspb")

        S.op("pool", lambda e: e.memset(identf[:], 0.0), writes=["identf"])
        S.op("pool", lambda e: e.affine_select(out=identf[:], in_=identf[:], pattern=[[-1, 128]],
                                               compare_op=ALU.not_equal, fill=1.0, base=0, channel_multiplier=1),
             reads=["identf"], writes=["identf"])
        S.op("dve", lambda e: e.tensor_copy(out=ident[:], in_=identf[:]), reads=["identf"], writes=["ident"])
        S.op("pool", lambda e: e.memset(mhalf[:], -0.5), writes=["mhalf"])
        emit_casts(8)
        S.op("pool", lambda e: e.memset(mtri[:], -1.0 / 16), writes=["mtri"])
        S.op("pool", lambda e: e.affine_select(out=mtri[:], in_=mtri[:], pattern=[[-1, 128]],
                                               compare_op=ALU.is_gt, fill=0.0, base=0, channel_multiplier=1),
             reads=["mtri"], writes=["mtri"])
        S.op("pool", lambda e: e.memset(mtri[64:128, 0:64], 0.0), reads=["mtri"], writes=["mtri"])
        S.op("pool", lambda e: e.memset(ind[:], 0.0), writes=["ind"])
        S.op("pool", lambda e: e.memset(ind[0:64, 0:1], -1.0 / 16), reads=["ind"], writes=["ind"])
        S.op("pool", lambda e: e.memset(ind[64:128, 1:2], -1.0 / 16), reads=["ind"], writes=["ind"])
        S.op("pool", lambda e: e.memset(state[:], 0.0), writes=[("state", h) for h in range(4)])
        S.op("pool", lambda e: e.memset(wspf[64:128, :, 0:64], 0.0), reads=["wspf"], writes=["wspf"])
        S.op("dve", lambda e: e.tensor_copy(out=wsp[:], in_=wspf[:]), reads=["wspf"], writes=["wsp"])

        total_w = NT * NW
        wstate = {"loaded": 0}

        def emit_load(j):
            i = j % NW
            wn, r0, nk, c0, ncw = tiles[i]
            slot = j % NB
            S.op("sp", lambda e: e.dma_start(
                out=wring[slot][:, 0:nk, 0:ncw],
                in_=wscr[wn][r0:r0 + nk * 128, c0:c0 + ncw].rearrange("(k p) e -> p k e", p=128)),
                reads=[("scr", i)], writes=[("w", slot)], chan=("w", slot))

        def w_acquire(j):
            if j < NW:
                emit_casts(j + 10)
            while wstate["loaded"] <= j and wstate["loaded"] < total_w:
                assert wstate["loaded"] < j + NB
                emit_load(wstate["loaded"])
                wstate["loaded"] += 1
            return wring[j % NB], ("w", j % NB)

        def w_release(j):
            nxt = j + NB
            if nxt < total_w and wstate["loaded"] == nxt:
                emit_load(nxt)
                wstate["loaded"] += 1


        scnt = [0]

        def scol():
            c = scnt[0] % 64
            scnt[0] += 1
            return c

        def rstd_from(ss_col, n, res_in):
            c1 = scol()
            c2 = scol()
            S.op("dve", lambda e: e.tensor_scalar(out=stat[:, c1:c1 + 1], in0=stat[:, ss_col:ss_col + 1],
                                                  scalar1=1.0 / n, scalar2=EPS, op0=ALU.mult, op1=ALU.add),
                 reads=[res_in], writes=[("stat", c1)])
            S.op("pool", lambda e: e.tensor_tensor(out=stat[:, c2:c2 + 1], in0=stat[:, c1:c1 + 1], in1=mhalf[:], op=ALU.pow),
                 reads=[("stat", c1), "mhalf"], writes=[("stat", c2)])
            return c2, ("stat", c2)

        def prenorm_transposes(s, gtile, gres, tpb):
            c0 = scol()
            S.op("act", lambda e: e.activation(out=junk[:], in_=xt[:, s, :], func=AF.Square, accum_out=stat[:, c0:c0 + 1]),
                 reads=[("xt", s)], writes=[("stat", c0)])
            c2, r2 = rstd_from(c0, D, ("stat", c0))
            S.op("dve", lambda e: e.scalar_tensor_tensor(out=xs[:], in0=xt[:, s, :], scalar=stat[:, c2:c2 + 1], in1=gtile[:],
                                                         op0=ALU.mult, op1=ALU.mult),
                 reads=[("xt", s), r2, gres], writes=["xs"])
            pv = bank(tpb).bitcast(BF16).rearrange("p (k t) -> p k t", k=8)
            for k in range(8):
                S.op("pe", lambda e, k=k: e.transpose(out=pv[:, k, :], in_=xs[:, k * 128:(k + 1) * 128], identity=ident[:]),
                     reads=["xs", "ident"], writes=[bres(tpb)])
            S.op("act", lambda e: e.activation(out=aT[:, :, s * 128:(s + 1) * 128], in_=pv, func=AF.Copy),
                 reads=[bres(tpb)], writes=[("aT", s)])

        def postnorm_residual(s, pp, gtile, gres, final, t):
            src = PP[pp][:].rearrange("p a b -> p (a b)")
            c0 = scol()
            yb = ytm[:, s % 2, :]
            S.op("act", lambda e: e.activation(out=junk[:], in_=src, func=AF.Square, accum_out=stat[:, c0:c0 + 1]),
                 reads=[bres(2 * pp), bres(2 * pp + 1)], writes=[("stat", c0)])
            c2, r2 = rstd_from(c0, D, ("stat", c0))
            S.op("dve", lambda e: e.scalar_tensor_tensor(out=yb, in0=src, scalar=stat[:, c2:c2 + 1], in1=gtile[:],
                                                         op0=ALU.mult, op1=ALU.mult),
                 reads=[bres(2 * pp), bres(2 * pp + 1), r2, gres], writes=[("ytm", s % 2)])
            S.op("pool", lambda e: e.tensor_tensor(out=xt[:, s, :], in0=xt[:, s, :], in1=yb, op=ALU.add),
                 reads=[("xt", s), ("ytm", s % 2)], writes=[("xt", s)])
            if final:
                S.op("sp", lambda e: e.dma_start(out=out[t * T + s * 128:t * T + (s + 1) * 128, :], in_=xt[:, s, :]),
                     reads=[("xt", s)], chan=("st", s), force=True)

        def fence(res):
            S.op("dve", lambda e: e.memset(fz[:], 0.0), writes=list(res) + ["fz"])

        XT_ALL = [("xt", s) for s in range(4)]
        AT_ALL = [("aT", s) for s in range(4)]

        for t in range(NT):
            wj = t * NW
            S.phase(0)
            S.op("sp", lambda e, t=t: e.dma_start(out=xt[:], in_=x[t * T:(t + 1) * T, :].rearrange("(s p) d -> p s d", p=128)),
                 writes=XT_ALL, chan="xld")
            if t == 0:
                for j in range(min(NB, total_w)):
                    emit_load(j)
                    wstate["loaded"] += 1
            S.phase(1)
            for s in range(4):
                prenorm_transposes(s, gpm, "gpm", s % 2)

            S.phase(2)
            fb = [0]

            def fbank():
                b = 4 + (fb[0] % 4)
                fb[0] += 1
                return b

            def fm_items(evac):
                nonlocal wj
                j = wj
                wj += 1
                items = []
                for c4 in range(4):
                    def item(c4=c4, j=j):
                        wt, wr = w_acquire(j)
                        pb = fbank()
                        for kc in range(8):
                            S.op("pe", lambda e, kc=kc, c4=c4, pb=pb, wt=wt: e.matmul(
                                bank(pb), lhsT=wt[:, kc, c4 * 128:(c4 + 1) * 128], rhs=aT[:, kc, :],
                                start=(kc == 0), stop=(kc == 7)),
                                reads=AT_ALL + [wr], writes=[bres(pb)])
                        evac(c4, pb)
                        if c4 == 3:
                            w_release(j)
                    items.append(item)
                return items

            def tm_items(evac):
                nonlocal wj
                j = wj
                wj += 1
                items = []
                for s in range(4):
                    def item(s=s, j=j):
                        wt, wr = w_acquire(j)
                        pb = fbank()
                        for kc in range(8):
                            S.op("pe", lambda e, kc=kc, s=s, pb=pb, wt=wt: e.matmul(
                                bank(pb), lhsT=aT[:, kc, s * 128:(s + 1) * 128], rhs=wt[:, kc, :],
                                start=(kc == 0), stop=(kc == 7)),
                                reads=[("aT", s), wr], writes=[bres(pb)])
                        evac(s, pb)
                        if s == 3:
                            w_release(j)
                    items.append(item)
                return items

            wt, wr = w_acquire(wj)
            for kc in range(8):
                S.op("pe", lambda e, kc=kc, wt=wt: e.matmul(bank(2)[0:16, :], lhsT=wt[:, kc, 0:16], rhs=aT[:, kc, :],
                                                             start=(kc == 0), stop=(kc == 7)),
                     reads=AT_ALL + [wr], writes=[bres(2)])
            w_release(wj)
            wj += 1
            S.op("dve", lambda e: e.tensor_copy(out=alT[:], in_=bank(2)[0:16, :]), reads=[bres(2)], writes=["alT"])

            def gate_logits(s):
                pb = 2 + (s % 2)
                S.op("pe", lambda e, s=s, pb=pb: e.matmul(bank(pb), lhsT=alT[:, s * 128:(s + 1) * 128], rhs=wgu[:], start=True, stop=True),
                     reads=["alT", "wgu"], writes=[bres(pb)])
                lgs = rtmp[:, s % 2, :]
                e1s = rtmp[:, 2 + (s % 2), :]
                A("dve", lambda e, pb=pb, lgs=lgs: e.tensor_tensor(out=lgs, in0=bank(pb), in1=bgb[:], op=ALU.add),
                  reads=[bres(pb), "bgb"], writes=[("lg", s % 2)], alias=[(G_TMP, 0, 4, True)])
                A("act", lambda e, lgs=lgs, e1s=e1s: e.activation(out=e1s, in_=lgs, func=AF.Exp, scale=-1.0),
                  reads=[("lg", s % 2)], writes=[("e1", s % 2)], alias=[(G_TMP, 0, 4, True)])
                A("act", lambda e, s=s, e1s=e1s: e.activation(out=loga[:, s, :], in_=e1s, func=AF.Ln, bias=1.0),
                  reads=[("e1", s % 2)], writes=[("loga", s)], alias=[(G_TMP, 0, 4, False), (G_GATE, 0, 2, True)])

            def gate_cumsum(s):
                pb = 2 + (s % 2)
                S.op("pe", lambda e, s=s, pb=pb: e.matmul(bank(pb), lhsT=mtri[:], rhs=loga[:, s, :], start=True, stop=True),
                     reads=["mtri", ("loga", s), ("tok", G_GATE, 0)], writes=[bres(pb)])
                A("act", lambda e, s=s, pb=pb: e.activation(out=edec[:, s, :], in_=bank(pb), func=AF.Exp),
                  reads=[bres(pb)], writes=[("edec", s)], alias=[(G_GATE, 0, 2, True)])

            def ev_q(h, pb):
                A("act", lambda e: e.activation(out=qT[:, h, :], in_=bank(pb), func=AF.Copy, scale=float(128 ** -0.5)),
                  reads=[bres(pb)], writes=[("qT", h)], alias=[(G_QK, 0, 2, True)])

            def mk_ev_v(half):
                def ev(s, pb):
                    A("dve", lambda e: e.tensor_copy(out=vtm[:, s, half * 512:(half + 1) * 512], in_=bank(pb)),
                      reads=[bres(pb)], writes=[("vtm", s, half)], alias=[(G_BIG, 0, 2, True)])
                return ev

            def ev_k(s, pb):
                A("dve", lambda e: e.tensor_tensor(out=kdec[:, s, :], in0=bank(pb), in1=edec[:, s, :], op=ALU.mult),
                  reads=[bres(pb), ("edec", s)], writes=[("kdec", s)], alias=[(G_QK, 0, 2, True), (G_GATE, 0, 2, False)])

            def mk_ev_r(half):
                def ev(s, pb):
                    S.op("act", lambda e: e.activation(out=sr[:, s, half * 512:(half + 1) * 512], in_=bank(pb), func=AF.Silu),
                         reads=[bres(pb)], writes=[("sr", s, half)])
                return ev

            def mk_ev_su(half):
                def ev(c4, pb):
                    ch = half * 4 + c4
                    A("act", lambda e: e.activation(out=u[:, ch, :], in_=bank(pb), func=AF.Gelu_apprx_tanh),
                      reads=[bres(pb)], writes=[("u", ch)], alias=[(G_BIG, 0, 2, True)])
                return ev

            gvc = [0]

            def mk_ev_sv(half):
                def ev(s, pb):
                    gi = gvc[0] % 4
                    gvc[0] += 1
                    gv = rtmp[:, gi, :]
                    gres = ("gv", gi)
                    A("act", lambda e: e.activation(out=gv, in_=bank(pb), func=AF.Gelu_apprx_tanh),
                      reads=[bres(pb)], writes=[gres], alias=[(G_TMP, 3, 4, True)])
                    cs = []
                    for g2 in range(2):
                        S.op("dve", lambda e, g2=g2: e.bn_stats(out=bnst[:, g2, :], in_=gv[:, g2 * 256:(g2 + 1) * 256]),
                             reads=[gres, ("tok", G_TMP, 3)], writes=[("bnst", g2)])
                        S.op("dve", lambda e, g2=g2: e.bn_aggr(out=bnag[:, g2, :], in_=bnst[:, g2, :]),
                             reads=[("bnst", g2)], writes=[("bnag", g2)])
                        c1 = scol()
                        c2 = scol()
                        c3 = scol()
                        S.op("dve", lambda e, g2=g2, c1=c1, c3=c3: e.tensor_scalar(out=stat[:, c1:c1 + 1], in0=bnag[:, g2, 1:2], scalar1=EPS, scalar2=None, op0=ALU.add),
                             reads=[("bnag", g2)], writes=[("stat", c1)])
                        S.op("dve", lambda e, g2=g2, c3=c3: e.tensor_copy(out=stat[:, c3:c3 + 1], in_=bnag[:, g2, 0:1]),
                             reads=[("bnag", g2)], writes=[("stat", c3)])
                        S.op("pool", lambda e, c1=c1, c2=c2: e.tensor_tensor(out=stat[:, c2:c2 + 1], in0=stat[:, c1:c1 + 1], in1=mhalf[:], op=ALU.pow),
                             reads=[("stat", c1), "mhalf"], writes=[("stat", c2)])
                        cs.append((c2, c3))
                    for g2 in range(2):
                        c2, c3 = cs[g2]
                        A("dve", lambda e, g2=g2, c2=c2, c3=c3: e.tensor_scalar(out=gv[:, g2 * 256:(g2 + 1) * 256], in0=gv[:, g2 * 256:(g2 + 1) * 256],
                                                                             scalar1=stat[:, c3:c3 + 1], scalar2=stat[:, c2:c2 + 1],
                                                                             op0=ALU.subtract, op1=ALU.mult),
                          reads=[gres, ("stat", c3), ("stat", c2)], writes=[gres], alias=[(G_TMP, 3, 4, True)])
                    A("pool", lambda e: e.tensor_tensor(out=gv, in0=gv, in1=lng[:, half * 512:(half + 1) * 512], op=ALU.mult),
                      reads=[gres, "lng"], writes=[gres], alias=[(G_TMP, 3, 4, True)])
                    A("pool", lambda e: e.tensor_tensor(out=vn[:, s, half * 512:(half + 1) * 512], in0=gv, in1=lnb[:, half * 512:(half + 1) * 512], op=ALU.add),
                      reads=[gres, "lnb", ("tok", G_TMP, 3)], writes=[("vn", s, half)], alias=[(G_BIG, 0, 2, True)])
                return ev

            def mk_ev_sig(dst, name, half):
                def ev(c4, pb):
                    ch = half * 4 + c4
                    S.op("act", lambda e: e.activation(out=dst[:, ch, :], in_=bank(pb), func=AF.Sigmoid),
                         reads=[bres(pb)], writes=[(name, ch)])
                return ev

            gate_logits(0)
            gate_logits(1)
            S.phase(3)
            for it in fm_items(ev_q):
                it()
            S.phase(2)
            gate_cumsum(0)
            gate_cumsum(1)
            gate_logits(2)
            gate_logits(3)
            S.phase(3)
            for it in tm_items(mk_ev_v(0)):
                it()
            S.phase(2)
            gate_cumsum(2)
            gate_cumsum(3)
            for s in range(4):
                for h in range(4):
                    cc = (s * 4 + h) * 2
                    S.op("pe", lambda e, s=s, h=h, cc=cc: e.matmul(bank(2)[:, cc:cc + 2], lhsT=loga[:, s, h * 128:(h + 1) * 128], rhs=ind[:],
                                                                    start=True, stop=True),
                         reads=[("loga", s), "ind", ("tok", G_GATE, 0)], writes=[bres(2)])
            S.op("act", lambda e: e.activation(out=dec[:], in_=bank(2)[:, 0:32], func=AF.Exp), reads=[bres(2)], writes=["dec"])
            S.phase(3)
            for it in tm_items(mk_ev_v(1)):
                it()
            for it in tm_items(ev_k):
                it()

            S.phase(4)
            from collections import deque
            filler = deque()
            filler.extend(tm_items(mk_ev_r(0)))
            filler.extend(tm_items(mk_ev_r(1)))
            filler.extend(fm_items(mk_ev_su(0)))
            filler.extend(fm_items(mk_ev_su(1)))
            filler.extend(tm_items(mk_ev_sv(0)))
            filler.extend(tm_items(mk_ev_sv(1)))
            filler.extend(fm_items(mk_ev_sig(sgg, "sgg", 0)))
            filler.extend(fm_items(mk_ev_sig(sgg, "sgg", 1)))
            filler.extend(fm_items(mk_ev_sig(sgs, "sgs", 0)))
            filler.extend(fm_items(mk_ev_sig(sgs, "sgs", 1)))

            def pull(n):
                for _ in range(n):
                    if filler:
                        filler.popleft()()

            gla_on = S.upto >= 5
            if KMODE == 1:
                pull(len(filler))
            UQ = []
            OPS = []
            if gla_on:
                fence(UQ + [bres(0), bres(1)])
                fence(OPS + [bres(2), bres(3)])
            pending = deque()
            for s in range(4):
                opv = PP[1][:].rearrange("p a b -> p (a b)")
                for j in range(2):
                    par = j
                    rows = slice(64 * j, 64 * (j + 1))
                    for h in range(4):
                        if not gla_on:
                            break
                        uoff = (h // 2) * 256
                        upd = bank(h % 2)[:, uoff:uoff + 256]
                        ures = bres(h % 2)
                        A("pe", lambda e, s=s, h=h, rows=rows, upd=upd: e.matmul(
                            upd, lhsT=kdec[rows, s, h * 128:(h + 1) * 128], rhs=vtm[rows, s, h * 256:(h + 1) * 256], start=True, stop=True),
                          reads=[("kdec", s), ("vtm", s, h // 2)], writes=[ures], alias=[(G_QK, 0, 2, False), (G_BIG, 0, 2, False)])
                        dcol = (s * 4 + h) * 2 + j
                        S.op("dve", lambda e, h=h, upd=upd, dcol=dcol: e.scalar_tensor_tensor(
                            out=state[:, h, :], in0=state[:, h, :], scalar=dec[:, dcol:dcol + 1], in1=upd, op0=ALU.mult, op1=ALU.add),
                            reads=[("state", h), "dec", ures], writes=[("state", h)])
                        S.op("pool", lambda e, h=h, par=par: e.tensor_copy(out=stbf[:, par, h, :], in_=state[:, h, :]),
                             reads=[("state", h)], writes=[("stbf", par, h)])
                        if h == 1 and KMODE == 0:
                            pull(1)
                    if KMODE == 0:
                        pull(3)
                    if j == 0 and pending:
                        pending.popleft()()
                    for h in range(4):
                        if not gla_on:
                            break
                        A("pe", lambda e, s=s, h=h, j=j, par=par, rows=rows: e.matmul(
                            PP[1][rows, :, :].rearrange("p a b -> p (a b)")[:, h * 256:(h + 1) * 256],
                            lhsT=qT[:, h, s * 128 + 64 * j:s * 128 + 64 * (j + 1)], rhs=stbf[:, par, h, :], start=True, stop=True),
                          reads=[("qT", h), ("stbf", par, h)], writes=[bres(2 + h // 2)], alias=[(G_QK, 0, 2, False)])
                    if KMODE == 2:
                        pull(4)
                if not gla_on:
                    continue
                yb = ytm[:, s % 2, :]
                c0s = []
                for h in range(4):
                    c0 = scol()
                    c0s.append(c0)
                    S.op("act", lambda e, h=h, c0=c0, opv=opv: e.activation(out=junk[:, h * 256:(h + 1) * 256], in_=opv[:, h * 256:(h + 1) * 256],
                                                               func=AF.Square, accum_out=stat[:, c0:c0 + 1]),
                         reads=[bres(2 + h // 2)], writes=[("stat", c0)])
                for h in range(4):
                    c2, r2 = rstd_from(c0s[h], 256, ("stat", c0s[h]))
                    S.op("dve", lambda e, h=h, c2=c2, yb=yb, opv=opv: e.scalar_tensor_tensor(
                        out=yb[:, h * 256:(h + 1) * 256], in0=opv[:, h * 256:(h + 1) * 256], scalar=stat[:, c2:c2 + 1],
                        in1=gnb[:, h * 256:(h + 1) * 256], op0=ALU.mult, op1=ALU.mult),
                        reads=[bres(2 + h // 2), r2, "gnb"], writes=[("ytm", s % 2)])
                S.op("pool", lambda e, s=s, yb=yb: e.tensor_tensor(out=ygbs[s % 2][:], in0=yb, in1=sr[:, s, :], op=ALU.mult),
                     reads=[("ytm", s % 2), ("sr", s, 0), ("sr", s, 1)], writes=[("ygb", s % 2)])
                def ytrans(s=s):
                    tb = fbank()
                    pv = bank(tb).bitcast(BF16).rearrange("p (k t) -> p k t", k=8)
                    for k in range(8):
                        S.op("pe", lambda e, k=k, pv=pv: e.transpose(out=pv[:, k, :], in_=ygbs[s % 2][:, k * 128:(k + 1) * 128], identity=ident[:]),
                             reads=[("ygb", s % 2), "ident"], writes=[bres(tb)])
                    A("act", lambda e, s=s, pv=pv: e.activation(out=yglaT[:, :, s * 128:(s + 1) * 128], in_=pv, func=AF.Copy),
                      reads=[bres(tb)], writes=[("ygla", s)], alias=[(G_GATE, 1, 2, True)])
                pending.append(ytrans)
            pull(4)
            while pending:
                pending.popleft()()
            pull(len(filler))

            S.phase(6)
            for s in range(4):
                pp = 1 + (s % 2)
                mv = PP[pp][:].rearrange("p a (c i) -> p (a c) i", c=4)
                for g in range(4):
                    for cc in range(2):
                        ch = g * 2 + cc
                        A("pe", lambda e, s=s, g=g, cc=cc, ch=ch, mv=mv: e.matmul(
                            mv[:, ch, :], lhsT=vn[:, s, g * 256 + cc * 128:g * 256 + (cc + 1) * 128], rhs=wsp[:, g, :], start=True, stop=True),
                          reads=[("vn", s, g // 2), "wsp"], writes=[bres(2 * pp + ch // 4)], alias=[(G_BIG, 0, 2, False)])
                yb4 = ytm[:, s % 2, :].rearrange("p (g c i) -> p g c i", g=4, c=2)
                S.op("dve", lambda e, mv=mv, yb4=yb4: e.tensor_tensor(
                    out=yb4, in0=mv.rearrange("p (g c) i -> p g c i", g=4),
                    in1=bspb[:].unsqueeze(2).broadcast_to([128, 4, 2, 128]), op=ALU.add),
                    reads=[bres(2 * pp), bres(2 * pp + 1), "bspb"], writes=[("ytm", s % 2)])
                A("dve", lambda e, s=s: e.tensor_tensor(out=ysguT[:, :, s * 128:(s + 1) * 128],
                                                        in0=ytm[:, s % 2, :].rearrange("p (c i) -> p c i", c=8),
                                                        in1=u[:, :, s * 128:(s + 1) * 128], op=ALU.mult),
                  reads=[("ytm", s % 2)] + [("u", ch) for ch in range(8)], writes=[("ysgu", s)],
                  alias=[(G_GATE, 1, 2, True), (G_BIG, 0, 2, False)])

            S.phase(7)
            YG = [("ygla", s) for s in range(4)]
            YS = [("ysgu", s) for s in range(4)]
            for half in range(2):
                wtg, wrg = w_acquire(wj)
                wts, wrs = w_acquire(wj + 1)
                for c4 in range(4):
                    ch = half * 4 + c4
                    pp = 1 + (ch % 3)
                    for kc in range(8):
                        A("pe", lambda e, kc=kc, c4=c4, pp=pp, wtg=wtg: e.matmul(
                            PP[pp][:, 0, :], lhsT=wtg[:, kc, c4 * 128:(c4 + 1) * 128], rhs=yglaT[:, kc, :], start=(kc == 0), stop=(kc == 7)),
                          reads=YG + [wrg], writes=[bres(2 * pp)], alias=[(G_GATE, 1, 2, False)])
                    for kc in range(8):
                        A("pe", lambda e, kc=kc, c4=c4, pp=pp, wts=wts: e.matmul(
                            PP[pp][:, 1, :], lhsT=wts[:, kc, c4 * 128:(c4 + 1) * 128], rhs=ysguT[:, kc, :], start=(kc == 0), stop=(kc == 7)),
                          reads=YS + [wrs], writes=[bres(2 * pp + 1)], alias=[(G_GATE, 1, 2, False)])
                    t1 = rtmp[:, (ch % 2) * 2, :]
                    t2 = rtmp[:, (ch % 2) * 2 + 1, :]
                    A("dve", lambda e, ch=ch, pp=pp, t1=t1: e.tensor_tensor(out=t1, in0=PP[pp][:, 0, :], in1=sgg[:, ch, :], op=ALU.mult),
                      reads=[bres(2 * pp), ("sgg", ch)], writes=[("t1", ch % 2)], alias=[(G_TMP, 1, 4, True)])
                    A("dve", lambda e, ch=ch, pp=pp, t2=t2: e.tensor_tensor(out=t2, in0=PP[pp][:, 1, :], in1=sgs[:, ch, :], op=ALU.mult),
                      reads=[bres(2 * pp + 1), ("sgs", ch)], writes=[("t2", ch % 2)], alias=[(G_TMP, 1, 4, True)])
                    A("pool", lambda e, ch=ch, t1=t1, t2=t2: e.tensor_tensor(out=mergedT[:, ch, :], in0=t1, in1=t2, op=ALU.add),
                      reads=[("t1", ch % 2), ("t2", ch % 2)], writes=[("merged", ch)], alias=[(G_TMP, 1, 4, False), (G_QK, 1, 2, True)])
                w_release(wj)
                w_release(wj + 1)
                wj += 2

            S.phase(8)
            MG = [("merged", ch) for ch in range(8)]
            wt0, wr0 = w_acquire(wj)
            wt1, wr1 = w_acquire(wj + 1)
            for s in range(4):
                pp = 1 + (s % 3)
                for half, (wt, wr) in enumerate(((wt0, wr0), (wt1, wr1))):
                    for kc in range(8):
                        A("pe", lambda e, kc=kc, s=s, pp=pp, half=half, wt=wt: e.matmul(
                            PP[pp][:, half, :], lhsT=mergedT[:, kc, s * 128:(s + 1) * 128], rhs=wt[:, kc, :], start=(kc == 0), stop=(kc == 7)),
                          reads=MG + [wr], writes=[bres(2 * pp + half)], alias=[(G_QK, 1, 2, False)])
                postnorm_residual(s, pp, gqm, "gqm", False, t)
            w_release(wj)
            w_release(wj + 1)
            wj += 2

            S.phase(9)
            for s in range(4):
                prenorm_transposes(s, gpf, "gpf", s % 2)

            S.phase(10)
            fc = 0
            for tt in range(6):
                wtg, wrg = w_acquire(wj)
                wtu, wru = w_acquire(wj + 1)
                for c4 in range(4 if tt < 5 else 2):
                    pp = 1 + (fc % 3)
                    for kc in range(8):
                        S.op("pe", lambda e, kc=kc, c4=c4, pp=pp, wtg=wtg: e.matmul(
                            PP[pp][:, 0, :], lhsT=wtg[:, kc, c4 * 128:(c4 + 1) * 128], rhs=aT[:, kc, :], start=(kc == 0), stop=(kc == 7)),
                            reads=AT_ALL + [wrg], writes=[bres(2 * pp)])
                    for kc in range(8):
                        S.op("pe", lambda e, kc=kc, c4=c4, pp=pp, wtu=wtu: e.matmul(
                            PP[pp][:, 1, :], lhsT=wtu[:, kc, c4 * 128:(c4 + 1) * 128], rhs=aT[:, kc, :], start=(kc == 0), stop=(kc == 7)),
                            reads=AT_ALL + [wru], writes=[bres(2 * pp + 1)])
                    sgt = rtmp[:, fc % 2, :]
                    A("act", lambda e, pp=pp, sgt=sgt: e.activation(out=sgt, in_=PP[pp][:, 0, :], func=AF.Silu),
                      reads=[bres(2 * pp)], writes=[("sgt", fc % 2)], alias=[(G_TMP, 2, 4, True)])
                    A("dve", lambda e, pp=pp, sgt=sgt, fc=fc: e.tensor_tensor(out=actT[:, fc, :], in0=PP[pp][:, 1, :], in1=sgt, op=ALU.mult),
                      reads=[bres(2 * pp + 1), ("sgt", fc % 2)], writes=[("act", fc)], alias=[(G_TMP, 2, 4, False), (G_BIG, 1, 2, True)])
                    fc += 1
                w_release(wj)
                w_release(wj + 1)
                wj += 2

            S.phase(11)
            for half in range(2):
                for kg, nk in enumerate((8, 8, 6)):
                    wt, wr = w_acquire(wj)
                    for s in range(4):
                        for k in range(nk):
                            fcc = kg * 8 + k
                            A("pe", lambda e, s=s, k=k, fcc=fcc, half=half, wt=wt: e.matmul(
                                PP[s][:, half, :], lhsT=actT[:, fcc, s * 128:(s + 1) * 128], rhs=wt[:, k, :],
                                start=(fcc == 0), stop=(fcc == 21)),
                              reads=[("act", fcc), wr], writes=[bres(2 * s + half)], alias=[(G_BIG, 1, 2, False)])
                    w_release(wj)
                    wj += 1
            for s in range(4):
                postnorm_residual(s, s, gqf, "gqf", True, t)

        S.emit()
    return nc, S


_CACHE = {}


def _prep_consts(inp):
    f = lambda a: np.ascontiguousarray(np.asarray(a, dtype=np.float32))
    return {
        "w_in": f(inp["w_in"][0]),
        "w_bg": f(inp["w_branch_gla"][0]),
        "w_bs": f(inp["w_branch_sgu"][0]),
        "w_out": f(inp["w_out"][0]),
        "w_fi": f(inp["w_ffn_in"][0]),
        "w_fo": f(inp["w_ffn_out"][0]),
        "g_pm": f(inp["norm_pre_mix"][0]),
        "g_pf": f(inp["norm_pre_ffn"][0]),
        "g_qm": f(inp["norm_post_mix"][0]),
        "g_qf": f(inp["norm_post_ffn"][0]),
        "b_gate": f(inp["b_gate"][0]),
        "gla_norm": f(np.asarray(inp["gla_norm"][0]).reshape(-1)),
        "ln_g": f(np.asarray(inp["sgu_ln_g"][0]).reshape(-1)),
        "ln_b": f(np.asarray(inp["sgu_ln_b"][0]).reshape(-1)),
        "w_gu": f(inp["w_gate_up"][0]),
        "wspT": f(np.transpose(np.asarray(inp["w_spatial"][0]), (2, 0, 1))),
        "b_sp": f(np.asarray(inp["b_spatial"][0])),
    }


def kernel(**inputs):
    x = np.asarray(inputs["x"], dtype=np.float32)
    B, S_tok, _ = x.shape
    key = S_tok
    if key not in _CACHE:
        _CACHE[key] = build(S_tok)[0]
    nc = _CACHE[key]
    consts = _prep_consts(inputs)
    in_maps = []
    for b in range(B):
        m = dict(consts)
        m["x"] = np.ascontiguousarray(x[b])
        in_maps.append(m)
    res = run_bass_kernel_spmd(nc, in_maps, core_ids=list(range(B)))
    return np.stack([np.asarray(r["out"]) for r in res.results], axis=0).astype(np.float32)


def _simulate(S):
    ops = S.ops
    engs = ("pe", "act", "dve", "pool", "sp")
    streams = {e: [i for i, o in enumerate(ops) if o.eng == e] for e in engs}
    pos = {e: 0 for e in engs}
    done = set()
    progress = True
    while progress:
        progress = False
        for e in engs:
            while pos[e] < len(streams[e]):
                i = streams[e][pos[e]]
                if all(d in done for (_k, d) in ops[i].waits):
                    done.add(i)
                    pos[e] += 1
                    progress = True
                else:
                    break
    stuck = {e: (pos[e], len(streams[e])) for e in engs if pos[e] < len(streams[e])}
    return stuck
```

```python
import contextlib
import numpy as np
import concourse.bass as bass
import concourse.mybir as mybir
from concourse.bass_utils import run_bass_kernel_spmd

F32 = mybir.dt.float32
BF16 = mybir.dt.bfloat16
AF = mybir.ActivationFunctionType
ALU = mybir.AluOpType

D = 1024
DIN = 7184
DFF = 2816
T = 512
NB = 4
import os
KMODE = int(os.environ.get("KMODE", "0"))
STRICT = int(os.environ.get("STRICT", "0"))
EPS = 1e-6


class _Op:
    __slots__ = ("eng", "fn", "deps", "chan", "seq", "waits")

    def __init__(self, eng, fn, chan):
        self.eng = eng
        self.fn = fn
        self.chan = chan
        self.deps = {}
        self.seq = 0
        self.waits = []


class Sched:
    COMPUTE = ("pe", "act", "dve", "pool")

    def __init__(self, nc):
        self.nc = nc
        self.ops = []
        self.last_w = {}
        self.readers = {}
        self.upto = 99
        self.stopped = False

    def phase(self, k):
        self.stopped = k > self.upto

    def op(self, eng, fn, reads=(), writes=(), chan=None, force=False):
        if self.stopped and not force:
            return -1
        idx = len(self.ops)
        o = _Op(eng, fn, chan)
        for r in reads:
            w = self.last_w.get(r)
            if w is not None:
                o.deps[w] = True
        for r in writes:
            w = self.last_w.get(r)
            if w is not None and w not in o.deps:
                o.deps[w] = False
            for rd in self.readers.get(r, ()):
                if rd not in o.deps:
                    o.deps[rd] = False
        for r in reads:
            self.readers.setdefault(r, []).append(idx)
        for r in writes:
            self.last_w[r] = idx
            self.readers[r] = []
        self.ops.append(o)
        return idx

    def emit(self):
        nc = self.nc
        ops = self.ops
        engs = ("pe", "act", "dve", "pool", "sp")
        streams = {e: [] for e in engs}
        for i, o in enumerate(ops):
            streams[o.eng].append(i)
            o.seq = len(streams[o.eng])
        chan_cnt = {}
        dma_val = {}
        for i, o in enumerate(ops):
            if o.chan is not None:
                chan_cnt[o.chan] = chan_cnt.get(o.chan, 0) + 16
                dma_val[i] = chan_cnt[o.chan]
        need_sig = set()
        for e in engs:
            waited = {}
            for i in streams[e]:
                o = ops[i]
                for d, is_raw in sorted(o.deps.items()):
                    p = ops[d]
                    if p.chan is not None:
                        key = ("dma", p.chan)
                        val = dma_val[d]
                        if p.chan == "const":
                            val = chan_cnt[p.chan]
                            dma_val[d] = val
                    else:
                        if p.eng == e and o.chan is None:
                            if e == "pe" or (not is_raw and not STRICT):
                                continue
                        key = ("eng", p.eng)
                        val = p.seq
                    if waited.get(key, 0) >= val:
                        continue
                    waited[key] = val
                    o.waits.append((key, d))
                    if p.chan is None:
                        need_sig.add(d)
        rank = {}
        for e in self.COMPUTE:
            n = 0
            for i in streams[e]:
                if i in need_sig:
                    n += 1
                    rank[i] = n
        self.stats = {e: len(streams[e]) for e in engs}
        with contextlib.ExitStack() as st:
            esem = {e: st.enter_context(nc.semaphore("s_" + e)) for e in self.COMPUTE}
            csem = {}
            for n_, c in enumerate(chan_cnt):
                csem[c] = st.enter_context(nc.semaphore("c%d" % n_))
            block = st.enter_context(nc.Block())

            def run_stream(e, engobj):
                for i in streams[e]:
                    o = ops[i]
                    for key, d in o.waits:
                        if key[0] == "dma":
                            engobj.wait_ge(csem[key[1]], dma_val[d])
                        else:
                            engobj.wait_ge(esem[key[1]], rank[d])
                    ins = o.fn(engobj)
                    if o.chan is not None:
                        ins.then_inc(csem[o.chan], 16)
                    elif i in need_sig:
                        ins.then_inc(esem[e], 1)
                if e == "sp":
                    for c, v in chan_cnt.items():
                        engobj.wait_ge(csem[c], v)

            @block.tensor
            def _(eng):
                run_stream("pe", eng)

            @block.scalar
            def _(eng):
                run_stream("act", eng)

            @block.vector
            def _(eng):
                run_stream("dve", eng)

            @block.gpsimd
            def _(eng):
                run_stream("pool", eng)

            @block.sync
            def _(eng):
                run_stream("sp", eng)


def weight_tiles():
    tl = []
    tl.append(("w_in", 0, 8, 3072, 16))
    tl.append(("w_in", 0, 8, 0, 512))
    tl.append(("w_in", 0, 8, 1024, 512))
    tl.append(("w_in", 0, 8, 1536, 512))
    tl.append(("w_in", 0, 8, 512, 512))
    tl.append(("w_in", 0, 8, 2048, 512))
    tl.append(("w_in", 0, 8, 2560, 512))
    tl.append(("w_in", 0, 8, 3088, 512))
    tl.append(("w_in", 0, 8, 3600, 512))
    tl.append(("w_in", 0, 8, 4112, 512))
    tl.append(("w_in", 0, 8, 4624, 512))
    tl.append(("w_in", 0, 8, 5136, 512))
    tl.append(("w_in", 0, 8, 5648, 512))
    tl.append(("w_in", 0, 8, 6160, 512))
    tl.append(("w_in", 0, 8, 6672, 512))
    for half in range(2):
        tl.append(("w_bg", 0, 8, half * 512, 512))
        tl.append(("w_bs", 0, 8, half * 512, 512))
    tl.append(("w_out", 0, 8, 0, 512))
    tl.append(("w_out", 0, 8, 512, 512))
    for t in range(6):
        ncw = 512 if t < 5 else 256
        tl.append(("w_fi", 0, 8, t * 512, ncw))
        tl.append(("w_fi", 0, 8, DFF + t * 512, ncw))
    for half in range(2):
        for kg, nk in enumerate((8, 8, 6)):
            tl.append(("w_fo", kg * 8 * 128, nk, half * 512, 512))
    return tl


def build(S_tok, upto=99):
    NT = S_tok // T
    nc = bass.Bass("TRN2", target_bir_lowering=False)

    def din(name, shape, dt=F32):
        return nc.dram_tensor(name, shape, dt, kind="ExternalInput").ap()

    x = din("x", [S_tok, D])
    wsrc = {
        "w_in": din("w_in", [D, DIN]),
        "w_bg": din("w_bg", [D, D]),
        "w_bs": din("w_bs", [D, D]),
        "w_out": din("w_out", [D, D]),
        "w_fi": din("w_fi", [D, 2 * DFF]),
        "w_fo": din("w_fo", [DFF, D]),
    }
    wscr = {k: nc.dram_tensor(k + "_bf", list(v.shape), BF16, kind="Internal").ap() for k, v in wsrc.items()}
    c_gpm = din("g_pm", [D])
    c_gpf = din("g_pf", [D])
    c_gqm = din("g_qm", [D])
    c_gqf = din("g_qf", [D])
    c_bg = din("b_gate", [512])
    c_gn = din("gla_norm", [D])
    c_lng = din("ln_g", [D])
    c_lnb = din("ln_b", [D])
    c_wgu = din("w_gu", [16, 512])
    c_wsp = din("wspT", [128, 4, 128])
    c_bsp = din("b_sp", [4, 128])
    out = nc.dram_tensor("out", [S_tok, D], F32, kind="ExternalOutput").ap()

    S = Sched(nc)
    S.upto = upto
    with contextlib.ExitStack() as st:
        def sb(name, shape, dt):
            return st.enter_context(nc.sbuf_tensor(name, shape, dt))

        xt = sb("xt", [128, 4, D], F32)
        aT = sb("aT", [128, 8, T], BF16)
        xs = sb("xs", [128, D], BF16)
        xsB = sb("xsB", [128, D], BF16)
        xs2 = [xs, xsB]
        sr = sb("sr", [128, 4, D], BF16)
        alT = sb("alT", [16, T], F32)
        state = sb("state", [128, 4, 256], F32)
        stbf = sb("stbf", [128, 2, 4, 256], BF16)
        sgg = sb("sgg", [128, 8, T], BF16)
        sgs = sb("sgs", [128, 8, T], BF16)
        ytm = sb("ytm", [128, 2, D], F32)
        ygb = sb("ygb", [128, D], BF16)
        ygb2 = sb("ygb2", [128, D], BF16)
        ygbs = [ygb, ygb2]
        big = sb("big", [128, 24, T], BF16)
        rgate = sb("rgate", [128, 8192], BF16)
        rqk = sb("rqk", [128, 8, T], BF16)
        rtmp = sb("rtmp", [128, 4, T], F32)
        stat = sb("stat", [128, 64], F32)
        dec = sb("dec", [128, 32], F32)
        bnst = sb("bnst", [128, 4, 6], F32)
        bnag = sb("bnag", [128, 4, 2], F32)
        wring = [sb("wr%d" % i, [128, 8, 512], BF16) for i in range(NB)]
        gpm = sb("gpm", [128, D], F32)
        gpf = sb("gpf", [128, D], F32)
        gqm = sb("gqm", [128, D], F32)
        gqf = sb("gqf", [128, D], F32)
        gnb = sb("gnb", [128, D], F32)
        lng = sb("lng", [128, D], F32)
        lnb = sb("lnb", [128, D], F32)
        bgb = sb("bgb", [128, 512], F32)
        wgu = sb("wgu", [16, 512], F32)
        wspf = sb("wspf", [128, 4, 128], F32)
        wsp = sb("wsp", [128, 4, 128], BF16)
        bspb = sb("bspb", [128, 4, 128], F32)
        identf = sb("identf", [128, 128], F32)
        ident = sb("ident", [128, 128], BF16)
        mtri = sb("mtri", [128, 128], F32)
        ind = sb("ind", [128, 2], F32)
        mhalf = sb("mhalf", [128, 1], F32)
        junk = sb("junk", [128, D], BF16)
        fz = sb("fz", [128, 1], F32)
        PP = [st.enter_context(nc.psum_tensor("pp%d" % i, [128, 2, 512], F32)) for i in range(4)]

        def bank(b):
            return PP[b // 2][:, b % 2, :]

        def bres(b):
            return ("ps", b)

        vtm = big[:, 0:8, :].rearrange("p (s a) t -> p s (a t)", s=4)
        vn = big[:, 8:16, :].rearrange("p (s a) t -> p s (a t)", s=4)
        u = big[:, 16:24, :]
        actT = big
        loga = rgate[:, 0:4096].bitcast(F32).rearrange("p (s e) -> p s e", s=4)
        edec = rgate[:, 4096:8192].bitcast(F32).rearrange("p (s e) -> p s e", s=4)
        yglaT = rgate[:, 0:4096].rearrange("p (c t) -> p c t", c=8)
        ysguT = rgate[:, 4096:8192].rearrange("p (c t) -> p c t", c=8)
        qT = rqk[:, 0:4, :]
        kdec = rqk[:, 4:8, :]
        mergedT = rqk
        lg = rtmp[:, 0, :]
        e1 = rtmp[:, 1, :]

        def al(group, side, nsides, write):
            r = [("tok", group, side)]
            w = [("tok", group, j) for j in range(nsides) if j != side] if write else []
            return r, w

        def A(eng, fn, reads=(), writes=(), alias=(), chan=None):
            rr = list(reads)
            ww = list(writes)
            for (g, sd, n, wr) in alias:
                r_, w_ = al(g, sd, n, wr)
                rr += r_
                ww += w_
            return S.op(eng, fn, rr, ww, chan)

        G_BIG, G_GATE, G_QK, G_TMP = "big", "gate", "qk", "tmp"

        tiles = weight_tiles()
        NW = len(tiles)
        cstate = {"n": 0}

        def emit_casts(upto_):
            while cstate["n"] < min(NW, upto_):
                i = cstate["n"]
                wn, r0, nk, c0, ncw = tiles[i]
                S.op("pool", lambda e, wn=wn, r0=r0, nk=nk, c0=c0, ncw=ncw:
                     e.dma_start(out=wscr[wn][r0:r0 + nk * 128, c0:c0 + ncw], in_=wsrc[wn][r0:r0 + nk * 128, c0:c0 + ncw]),
                     writes=[("scr", i)], chan=("cast", i))
                cstate["n"] += 1

        def cload(dst, src, res):
            S.op("sp", lambda e: e.dma_start(out=dst, in_=src), writes=[res], chan="const")

        cload(gpm[:], c_gpm.partition_broadcast(128), "gpm")
        cload(gpf[:], c_gpf.partition_broadcast(128), "gpf")
        cload(gqm[:], c_gqm.partition_broadcast(128), "gqm")
        cload(gqf[:], c_gqf.partition_broadcast(128), "gqf")
        cload(gnb[:], c_gn.partition_broadcast(128), "gnb")
        cload(lng[:], c_lng.partition_broadcast(128), "lng")
        cload(lnb[:], c_lnb.partition_broadcast(128), "lnb")
        cload(bgb[:], c_bg.partition_broadcast(128), "bgb")
        cload(wgu[:], c_wgu, "wgu")
        cload(wspf[:], c_wsp, "wspf")
        cload(bspb[:], c_bsp.partition_broadcast(128), "bspb")

        S.op("pool", lambda e: e.memset(identf[:], 0.0), writes=["identf"])
        S.op("pool", lambda e: e.affine_select(out=identf[:], in_=identf[:], pattern=[[-1, 128]],
                                               compare_op=ALU.not_equal, fill=1.0, base=0, channel_multiplier=1),
             reads=["identf"], writes=["identf"])
        S.op("dve", lambda e: e.tensor_copy(out=ident[:], in_=identf[:]), reads=["identf"], writes=["ident"])
        S.op("pool", lambda e: e.memset(mhalf[:], -0.5), writes=["mhalf"])
        emit_casts(8)
        S.op("pool", lambda e: e.memset(mtri[:], -1.0 / 16), writes=["mtri"])
        S.op("pool", lambda e: e.affine_select(out=mtri[:], in_=mtri[:], pattern=[[-1, 128]],
                                               compare_op=ALU.is_gt, fill=0.0, base=0, channel_multiplier=1),
             reads=["mtri"], writes=["mtri"])
        S.op("pool", lambda e: e.memset(mtri[64:128, 0:64], 0.0), reads=["mtri"], writes=["mtri"])
        S.op("pool", lambda e: e.memset(ind[:], 0.0), writes=["ind"])
        S.op("pool", lambda e: e.memset(ind[0:64, 0:1], -1.0 / 16), reads=["ind"], writes=["ind"])
        S.op("pool", lambda e: e.memset(ind[64:128, 1:2], -1.0 / 16), reads=["ind"], writes=["ind"])
        S.op("pool", lambda e: e.memset(state[:], 0.0), writes=[("state", h) for h in range(4)])
        S.op("pool", lambda e: e.memset(wspf[64:128, :, 0:64], 0.0), reads=["wspf"], writes=["wspf"])
        S.op("dve", lambda e: e.tensor_copy(out=wsp[:], in_=wspf[:]), reads=["wspf"], writes=["wsp"])

        total_w = NT * NW
        wstate = {"loaded": 0}

        def emit_load(j):
            i = j % NW
            wn, r0, nk, c0, ncw = tiles[i]
            slot = j % NB
            S.op("sp", lambda e: e.dma_start(
                out=wring[slot][:, 0:nk, 0:ncw],
                in_=wscr[wn][r0:r0 + nk * 128, c0:c0 + ncw].rearrange("(k p) e -> p k e", p=128)),
                reads=[("scr", i)], writes=[("w", slot)], chan=("w", slot))

        def w_acquire(j):
            if j < NW:
                emit_casts(j + 10)
            while wstate["loaded"] <= j and wstate["loaded"] < total_w:
                assert wstate["loaded"] < j + NB
                emit_load(wstate["loaded"])
                wstate["loaded"] += 1
            return wring[j % NB], ("w", j % NB)

        def w_release(j):
            nxt = j + NB
            if nxt < total_w and wstate["loaded"] == nxt:
                emit_load(nxt)
                wstate["loaded"] += 1


        scnt = [0]

        def scol():
            c = scnt[0] % 64
            scnt[0] += 1
            return c

        def rstd_from(ss_col, n, res_in):
            c1 = scol()
            c2 = scol()
            S.op("dve", lambda e: e.tensor_scalar(out=stat[:, c1:c1 + 1], in0=stat[:, ss_col:ss_col + 1],
                                                  scalar1=1.0 / n, scalar2=EPS, op0=ALU.mult, op1=ALU.add),
                 reads=[res_in], writes=[("stat", c1)])
            S.op("pool", lambda e: e.tensor_tensor(out=stat[:, c2:c2 + 1], in0=stat[:, c1:c1 + 1], in1=mhalf[:], op=ALU.pow),
                 reads=[("stat", c1), "mhalf"], writes=[("stat", c2)])
            return c2, ("stat", c2)

        def prenorm_compute(s, gtile, gres):
            c0 = scol()
            xsb = xs2[s % 2]
            S.op("act", lambda e: e.activation(out=junk[:], in_=xt[:, s, :], func=AF.Square, accum_out=stat[:, c0:c0 + 1]),
                 reads=[("xt", s)], writes=[("stat", c0)])
            c2, r2 = rstd_from(c0, D, ("stat", c0))
            S.op("dve", lambda e: e.scalar_tensor_tensor(out=xsb[:], in0=xt[:, s, :], scalar=stat[:, c2:c2 + 1], in1=gtile[:],
                                                         op0=ALU.mult, op1=ALU.mult),
                 reads=[("xt", s), r2, gres], writes=[("xs", s % 2)])

        def prenorm_pe(s, tpb):
            xsb = xs2[s % 2]
            pv = bank(tpb).bitcast(BF16).rearrange("p (k t) -> p k t", k=8)
            for k in range(8):
                S.op("pe", lambda e, k=k: e.transpose(out=pv[:, k, :], in_=xsb[:, k * 128:(k + 1) * 128], identity=ident[:]),
                     reads=[("xs", s % 2), "ident"], writes=[bres(tpb)])
            S.op("act", lambda e: e.activation(out=aT[:, :, s * 128:(s + 1) * 128], in_=pv, func=AF.Copy),
                 reads=[bres(tpb)], writes=[("aT", s)])

        def prenorm_transposes(s, gtile, gres, tpb):
            prenorm_compute(s, gtile, gres)
            prenorm_pe(s, tpb)

        def postnorm_residual(s, pp, gtile, gres, final, t):
            src = PP[pp][:].rearrange("p a b -> p (a b)")
            c0 = scol()
            yb = ytm[:, s % 2, :]
            S.op("act", lambda e: e.activation(out=junk[:], in_=src, func=AF.Square, accum_out=stat[:, c0:c0 + 1]),
                 reads=[bres(2 * pp), bres(2 * pp + 1)], writes=[("stat", c0)])
            c2, r2 = rstd_from(c0, D, ("stat", c0))
            S.op("dve", lambda e: e.scalar_tensor_tensor(out=yb, in0=src, scalar=stat[:, c2:c2 + 1], in1=gtile[:],
                                                         op0=ALU.mult, op1=ALU.mult),
                 reads=[bres(2 * pp), bres(2 * pp + 1), r2, gres], writes=[("ytm", s % 2)])
            S.op("pool", lambda e: e.tensor_tensor(out=xt[:, s, :], in0=xt[:, s, :], in1=yb, op=ALU.add),
                 reads=[("xt", s), ("ytm", s % 2)], writes=[("xt", s)])
            if final:
                S.op("sp", lambda e: e.dma_start(out=out[t * T + s * 128:t * T + (s + 1) * 128, :], in_=xt[:, s, :]),
                     reads=[("xt", s)], chan=("st", s), force=True)

        def fence(res):
            S.op("dve", lambda e: e.memset(fz[:], 0.0), writes=list(res) + ["fz"])

        XT_ALL = [("xt", s) for s in range(4)]
        AT_ALL = [("aT", s) for s in range(4)]

        for t in range(NT):
            wj = t * NW
            S.phase(0)
            for s_ in range(4):
                S.op("sp", lambda e, t=t, s_=s_: e.dma_start(out=xt[:, s_, :], in_=x[t * T + s_ * 128:t * T + (s_ + 1) * 128, :]),
                     writes=[("xt", s_)], chan=("xld", s_))
            if t == 0:
                for j in range(min(NB, total_w)):
                    emit_load(j)
                    wstate["loaded"] += 1
            S.phase(1)
            for s in range(4):
                prenorm_transposes(s, gpm, "gpm", s % 2)

            S.phase(2)
            fb = [0]

            def fbank():
                b = 4 + (fb[0] % 4)
                fb[0] += 1
                return b

            def fm_items(evac):
                nonlocal wj
                j = wj
                wj += 1
                items = []
                for c4 in range(4):
                    def item(c4=c4, j=j):
                        wt, wr = w_acquire(j)
                        pb = fbank()
                        for kc in range(8):
                            S.op("pe", lambda e, kc=kc, c4=c4, pb=pb, wt=wt: e.matmul(
                                bank(pb), lhsT=wt[:, kc, c4 * 128:(c4 + 1) * 128], rhs=aT[:, kc, :],
                                start=(kc == 0), stop=(kc == 7)),
                                reads=AT_ALL + [wr], writes=[bres(pb)])
                        evac(c4, pb)
                        if c4 == 3:
                            w_release(j)
                    items.append(item)
                return items

            def tm_items(evac):
                nonlocal wj
                j = wj
                wj += 1
                items = []
                for s in range(4):
                    def item(s=s, j=j):
                        wt, wr = w_acquire(j)
                        pb = fbank()
                        for kc in range(8):
                            S.op("pe", lambda e, kc=kc, s=s, pb=pb, wt=wt: e.matmul(
                                bank(pb), lhsT=aT[:, kc, s * 128:(s + 1) * 128], rhs=wt[:, kc, :],
                                start=(kc == 0), stop=(kc == 7)),
                                reads=[("aT", s), wr], writes=[bres(pb)])
                        evac(s, pb)
                        if s == 3:
                            w_release(j)
                    items.append(item)
                return items

            wt, wr = w_acquire(wj)
            for kc in range(8):
                S.op("pe", lambda e, kc=kc, wt=wt: e.matmul(bank(2)[0:16, :], lhsT=wt[:, kc, 0:16], rhs=aT[:, kc, :],
                                                             start=(kc == 0), stop=(kc == 7)),
                     reads=AT_ALL + [wr], writes=[bres(2)])
            w_release(wj)
            wj += 1
            S.op("dve", lambda e: e.tensor_copy(out=alT[:], in_=bank(2)[0:16, :]), reads=[bres(2)], writes=["alT"])

            def gate_logits(s):
                pb = 2 + (s % 2)
                S.op("pe", lambda e, s=s, pb=pb: e.matmul(bank(pb), lhsT=alT[:, s * 128:(s + 1) * 128], rhs=wgu[:], start=True, stop=True),
                     reads=["alT", "wgu"], writes=[bres(pb)])
                lgs = rtmp[:, s % 2, :]
                e1s = rtmp[:, 2 + (s % 2), :]
                A("dve", lambda e, pb=pb, lgs=lgs: e.tensor_tensor(out=lgs, in0=bank(pb), in1=bgb[:], op=ALU.add),
                  reads=[bres(pb), "bgb"], writes=[("lg", s % 2)], alias=[(G_TMP, 0, 4, True)])
                A("act", lambda e, lgs=lgs, e1s=e1s: e.activation(out=e1s, in_=lgs, func=AF.Exp, scale=-1.0),
                  reads=[("lg", s % 2)], writes=[("e1", s % 2)], alias=[(G_TMP, 0, 4, True)])
                A("act", lambda e, s=s, e1s=e1s: e.activation(out=loga[:, s, :], in_=e1s, func=AF.Ln, bias=1.0),
                  reads=[("e1", s % 2)], writes=[("loga", s)], alias=[(G_TMP, 0, 4, False), (G_GATE, 0, 2, True)])

            def gate_cumsum(s):
                pb = 2 + (s % 2)
                S.op("pe", lambda e, s=s, pb=pb: e.matmul(bank(pb), lhsT=mtri[:], rhs=loga[:, s, :], start=True, stop=True),
                     reads=["mtri", ("loga", s), ("tok", G_GATE, 0)], writes=[bres(pb)])
                A("act", lambda e, s=s, pb=pb: e.activation(out=edec[:, s, :], in_=bank(pb), func=AF.Exp),
                  reads=[bres(pb)], writes=[("edec", s)], alias=[(G_GATE, 0, 2, True)])

            def ev_q(h, pb):
                A("act", lambda e: e.activation(out=qT[:, h, :], in_=bank(pb), func=AF.Copy, scale=float(128 ** -0.5)),
                  reads=[bres(pb)], writes=[("qT", h)], alias=[(G_QK, 0, 2, True)])

            def mk_ev_v(half):
                def ev(s, pb):
                    A("dve", lambda e: e.tensor_copy(out=vtm[:, s, half * 512:(half + 1) * 512], in_=bank(pb)),
                      reads=[bres(pb)], writes=[("vtm", s, half)], alias=[(G_BIG, 0, 2, True)])
                return ev

            def ev_k(s, pb):
                A("dve", lambda e: e.tensor_tensor(out=kdec[:, s, :], in0=bank(pb), in1=edec[:, s, :], op=ALU.mult),
                  reads=[bres(pb), ("edec", s)], writes=[("kdec", s)], alias=[(G_QK, 0, 2, True), (G_GATE, 0, 2, False)])

            def mk_ev_r(half):
                def ev(s, pb):
                    S.op("act", lambda e: e.activation(out=sr[:, s, half * 512:(half + 1) * 512], in_=bank(pb), func=AF.Silu),
                         reads=[bres(pb)], writes=[("sr", s, half)])
                return ev

            def mk_ev_su(half):
                def ev(c4, pb):
                    ch = half * 4 + c4
                    A("act", lambda e: e.activation(out=u[:, ch, :], in_=bank(pb), func=AF.Gelu_apprx_tanh),
                      reads=[bres(pb)], writes=[("u", ch)], alias=[(G_BIG, 0, 2, True)])
                return ev

            gvc = [0]

            def mk_ev_sv(half):
                def ev(s, pb):
                    gi = gvc[0] % 4
                    gvc[0] += 1
                    gv = rtmp[:, gi, :]
                    gres = ("gv", gi)
                    A("act", lambda e: e.activation(out=gv, in_=bank(pb), func=AF.Gelu_apprx_tanh),
                      reads=[bres(pb)], writes=[gres], alias=[(G_TMP, 3, 4, True)])
                    cs = []
                    for g2 in range(2):
                        S.op("dve", lambda e, g2=g2: e.bn_stats(out=bnst[:, g2, :], in_=gv[:, g2 * 256:(g2 + 1) * 256]),
                             reads=[gres, ("tok", G_TMP, 3)], writes=[("bnst", g2)])
                        S.op("dve", lambda e, g2=g2: e.bn_aggr(out=bnag[:, g2, :], in_=bnst[:, g2, :]),
                             reads=[("bnst", g2)], writes=[("bnag", g2)])
                        c1 = scol()
                        c2 = scol()
                        c3 = scol()
                        S.op("dve", lambda e, g2=g2, c1=c1, c3=c3: e.tensor_scalar(out=stat[:, c1:c1 + 1], in0=bnag[:, g2, 1:2], scalar1=EPS, scalar2=None, op0=ALU.add),
                             reads=[("bnag", g2)], writes=[("stat", c1)])
                        S.op("dve", lambda e, g2=g2, c3=c3: e.tensor_copy(out=stat[:, c3:c3 + 1], in_=bnag[:, g2, 0:1]),
                             reads=[("bnag", g2)], writes=[("stat", c3)])
                        S.op("pool", lambda e, c1=c1, c2=c2: e.tensor_tensor(out=stat[:, c2:c2 + 1], in0=stat[:, c1:c1 + 1], in1=mhalf[:], op=ALU.pow),
                             reads=[("stat", c1), "mhalf"], writes=[("stat", c2)])
                        cs.append((c2, c3))
                    for g2 in range(2):
                        c2, c3 = cs[g2]
                        A("dve", lambda e, g2=g2, c2=c2, c3=c3: e.tensor_scalar(out=gv[:, g2 * 256:(g2 + 1) * 256], in0=gv[:, g2 * 256:(g2 + 1) * 256],
                                                                             scalar1=stat[:, c3:c3 + 1], scalar2=stat[:, c2:c2 + 1],
                                                                             op0=ALU.subtract, op1=ALU.mult),
                          reads=[gres, ("stat", c3), ("stat", c2)], writes=[gres], alias=[(G_TMP, 3, 4, True)])
                    A("pool", lambda e: e.tensor_tensor(out=gv, in0=gv, in1=lng[:, half * 512:(half + 1) * 512], op=ALU.mult),
                      reads=[gres, "lng"], writes=[gres], alias=[(G_TMP, 3, 4, True)])
                    A("pool", lambda e: e.tensor_tensor(out=vn[:, s, half * 512:(half + 1) * 512], in0=gv, in1=lnb[:, half * 512:(half + 1) * 512], op=ALU.add),
                      reads=[gres, "lnb", ("tok", G_TMP, 3)], writes=[("vn", s, half)], alias=[(G_BIG, 0, 2, True)])
                return ev

            def mk_ev_sig(dst, name, half):
                def ev(c4, pb):
                    ch = half * 4 + c4
                    S.op("act", lambda e: e.activation(out=dst[:, ch, :], in_=bank(pb), func=AF.Sigmoid),
                         reads=[bres(pb)], writes=[(name, ch)])
                return ev

            gate_logits(0)
            gate_logits(1)
            S.phase(3)
            for it in fm_items(ev_q):
                it()
            S.phase(2)
            gate_cumsum(0)
            gate_cumsum(1)
            gate_logits(2)
            gate_logits(3)
            S.phase(3)
            for it in tm_items(mk_ev_v(0)):
                it()
            S.phase(2)
            gate_cumsum(2)
            gate_cumsum(3)
            for s in range(4):
                for h in range(4):
                    cc = (s * 4 + h) * 2
                    S.op("pe", lambda e, s=s, h=h, cc=cc: e.matmul(bank(2)[:, cc:cc + 2], lhsT=loga[:, s, h * 128:(h + 1) * 128], rhs=ind[:],
                                                                    start=True, stop=True),
                         reads=[("loga", s), "ind", ("tok", G_GATE, 0)], writes=[bres(2)])
            S.op("act", lambda e: e.activation(out=dec[:], in_=bank(2)[:, 0:32], func=AF.Exp), reads=[bres(2)], writes=["dec"])
            S.phase(3)
            for it in tm_items(mk_ev_v(1)):
                it()
            for it in tm_items(ev_k):
                it()

            S.phase(4)
            from collections import deque
            filler = deque()
            filler.extend(tm_items(mk_ev_r(0)))
            filler.extend(tm_items(mk_ev_r(1)))
            filler.extend(fm_items(mk_ev_su(0)))
            filler.extend(fm_items(mk_ev_su(1)))
            filler.extend(tm_items(mk_ev_sv(0)))
            filler.extend(tm_items(mk_ev_sv(1)))
            filler.extend(fm_items(mk_ev_sig(sgg, "sgg", 0)))
            filler.extend(fm_items(mk_ev_sig(sgg, "sgg", 1)))
            filler.extend(fm_items(mk_ev_sig(sgs, "sgs", 0)))
            filler.extend(fm_items(mk_ev_sig(sgs, "sgs", 1)))

            def pull(n):
                for _ in range(n):
                    if filler:
                        filler.popleft()()

            gla_on = S.upto >= 5
            if KMODE == 1:
                pull(len(filler))
            UQ = []
            OPS = []
            if gla_on:
                fence(UQ + [bres(0), bres(1)])
                fence(OPS + [bres(2), bres(3)])
            pending = deque()
            for s in range(4):
                opv = PP[1][:].rearrange("p a b -> p (a b)")
                for j in range(2):
                    par = j
                    rows = slice(64 * j, 64 * (j + 1))
                    for h in range(4):
                        if not gla_on:
                            break
                        uoff = (h // 2) * 256
                        upd = bank(h % 2)[:, uoff:uoff + 256]
                        ures = bres(h % 2)
                        A("pe", lambda e, s=s, h=h, rows=rows, upd=upd: e.matmul(
                            upd, lhsT=kdec[rows, s, h * 128:(h + 1) * 128], rhs=vtm[rows, s, h * 256:(h + 1) * 256], start=True, stop=True),
                          reads=[("kdec", s), ("vtm", s, h // 2)], writes=[ures], alias=[(G_QK, 0, 2, False), (G_BIG, 0, 2, False)])
                        dcol = (s * 4 + h) * 2 + j
                        S.op("dve", lambda e, h=h, upd=upd, dcol=dcol: e.scalar_tensor_tensor(
                            out=state[:, h, :], in0=state[:, h, :], scalar=dec[:, dcol:dcol + 1], in1=upd, op0=ALU.mult, op1=ALU.add),
                            reads=[("state", h), "dec", ures], writes=[("state", h)])
                        S.op("pool", lambda e, h=h, par=par: e.tensor_copy(out=stbf[:, par, h, :], in_=state[:, h, :]),
                             reads=[("state", h)], writes=[("stbf", par, h)])
                        if h == 1 and KMODE == 0:
                            pull(1)
                    if KMODE == 0:
                        pull(3)
                    if j == 0 and pending:
                        pending.popleft()()
                    for h in range(4):
                        if not gla_on:
                            break
                        A("pe", lambda e, s=s, h=h, j=j, par=par, rows=rows: e.matmul(
                            PP[1][rows, :, :].rearrange("p a b -> p (a b)")[:, h * 256:(h + 1) * 256],
                            lhsT=qT[:, h, s * 128 + 64 * j:s * 128 + 64 * (j + 1)], rhs=stbf[:, par, h, :], start=True, stop=True),
                          reads=[("qT", h), ("stbf", par, h)], writes=[bres(2 + h // 2)], alias=[(G_QK, 0, 2, False)])
                    if KMODE == 2:
                        pull(4)
                if not gla_on:
                    continue
                yb = ytm[:, s % 2, :]
                c0s = []
                for h in range(4):
                    c0 = scol()
                    c0s.append(c0)
                    S.op("act", lambda e, h=h, c0=c0, opv=opv: e.activation(out=junk[:, h * 256:(h + 1) * 256], in_=opv[:, h * 256:(h + 1) * 256],
                                                               func=AF.Square, accum_out=stat[:, c0:c0 + 1]),
                         reads=[bres(2 + h // 2)], writes=[("stat", c0)])
                for h in range(4):
                    c2, r2 = rstd_from(c0s[h], 256, ("stat", c0s[h]))
                    S.op("dve", lambda e, h=h, c2=c2, yb=yb, opv=opv: e.scalar_tensor_tensor(
                        out=yb[:, h * 256:(h + 1) * 256], in0=opv[:, h * 256:(h + 1) * 256], scalar=stat[:, c2:c2 + 1],
                        in1=gnb[:, h * 256:(h + 1) * 256], op0=ALU.mult, op1=ALU.mult),
                        reads=[bres(2 + h // 2), r2, "gnb"], writes=[("ytm", s % 2)])
                S.op("pool", lambda e, s=s, yb=yb: e.tensor_tensor(out=ygbs[s % 2][:], in0=yb, in1=sr[:, s, :], op=ALU.mult),
                     reads=[("ytm", s % 2), ("sr", s, 0), ("sr", s, 1)], writes=[("ygb", s % 2)])
                def ytrans(s=s):
                    tb = fbank()
                    pv = bank(tb).bitcast(BF16).rearrange("p (k t) -> p k t", k=8)
                    for k in range(8):
                        S.op("pe", lambda e, k=k, pv=pv: e.transpose(out=pv[:, k, :], in_=ygbs[s % 2][:, k * 128:(k + 1) * 128], identity=ident[:]),
                             reads=[("ygb", s % 2), "ident"], writes=[bres(tb)])
                    A("act", lambda e, s=s, pv=pv: e.activation(out=yglaT[:, :, s * 128:(s + 1) * 128], in_=pv, func=AF.Copy),
                      reads=[bres(tb)], writes=[("ygla", s)], alias=[(G_GATE, 1, 2, True)])
                pending.append(ytrans)
            pull(4)
            while pending:
                pending.popleft()()
            pull(len(filler))

            S.phase(6)
            for s in range(4):
                pp = 1 + (s % 2)
                mv = PP[pp][:].rearrange("p a (c i) -> p (a c) i", c=4)
                for g in range(4):
                    for cc in range(2):
                        ch = g * 2 + cc
                        A("pe", lambda e, s=s, g=g, cc=cc, ch=ch, mv=mv: e.matmul(
                            mv[:, ch, :], lhsT=vn[:, s, g * 256 + cc * 128:g * 256 + (cc + 1) * 128], rhs=wsp[:, g, :], start=True, stop=True),
                          reads=[("vn", s, g // 2), "wsp"], writes=[bres(2 * pp + ch // 4)], alias=[(G_BIG, 0, 2, False)])
                yb4 = ytm[:, s % 2, :].rearrange("p (g c i) -> p g c i", g=4, c=2)
                S.op("dve", lambda e, mv=mv, yb4=yb4: e.tensor_tensor(
                    out=yb4, in0=mv.rearrange("p (g c) i -> p g c i", g=4),
                    in1=bspb[:].unsqueeze(2).broadcast_to([128, 4, 2, 128]), op=ALU.add),
                    reads=[bres(2 * pp), bres(2 * pp + 1), "bspb"], writes=[("ytm", s % 2)])
                A("dve", lambda e, s=s: e.tensor_tensor(out=ysguT[:, :, s * 128:(s + 1) * 128],
                                                        in0=ytm[:, s % 2, :].rearrange("p (c i) -> p c i", c=8),
                                                        in1=u[:, :, s * 128:(s + 1) * 128], op=ALU.mult),
                  reads=[("ytm", s % 2)] + [("u", ch) for ch in range(8)], writes=[("ysgu", s)],
                  alias=[(G_GATE, 1, 2, True), (G_BIG, 0, 2, False)])

            S.phase(7)
            YG = [("ygla", s) for s in range(4)]
            YS = [("ysgu", s) for s in range(4)]
            for half in range(2):
                wtg, wrg = w_acquire(wj)
                wts, wrs = w_acquire(wj + 1)
                for c4 in range(4):
                    ch = half * 4 + c4
                    pp = 1 + (ch % 3)
                    for kc in range(8):
                        A("pe", lambda e, kc=kc, c4=c4, pp=pp, wtg=wtg: e.matmul(
                            PP[pp][:, 0, :], lhsT=wtg[:, kc, c4 * 128:(c4 + 1) * 128], rhs=yglaT[:, kc, :], start=(kc == 0), stop=(kc == 7)),
                          reads=YG + [wrg], writes=[bres(2 * pp)], alias=[(G_GATE, 1, 2, False)])
                    for kc in range(8):
                        A("pe", lambda e, kc=kc, c4=c4, pp=pp, wts=wts: e.matmul(
                            PP[pp][:, 1, :], lhsT=wts[:, kc, c4 * 128:(c4 + 1) * 128], rhs=ysguT[:, kc, :], start=(kc == 0), stop=(kc == 7)),
                          reads=YS + [wrs], writes=[bres(2 * pp + 1)], alias=[(G_GATE, 1, 2, False)])
                    t1 = rtmp[:, (ch % 2) * 2, :]
                    t2 = rtmp[:, (ch % 2) * 2 + 1, :]
                    A("dve", lambda e, ch=ch, pp=pp, t1=t1: e.tensor_tensor(out=t1, in0=PP[pp][:, 0, :], in1=sgg[:, ch, :], op=ALU.mult),
                      reads=[bres(2 * pp), ("sgg", ch)], writes=[("t1", ch % 2)], alias=[(G_TMP, 1, 4, True)])
                    A("dve", lambda e, ch=ch, pp=pp, t2=t2: e.tensor_tensor(out=t2, in0=PP[pp][:, 1, :], in1=sgs[:, ch, :], op=ALU.mult),
                      reads=[bres(2 * pp + 1), ("sgs", ch)], writes=[("t2", ch % 2)], alias=[(G_TMP, 1, 4, True)])
                    A("pool", lambda e, ch=ch, t1=t1, t2=t2: e.tensor_tensor(out=mergedT[:, ch, :], in0=t1, in1=t2, op=ALU.add),
                      reads=[("t1", ch % 2), ("t2", ch % 2)], writes=[("merged", ch)], alias=[(G_TMP, 1, 4, False), (G_QK, 1, 2, True)])
                w_release(wj)
                w_release(wj + 1)
                wj += 2

            S.phase(8)
            MG = [("merged", ch) for ch in range(8)]
            wt0, wr0 = w_acquire(wj)
            wt1, wr1 = w_acquire(wj + 1)
            for s in range(4):
                pp = 1 + (s % 3)
                for half, (wt, wr) in enumerate(((wt0, wr0), (wt1, wr1))):
                    for kc in range(8):
                        A("pe", lambda e, kc=kc, s=s, pp=pp, half=half, wt=wt: e.matmul(
                            PP[pp][:, half, :], lhsT=mergedT[:, kc, s * 128:(s + 1) * 128], rhs=wt[:, kc, :], start=(kc == 0), stop=(kc == 7)),
                          reads=MG + [wr], writes=[bres(2 * pp + half)], alias=[(G_QK, 1, 2, False)])
                postnorm_residual(s, pp, gqm, "gqm", False, t)
                S.phase(9)
                prenorm_compute(s, gpf, "gpf")
                if s >= 1:
                    prenorm_pe(s - 1, (s - 1) % 2)
                S.phase(8)
            w_release(wj)
            w_release(wj + 1)
            wj += 2

            S.phase(9)
            prenorm_pe(3, 1)

            S.phase(10)
            fc = 0
            for tt in range(6):
                wtg, wrg = w_acquire(wj)
                wtu, wru = w_acquire(wj + 1)
                for c4 in range(4 if tt < 5 else 2):
                    pp = 1 + (fc % 3)
                    for kc in range(8):
                        S.op("pe", lambda e, kc=kc, c4=c4, pp=pp, wtg=wtg: e.matmul(
                            PP[pp][:, 0, :], lhsT=wtg[:, kc, c4 * 128:(c4 + 1) * 128], rhs=aT[:, kc, :], start=(kc == 0), stop=(kc == 7)),
                            reads=AT_ALL + [wrg], writes=[bres(2 * pp)])
                    for kc in range(8):
                        S.op("pe", lambda e, kc=kc, c4=c4, pp=pp, wtu=wtu: e.matmul(
                            PP[pp][:, 1, :], lhsT=wtu[:, kc, c4 * 128:(c4 + 1) * 128], rhs=aT[:, kc, :], start=(kc == 0), stop=(kc == 7)),
                            reads=AT_ALL + [wru], writes=[bres(2 * pp + 1)])
                    sgt = rtmp[:, fc % 2, :]
                    A("act", lambda e, pp=pp, sgt=sgt: e.activation(out=sgt, in_=PP[pp][:, 0, :], func=AF.Silu),
                      reads=[bres(2 * pp)], writes=[("sgt", fc % 2)], alias=[(G_TMP, 2, 4, True)])
                    A("dve", lambda e, pp=pp, sgt=sgt, fc=fc: e.tensor_tensor(out=actT[:, fc, :], in0=PP[pp][:, 1, :], in1=sgt, op=ALU.mult),
                      reads=[bres(2 * pp + 1), ("sgt", fc % 2)], writes=[("act", fc)], alias=[(G_TMP, 2, 4, False), (G_BIG, 1, 2, True)])
                    fc += 1
                w_release(wj)
                w_release(wj + 1)
                wj += 2

            S.phase(11)
            for half in range(2):
                for kg, nk in enumerate((8, 8, 6)):
                    wt, wr = w_acquire(wj)
                    for s in range(4):
                        for k in range(nk):
                            fcc = kg * 8 + k
                            A("pe", lambda e, s=s, k=k, fcc=fcc, half=half, wt=wt: e.matmul(
                                PP[s][:, half, :], lhsT=actT[:, fcc, s * 128:(s + 1) * 128], rhs=wt[:, k, :],
                                start=(fcc == 0), stop=(fcc == 21)),
                              reads=[("act", fcc), wr], writes=[bres(2 * s + half)], alias=[(G_BIG, 1, 2, False)])
                    w_release(wj)
                    wj += 1
            for s in range(4):
                postnorm_residual(s, s, gqf, "gqf", True, t)

        S.emit()
    return nc, S


_CACHE = {}


def _prep_consts(inp):
    f = lambda a: np.ascontiguousarray(np.asarray(a, dtype=np.float32))
    return {
        "w_in": f(inp["w_in"][0]),
        "w_bg": f(inp["w_branch_gla"][0]),
        "w_bs": f(inp["w_branch_sgu"][0]),
        "w_out": f(inp["w_out"][0]),
        "w_fi": f(inp["w_ffn_in"][0]),
        "w_fo": f(inp["w_ffn_out"][0]),
        "g_pm": f(inp["norm_pre_mix"][0]),
        "g_pf": f(inp["norm_pre_ffn"][0]),
        "g_qm": f(inp["norm_post_mix"][0]),
        "g_qf": f(inp["norm_post_ffn"][0]),
        "b_gate": f(inp["b_gate"][0]),
        "gla_norm": f(np.asarray(inp["gla_norm"][0]).reshape(-1)),
        "ln_g": f(np.asarray(inp["sgu_ln_g"][0]).reshape(-1)),
        "ln_b": f(np.asarray(inp["sgu_ln_b"][0]).reshape(-1)),
        "w_gu": f(inp["w_gate_up"][0]),
        "wspT": f(np.transpose(np.asarray(inp["w_spatial"][0]), (2, 0, 1))),
        "b_sp": f(np.asarray(inp["b_spatial"][0])),
    }


def kernel(**inputs):
    x = np.asarray(inputs["x"], dtype=np.float32)
    B, S_tok, _ = x.shape
    key = S_tok
    if key not in _CACHE:
        _CACHE[key] = build(S_tok)[0]
    nc = _CACHE[key]
    consts = _prep_consts(inputs)
    in_maps = []
    for b in range(B):
        m = dict(consts)
        m["x"] = np.ascontiguousarray(x[b])
        in_maps.append(m)
    res = run_bass_kernel_spmd(nc, in_maps, core_ids=list(range(B)))
    return np.stack([np.asarray(r["out"]) for r in res.results], axis=0).astype(np.float32)


def _simulate(S):
    ops = S.ops
    engs = ("pe", "act", "dve", "pool", "sp")
    streams = {e: [i for i, o in enumerate(ops) if o.eng == e] for e in engs}
    pos = {e: 0 for e in engs}
    done = set()
    progress = True
    while progress:
        progress = False
        for e in engs:
            while pos[e] < len(streams[e]):
                i = streams[e][pos[e]]
                if all(d in done for (_k, d) in ops[i].waits):
                    done.add(i)
                    pos[e] += 1
                    progress = True
                else:
                    break
    stuck = {e: (pos[e], len(streams[e])) for e in engs if pos[e] < len(streams[e])}
    return stuck
```

```python
import contextlib
import numpy as np
import concourse.bass as bass
import concourse.mybir as mybir
from concourse.bass_utils import run_bass_kernel_spmd

F32 = mybir.dt.float32
BF16 = mybir.dt.bfloat16
AF = mybir.ActivationFunctionType
ALU = mybir.AluOpType

D = 1024
DIN = 7184
DFF = 2816
T = 512
NB = 4
import os
KMODE = int(os.environ.get("KMODE", "0"))
STRICT = int(os.environ.get("STRICT", "0"))
EPS = 1e-6


class _Op:
    __slots__ = ("eng", "fn", "deps", "chan", "seq", "waits")

    def __init__(self, eng, fn, chan):
        self.eng = eng
        self.fn = fn
        self.chan = chan
        self.deps = {}
        self.seq = 0
        self.waits = []


class Sched:
    COMPUTE = ("pe", "act", "dve", "pool")

    def __init__(self, nc):
        self.nc = nc
        self.ops = []
        self.last_w = {}
        self.readers = {}
        self.upto = 99
        self.stopped = False

    def phase(self, k):
        self.stopped = k > self.upto

    def op(self, eng, fn, reads=(), writes=(), chan=None, force=False):
        if self.stopped and not force:
            return -1
        idx = len(self.ops)
        o = _Op(eng, fn, chan)
        for r in reads:
            w = self.last_w.get(r)
            if w is not None:
                o.deps[w] = True
        for r in writes:
            w = self.last_w.get(r)
            if w is not None and w not in o.deps:
                o.deps[w] = False
            for rd in self.readers.get(r, ()):
                if rd not in o.deps:
                    o.deps[rd] = False
        for r in reads:
            self.readers.setdefault(r, []).append(idx)
        for r in writes:
            self.last_w[r] = idx
            self.readers[r] = []
        self.ops.append(o)
        return idx

    def emit(self):
        nc = self.nc
        ops = self.ops
        engs = ("pe", "act", "dve", "pool", "sp")
        streams = {e: [] for e in engs}
        for i, o in enumerate(ops):
            streams[o.eng].append(i)
            o.seq = len(streams[o.eng])
        chan_cnt = {}
        dma_val = {}
        for i, o in enumerate(ops):
            if o.chan is not None:
                chan_cnt[o.chan] = chan_cnt.get(o.chan, 0) + 16
                dma_val[i] = chan_cnt[o.chan]
        need_sig = set()
        for e in engs:
            waited = {}
            for i in streams[e]:
                o = ops[i]
                for d, is_raw in sorted(o.deps.items()):
                    p = ops[d]
                    if p.chan is not None:
                        key = ("dma", p.chan)
                        val = dma_val[d]
                        if p.chan == "const":
                            val = chan_cnt[p.chan]
                            dma_val[d] = val
                    else:
                        if p.eng == e and o.chan is None:
                            if e == "pe" or (not is_raw and not STRICT):
                                continue
                        key = ("eng", p.eng)
                        val = p.seq
                    if waited.get(key, 0) >= val:
                        continue
                    waited[key] = val
                    o.waits.append((key, d))
                    if p.chan is None:
                        need_sig.add(d)
        rank = {}
        for e in self.COMPUTE:
            n = 0
            for i in streams[e]:
                if i in need_sig:
                    n += 1
                    rank[i] = n
        self.stats = {e: len(streams[e]) for e in engs}
        with contextlib.ExitStack() as st:
            esem = {e: st.enter_context(nc.semaphore("s_" + e)) for e in self.COMPUTE}
            csem = {}
            for n_, c in enumerate(chan_cnt):
                csem[c] = st.enter_context(nc.semaphore("c%d" % n_))
            block = st.enter_context(nc.Block())

            def run_stream(e, engobj):
                for i in streams[e]:
                    o = ops[i]
                    for key, d in o.waits:
                        if key[0] == "dma":
                            engobj.wait_ge(csem[key[1]], dma_val[d])
                        else:
                            engobj.wait_ge(esem[key[1]], rank[d])
                    ins = o.fn(engobj)
                    if o.chan is not None:
                        ins.then_inc(csem[o.chan], 16)
                    elif i in need_sig:
                        ins.then_inc(esem[e], 1)
                if e == "sp":
                    for c, v in chan_cnt.items():
                        engobj.wait_ge(csem[c], v)

            @block.tensor
            def _(eng):
                run_stream("pe", eng)

            @block.scalar
            def _(eng):
                run_stream("act", eng)

            @block.vector
            def _(eng):
                run_stream("dve", eng)

            @block.gpsimd
            def _(eng):
                run_stream("pool", eng)

            @block.sync
            def _(eng):
                run_stream("sp", eng)


def weight_tiles():
    tl = []
    tl.append(("w_in", 0, 8, 3072, 16))
    tl.append(("w_in", 0, 8, 0, 512))
    tl.append(("w_in", 0, 8, 1024, 512))
    tl.append(("w_in", 0, 8, 1536, 512))
    tl.append(("w_in", 0, 8, 512, 512))
    tl.append(("w_in", 0, 8, 2048, 512))
    tl.append(("w_in", 0, 8, 2560, 512))
    tl.append(("w_in", 0, 8, 3088, 512))
    tl.append(("w_in", 0, 8, 3600, 512))
    tl.append(("w_in", 0, 8, 4112, 512))
    tl.append(("w_in", 0, 8, 4624, 512))
    tl.append(("w_in", 0, 8, 5136, 512))
    tl.append(("w_in", 0, 8, 5648, 512))
    tl.append(("w_in", 0, 8, 6160, 512))
    tl.append(("w_in", 0, 8, 6672, 512))
    for half in range(2):
        tl.append(("w_bg", 0, 8, half * 512, 512))
        tl.append(("w_bs", 0, 8, half * 512, 512))
    tl.append(("w_out", 0, 8, 0, 512))
    tl.append(("w_out", 0, 8, 512, 512))
    for t in range(6):
        ncw = 512 if t < 5 else 256
        tl.append(("w_fi", 0, 8, t * 512, ncw))
        tl.append(("w_fi", 0, 8, DFF + t * 512, ncw))
    for half in range(2):
        for kg, nk in enumerate((8, 8, 6)):
            tl.append(("w_fo", kg * 8 * 128, nk, half * 512, 512))
    return tl


def build(S_tok, upto=99):
    NT = S_tok // T
    nc = bass.Bass("TRN2", target_bir_lowering=False)

    def din(name, shape, dt=F32):
        return nc.dram_tensor(name, shape, dt, kind="ExternalInput").ap()

    x = din("x", [S_tok, D])
    wsrc = {
        "w_in": din("w_in", [D, DIN]),
        "w_bg": din("w_bg", [D, D]),
        "w_bs": din("w_bs", [D, D]),
        "w_out": din("w_out", [D, D]),
        "w_fi": din("w_fi", [D, 2 * DFF]),
        "w_fo": din("w_fo", [DFF, D]),
    }
    wscr = {k: nc.dram_tensor(k + "_bf", list(v.shape), BF16, kind="Internal").ap() for k, v in wsrc.items()}
    c_gpm = din("g_pm", [D])
    c_gpf = din("g_pf", [D])
    c_gqm = din("g_qm", [D])
    c_gqf = din("g_qf", [D])
    c_bg = din("b_gate", [512])
    c_gn = din("gla_norm", [D])
    c_lng = din("ln_g", [D])
    c_lnb = din("ln_b", [D])
    c_wgu = din("w_gu", [16, 512])
    c_wsp = din("wspT", [128, 4, 128])
    c_bsp = din("b_sp", [4, 128])
    out = nc.dram_tensor("out", [S_tok, D], F32, kind="ExternalOutput").ap()

    S = Sched(nc)
    S.upto = upto
    with contextlib.ExitStack() as st:
        def sb(name, shape, dt):
            return st.enter_context(nc.sbuf_tensor(name, shape, dt))

        xt = sb("xt", [128, 4, D], F32)
        aT = sb("aT", [128, 8, T], BF16)
        xs = sb("xs", [128, D], BF16)
        xsB = sb("xsB", [128, D], BF16)
        xs2 = [xs, xsB]
        sr = sb("sr", [128, 4, D], BF16)
        alT = sb("alT", [16, T], F32)
        state = sb("state", [128, 4, 256], F32)
        stbf = sb("stbf", [128, 2, 4, 256], BF16)
        sgg = sb("sgg", [128, 8, T], BF16)
        sgs = sb("sgs", [128, 8, T], BF16)
        ytm = sb("ytm", [128, 2, D], F32)
        ygb = sb("ygb", [128, D], BF16)
        ygb2 = sb("ygb2", [128, D], BF16)
        ygbs = [ygb, ygb2]
        big = sb("big", [128, 24, T], BF16)
        rgate = sb("rgate", [128, 8192], BF16)
        rqk = sb("rqk", [128, 8, T], BF16)
        rtmp = sb("rtmp", [128, 4, T], F32)
        stat = sb("stat", [128, 64], F32)
        dec = sb("dec", [128, 32], F32)
        bnst = sb("bnst", [128, 4, 6], F32)
        bnag = sb("bnag", [128, 4, 2], F32)
        wring = [sb("wr%d" % i, [128, 8, 512], BF16) for i in range(NB)]
        gpm = sb("gpm", [128, D], F32)
        gpf = sb("gpf", [128, D], F32)
        gqm = sb("gqm", [128, D], F32)
        gqf = sb("gqf", [128, D], F32)
        gnb = sb("gnb", [128, D], F32)
        lng = sb("lng", [128, D], F32)
        lnb = sb("lnb", [128, D], F32)
        bgb = sb("bgb", [128, 512], F32)
        wgu = sb("wgu", [16, 512], F32)
        wspf = sb("wspf", [128, 4, 128], F32)
        wsp = sb("wsp", [128, 4, 128], BF16)
        bspb = sb("bspb", [128, 4, 128], F32)
        identf = sb("identf", [128, 128], F32)
        ident = sb("ident", [128, 128], BF16)
        mtri = sb("mtri", [128, 128], F32)
        ind = sb("ind", [128, 2], F32)
        mhalf = sb("mhalf", [128, 1], F32)
        junk = sb("junk", [128, D], BF16)
        fz = sb("fz", [128, 1], F32)
        PP = [st.enter_context(nc.psum_tensor("pp%d" % i, [128, 2, 512], F32)) for i in range(4)]

        def bank(b):
            return PP[b // 2][:, b % 2, :]

        def bres(b):
            return ("ps", b)

        vtm = big[:, 0:8, :].rearrange("p (s a) t -> p s (a t)", s=4)
        vn = big[:, 8:16, :].rearrange("p (s a) t -> p s (a t)", s=4)
        u = big[:, 16:24, :]
        actT = big
        loga = rgate[:, 0:4096].bitcast(F32).rearrange("p (s e) -> p s e", s=4)
        edec = rgate[:, 4096:8192].bitcast(F32).rearrange("p (s e) -> p s e", s=4)
        yglaT = rgate[:, 0:4096].rearrange("p (c t) -> p c t", c=8)
        ysguT = rgate[:, 4096:8192].rearrange("p (c t) -> p c t", c=8)
        qT = rqk[:, 0:4, :]
        kdec = rqk[:, 4:8, :]
        mergedT = rqk
        lg = rtmp[:, 0, :]
        e1 = rtmp[:, 1, :]

        def al(group, side, nsides, write):
            r = [("tok", group, side)]
            w = [("tok", group, j) for j in range(nsides) if j != side] if write else []
            return r, w

        def A(eng, fn, reads=(), writes=(), alias=(), chan=None):
            rr = list(reads)
            ww = list(writes)
            for (g, sd, n, wr) in alias:
                r_, w_ = al(g, sd, n, wr)
                rr += r_
                ww += w_
            return S.op(eng, fn, rr, ww, chan)

        G_BIG, G_GATE, G_QK, G_TMP = "big", "gate", "qk", "tmp"

        tiles = weight_tiles()
        NW = len(tiles)
        cstate = {"n": 0}

        def emit_casts(upto_):
            while cstate["n"] < min(NW, upto_):
                i = cstate["n"]
                wn, r0, nk, c0, ncw = tiles[i]
                S.op("pool", lambda e, wn=wn, r0=r0, nk=nk, c0=c0, ncw=ncw:
                     e.dma_start(out=wscr[wn][r0:r0 + nk * 128, c0:c0 + ncw], in_=wsrc[wn][r0:r0 + nk * 128, c0:c0 + ncw]),
                     writes=[("scr", i)], chan=("cast", i))
                cstate["n"] += 1

        def cload(dst, src, res):
            S.op("sp", lambda e: e.dma_start(out=dst, in_=src), writes=[res], chan="const")

        cload(gpm[:], c_gpm.partition_broadcast(128), "gpm")
        cload(gpf[:], c_gpf.partition_broadcast(128), "gpf")
        cload(gqm[:], c_gqm.partition_broadcast(128), "gqm")
        cload(gqf[:], c_gqf.partition_broadcast(128), "gqf")
        cload(gnb[:], c_gn.partition_broadcast(128), "gnb")
        cload(lng[:], c_lng.partition_broadcast(128), "lng")
        cload(lnb[:], c_lnb.partition_broadcast(128), "lnb")
        cload(bgb[:], c_bg.partition_broadcast(128), "bgb")
        cload(wgu[:], c_wgu, "wgu")
        cload(wspf[:], c_wsp, "wspf")
        cload(bspb[:], c_bsp.partition_broadcast(128), "bspb")

        S.op("pool", lambda e: e.memset(identf[:], 0.0), writes=["identf"])
        S.op("pool", lambda e: e.affine_select(out=identf[:], in_=identf[:], pattern=[[-1, 128]],
                                               compare_op=ALU.not_equal, fill=1.0, base=0, channel_multiplier=1),
             reads=["identf"], writes=["identf"])
        S.op("dve", lambda e: e.tensor_copy(out=ident[:], in_=identf[:]), reads=["identf"], writes=["ident"])
        S.op("pool", lambda e: e.memset(mhalf[:], -0.5), writes=["mhalf"])
        emit_casts(8)
        S.op("pool", lambda e: e.memset(mtri[:], -1.0 / 16), writes=["mtri"])
        S.op("pool", lambda e: e.affine_select(out=mtri[:], in_=mtri[:], pattern=[[-1, 128]],
                                               compare_op=ALU.is_gt, fill=0.0, base=0, channel_multiplier=1),
             reads=["mtri"], writes=["mtri"])
        S.op("pool", lambda e: e.memset(mtri[64:128, 0:64], 0.0), reads=["mtri"], writes=["mtri"])
        S.op("pool", lambda e: e.memset(ind[:], 0.0), writes=["ind"])
        S.op("pool", lambda e: e.memset(ind[0:64, 0:1], -1.0 / 16), reads=["ind"], writes=["ind"])
        S.op("pool", lambda e: e.memset(ind[64:128, 1:2], -1.0 / 16), reads=["ind"], writes=["ind"])
        S.op("pool", lambda e: e.memset(state[:], 0.0), writes=[("state", h) for h in range(4)])
        S.op("pool", lambda e: e.memset(wspf[64:128, :, 0:64], 0.0), reads=["wspf"], writes=["wspf"])
        S.op("dve", lambda e: e.tensor_copy(out=wsp[:], in_=wspf[:]), reads=["wspf"], writes=["wsp"])

        total_w = NT * NW
        wstate = {"loaded": 0}

        def emit_load(j):
            i = j % NW
            wn, r0, nk, c0, ncw = tiles[i]
            slot = j % NB
            S.op("sp", lambda e: e.dma_start(
                out=wring[slot][:, 0:nk, 0:ncw],
                in_=wscr[wn][r0:r0 + nk * 128, c0:c0 + ncw].rearrange("(k p) e -> p k e", p=128)),
                reads=[("scr", i)], writes=[("w", slot)], chan=("w", slot))

        def w_acquire(j):
            if j < NW:
                emit_casts(j + 10)
            while wstate["loaded"] <= j and wstate["loaded"] < total_w:
                assert wstate["loaded"] < j + NB
                emit_load(wstate["loaded"])
                wstate["loaded"] += 1
            return wring[j % NB], ("w", j % NB)

        def w_release(j):
            nxt = j + NB
            if nxt < total_w and wstate["loaded"] == nxt:
                emit_load(nxt)
                wstate["loaded"] += 1


        scnt = [0]

        def scol():
            c = scnt[0] % 64
            scnt[0] += 1
            return c

        def rstd_from(ss_col, n, res_in):
            c1 = scol()
            c2 = scol()
            S.op("dve", lambda e: e.tensor_scalar(out=stat[:, c1:c1 + 1], in0=stat[:, ss_col:ss_col + 1],
                                                  scalar1=1.0 / n, scalar2=EPS, op0=ALU.mult, op1=ALU.add),
                 reads=[res_in], writes=[("stat", c1)])
            S.op("pool", lambda e: e.tensor_tensor(out=stat[:, c2:c2 + 1], in0=stat[:, c1:c1 + 1], in1=mhalf[:], op=ALU.pow),
                 reads=[("stat", c1), "mhalf"], writes=[("stat", c2)])
            return c2, ("stat", c2)

        def prenorm_compute(s, gtile, gres):
            c0 = scol()
            xsb = xs2[s % 2]
            S.op("act", lambda e: e.activation(out=junk[:], in_=xt[:, s, :], func=AF.Square, accum_out=stat[:, c0:c0 + 1]),
                 reads=[("xt", s)], writes=[("stat", c0)])
            c2, r2 = rstd_from(c0, D, ("stat", c0))
            S.op("dve", lambda e: e.scalar_tensor_tensor(out=xsb[:], in0=xt[:, s, :], scalar=stat[:, c2:c2 + 1], in1=gtile[:],
                                                         op0=ALU.mult, op1=ALU.mult),
                 reads=[("xt", s), r2, gres], writes=[("xs", s % 2)])

        def prenorm_pe(s, tpb):
            xsb = xs2[s % 2]
            pv = bank(tpb).bitcast(BF16).rearrange("p (k t) -> p k t", k=8)
            for k in range(8):
                S.op("pe", lambda e, k=k: e.transpose(out=pv[:, k, :], in_=xsb[:, k * 128:(k + 1) * 128], identity=ident[:]),
                     reads=[("xs", s % 2), "ident"], writes=[bres(tpb)])
            S.op("act", lambda e: e.activation(out=aT[:, :, s * 128:(s + 1) * 128], in_=pv, func=AF.Copy),
                 reads=[bres(tpb)], writes=[("aT", s)])

        def prenorm_transposes(s, gtile, gres, tpb):
            prenorm_compute(s, gtile, gres)
            prenorm_pe(s, tpb)

        def postnorm_residual(s, pp, gtile, gres, final, t):
            src = PP[pp][:].rearrange("p a b -> p (a b)")
            c0 = scol()
            yb = ytm[:, s % 2, :]
            S.op("act", lambda e: e.activation(out=junk[:], in_=src, func=AF.Square, accum_out=stat[:, c0:c0 + 1]),
                 reads=[bres(2 * pp), bres(2 * pp + 1)], writes=[("stat", c0)])
            c2, r2 = rstd_from(c0, D, ("stat", c0))
            S.op("dve", lambda e: e.scalar_tensor_tensor(out=yb, in0=src, scalar=stat[:, c2:c2 + 1], in1=gtile[:],
                                                         op0=ALU.mult, op1=ALU.mult),
                 reads=[bres(2 * pp), bres(2 * pp + 1), r2, gres], writes=[("ytm", s % 2)])
            S.op("pool", lambda e: e.tensor_tensor(out=xt[:, s, :], in0=xt[:, s, :], in1=yb, op=ALU.add),
                 reads=[("xt", s), ("ytm", s % 2)], writes=[("xt", s)])
            if final:
                S.op("sp", lambda e: e.dma_start(out=out[t * T + s * 128:t * T + (s + 1) * 128, :], in_=xt[:, s, :]),
                     reads=[("xt", s)], chan=("st", s), force=True)

        def fence(res):
            S.op("dve", lambda e: e.memset(fz[:], 0.0), writes=list(res) + ["fz"])

        XT_ALL = [("xt", s) for s in range(4)]
        AT_ALL = [("aT", s) for s in range(4)]

        for t in range(NT):
            wj = t * NW
            S.phase(0)
            for s_ in range(4):
                S.op("sp", lambda e, t=t, s_=s_: e.dma_start(out=xt[:, s_, :], in_=x[t * T + s_ * 128:t * T + (s_ + 1) * 128, :]),
                     writes=[("xt", s_)], chan=("xld", s_))
            if t == 0:
                for j in range(min(NB, total_w)):
                    emit_load(j)
                    wstate["loaded"] += 1
            S.phase(1)
            for s in range(4):
                prenorm_transposes(s, gpm, "gpm", s % 2)

            S.phase(2)
            fb = [0]

            def fbank():
                b = 4 + (fb[0] % 4)
                fb[0] += 1
                return b

            def fm_items(evac):
                nonlocal wj
                j = wj
                wj += 1
                items = []
                for c4 in range(4):
                    def item(c4=c4, j=j):
                        wt, wr = w_acquire(j)
                        pb = fbank()
                        for kc in range(8):
                            S.op("pe", lambda e, kc=kc, c4=c4, pb=pb, wt=wt: e.matmul(
                                bank(pb), lhsT=wt[:, kc, c4 * 128:(c4 + 1) * 128], rhs=aT[:, kc, :],
                                start=(kc == 0), stop=(kc == 7)),
                                reads=AT_ALL + [wr], writes=[bres(pb)])
                        evac(c4, pb)
                        if c4 == 3:
                            w_release(j)
                    items.append(item)
                return items

            def tm_items(evac):
                nonlocal wj
                j = wj
                wj += 1
                items = []
                for s in range(4):
                    def item(s=s, j=j):
                        wt, wr = w_acquire(j)
                        pb = fbank()
                        for kc in range(8):
                            S.op("pe", lambda e, kc=kc, s=s, pb=pb, wt=wt: e.matmul(
                                bank(pb), lhsT=aT[:, kc, s * 128:(s + 1) * 128], rhs=wt[:, kc, :],
                                start=(kc == 0), stop=(kc == 7)),
                                reads=[("aT", s), wr], writes=[bres(pb)])
                        evac(s, pb)
                        if s == 3:
                            w_release(j)
                    items.append(item)
                return items

            wt, wr = w_acquire(wj)
            for kc in range(8):
                S.op("pe", lambda e, kc=kc, wt=wt: e.matmul(bank(2)[0:16, :], lhsT=wt[:, kc, 0:16], rhs=aT[:, kc, :],
                                                             start=(kc == 0), stop=(kc == 7)),
                     reads=AT_ALL + [wr], writes=[bres(2)])
            w_release(wj)
            wj += 1
            S.op("dve", lambda e: e.tensor_copy(out=alT[:], in_=bank(2)[0:16, :]), reads=[bres(2)], writes=["alT"])

            def gate_logits(s):
                pb = 2 + (s % 2)
                S.op("pe", lambda e, s=s, pb=pb: e.matmul(bank(pb), lhsT=alT[:, s * 128:(s + 1) * 128], rhs=wgu[:], start=True, stop=True),
                     reads=["alT", "wgu"], writes=[bres(pb)])
                lgs = rtmp[:, s % 2, :]
                e1s = rtmp[:, 2 + (s % 2), :]
                A("dve", lambda e, pb=pb, lgs=lgs: e.tensor_tensor(out=lgs, in0=bank(pb), in1=bgb[:], op=ALU.add),
                  reads=[bres(pb), "bgb"], writes=[("lg", s % 2)], alias=[(G_TMP, 0, 4, True)])
                A("act", lambda e, lgs=lgs, e1s=e1s: e.activation(out=e1s, in_=lgs, func=AF.Exp, scale=-1.0),
                  reads=[("lg", s % 2)], writes=[("e1", s % 2)], alias=[(G_TMP, 0, 4, True)])
                A("act", lambda e, s=s, e1s=e1s: e.activation(out=loga[:, s, :], in_=e1s, func=AF.Ln, bias=1.0),
                  reads=[("e1", s % 2)], writes=[("loga", s)], alias=[(G_TMP, 0, 4, False), (G_GATE, 0, 2, True)])

            def gate_cumsum(s):
                pb = 2 + (s % 2)
                S.op("pe", lambda e, s=s, pb=pb: e.matmul(bank(pb), lhsT=mtri[:], rhs=loga[:, s, :], start=True, stop=True),
                     reads=["mtri", ("loga", s), ("tok", G_GATE, 0)], writes=[bres(pb)])
                A("act", lambda e, s=s, pb=pb: e.activation(out=edec[:, s, :], in_=bank(pb), func=AF.Exp),
                  reads=[bres(pb)], writes=[("edec", s)], alias=[(G_GATE, 0, 2, True)])

            def ev_q(h, pb):
                A("act", lambda e: e.activation(out=qT[:, h, :], in_=bank(pb), func=AF.Copy, scale=float(128 ** -0.5)),
                  reads=[bres(pb)], writes=[("qT", h)], alias=[(G_QK, 0, 2, True)])

            def mk_ev_v(half):
                def ev(s, pb):
                    A("dve", lambda e: e.tensor_copy(out=vtm[:, s, half * 512:(half + 1) * 512], in_=bank(pb)),
                      reads=[bres(pb)], writes=[("vtm", s, half)], alias=[(G_BIG, 0, 2, True)])
                return ev

            def ev_k(s, pb):
                A("dve", lambda e: e.tensor_tensor(out=kdec[:, s, :], in0=bank(pb), in1=edec[:, s, :], op=ALU.mult),
                  reads=[bres(pb), ("edec", s)], writes=[("kdec", s)], alias=[(G_QK, 0, 2, True), (G_GATE, 0, 2, False)])

            def mk_ev_r(half):
                def ev(s, pb):
                    S.op("act", lambda e: e.activation(out=sr[:, s, half * 512:(half + 1) * 512], in_=bank(pb), func=AF.Silu),
                         reads=[bres(pb)], writes=[("sr", s, half)])
                return ev

            def mk_ev_su(half):
                def ev(c4, pb):
                    ch = half * 4 + c4
                    A("act", lambda e: e.activation(out=u[:, ch, :], in_=bank(pb), func=AF.Gelu_apprx_tanh),
                      reads=[bres(pb)], writes=[("u", ch)], alias=[(G_BIG, 0, 2, True)])
                return ev

            gvc = [0]

            def mk_ev_sv(half):
                def ev(s, pb):
                    gi = gvc[0] % 4
                    gvc[0] += 1
                    gv = rtmp[:, gi, :]
                    gres = ("gv", gi)
                    A("act", lambda e: e.activation(out=gv, in_=bank(pb), func=AF.Gelu_apprx_tanh),
                      reads=[bres(pb)], writes=[gres], alias=[(G_TMP, 3, 4, True)])
                    cs = []
                    for g2 in range(2):
                        S.op("dve", lambda e, g2=g2: e.bn_stats(out=bnst[:, g2, :], in_=gv[:, g2 * 256:(g2 + 1) * 256]),
                             reads=[gres, ("tok", G_TMP, 3)], writes=[("bnst", g2)])
                        S.op("dve", lambda e, g2=g2: e.bn_aggr(out=bnag[:, g2, :], in_=bnst[:, g2, :]),
                             reads=[("bnst", g2)], writes=[("bnag", g2)])
                        c1 = scol()
                        c2 = scol()
                        c3 = scol()
                        S.op("dve", lambda e, g2=g2, c1=c1, c3=c3: e.tensor_scalar(out=stat[:, c1:c1 + 1], in0=bnag[:, g2, 1:2], scalar1=EPS, scalar2=None, op0=ALU.add),
                             reads=[("bnag", g2)], writes=[("stat", c1)])
                        S.op("dve", lambda e, g2=g2, c3=c3: e.tensor_copy(out=stat[:, c3:c3 + 1], in_=bnag[:, g2, 0:1]),
                             reads=[("bnag", g2)], writes=[("stat", c3)])
                        S.op("pool", lambda e, c1=c1, c2=c2: e.tensor_tensor(out=stat[:, c2:c2 + 1], in0=stat[:, c1:c1 + 1], in1=mhalf[:], op=ALU.pow),
                             reads=[("stat", c1), "mhalf"], writes=[("stat", c2)])
                        cs.append((c2, c3))
                    for g2 in range(2):
                        c2, c3 = cs[g2]
                        A("dve", lambda e, g2=g2, c2=c2, c3=c3: e.tensor_scalar(out=gv[:, g2 * 256:(g2 + 1) * 256], in0=gv[:, g2 * 256:(g2 + 1) * 256],
                                                                             scalar1=stat[:, c3:c3 + 1], scalar2=stat[:, c2:c2 + 1],
                                                                             op0=ALU.subtract, op1=ALU.mult),
                          reads=[gres, ("stat", c3), ("stat", c2)], writes=[gres], alias=[(G_TMP, 3, 4, True)])
                    A("dve", lambda e: e.tensor_tensor(out=gv, in0=gv, in1=lng[:, half * 512:(half + 1) * 512], op=ALU.mult),
                      reads=[gres, "lng"], writes=[gres], alias=[(G_TMP, 3, 4, True)])
                    A("dve", lambda e: e.tensor_tensor(out=vn[:, s, half * 512:(half + 1) * 512], in0=gv, in1=lnb[:, half * 512:(half + 1) * 512], op=ALU.add),
                      reads=[gres, "lnb", ("tok", G_TMP, 3)], writes=[("vn", s, half)], alias=[(G_BIG, 0, 2, True)])
                return ev

            def mk_ev_sig(dst, name, half):
                def ev(c4, pb):
                    ch = half * 4 + c4
                    S.op("act", lambda e: e.activation(out=dst[:, ch, :], in_=bank(pb), func=AF.Sigmoid),
                         reads=[bres(pb)], writes=[(name, ch)])
                return ev

            gate_logits(0)
            gate_logits(1)
            S.phase(3)
            for it in fm_items(ev_q):
                it()
            S.phase(2)
            gate_cumsum(0)
            gate_cumsum(1)
            gate_logits(2)
            gate_logits(3)
            S.phase(3)
            for it in tm_items(mk_ev_v(0)):
                it()
            S.phase(2)
            gate_cumsum(2)
            gate_cumsum(3)
            for s in range(4):
                for h in range(4):
                    cc = (s * 4 + h) * 2
                    S.op("pe", lambda e, s=s, h=h, cc=cc: e.matmul(bank(2)[:, cc:cc + 2], lhsT=loga[:, s, h * 128:(h + 1) * 128], rhs=ind[:],
                                                                    start=True, stop=True),
                         reads=[("loga", s), "ind", ("tok", G_GATE, 0)], writes=[bres(2)])
            S.op("act", lambda e: e.activation(out=dec[:], in_=bank(2)[:, 0:32], func=AF.Exp), reads=[bres(2)], writes=["dec"])
            S.phase(3)
            for it in tm_items(mk_ev_v(1)):
                it()
            for it in tm_items(ev_k):
                it()

            S.phase(4)
            from collections import deque
            filler = deque()
            filler.extend(tm_items(mk_ev_r(0)))
            filler.extend(tm_items(mk_ev_r(1)))
            filler.extend(fm_items(mk_ev_su(0)))
            filler.extend(fm_items(mk_ev_su(1)))
            filler.extend(tm_items(mk_ev_sv(0)))
            filler.extend(tm_items(mk_ev_sv(1)))
            filler.extend(fm_items(mk_ev_sig(sgg, "sgg", 0)))
            filler.extend(fm_items(mk_ev_sig(sgg, "sgg", 1)))
            filler.extend(fm_items(mk_ev_sig(sgs, "sgs", 0)))
            filler.extend(fm_items(mk_ev_sig(sgs, "sgs", 1)))

            def pull(n):
                for _ in range(n):
                    if filler:
                        filler.popleft()()

            gla_on = S.upto >= 5
            if KMODE == 1:
                pull(len(filler))
            UQ = []
            OPS = []
            if gla_on:
                fence(UQ + [bres(0), bres(1)])
                fence(OPS + [bres(2), bres(3)])
            pending = deque()
            for s in range(4):
                opv = PP[1][:].rearrange("p a b -> p (a b)")
                for j in range(2):
                    par = j
                    rows = slice(64 * j, 64 * (j + 1))
                    if gla_on:
                        upds = []
                        for h in range(4):
                            ub = h // 2
                            uoff = (h % 2) * 256
                            upd = bank(ub)[:, uoff:uoff + 256]
                            upds.append((upd, bres(ub)))
                            A("pe", lambda e, s=s, h=h, rows=rows, upd=upd: e.matmul(
                                upd, lhsT=kdec[rows, s, h * 128:(h + 1) * 128], rhs=vtm[rows, s, h * 256:(h + 1) * 256], start=True, stop=True),
                              reads=[("kdec", s), ("vtm", s, h // 2)], writes=[bres(ub)], alias=[(G_QK, 0, 2, False), (G_BIG, 0, 2, False)])
                        for h in range(4):
                            upd, ures = upds[h]
                            dcol = (s * 4 + h) * 2 + j
                            S.op("dve", lambda e, h=h, upd=upd, dcol=dcol: e.scalar_tensor_tensor(
                                out=state[:, h, :], in0=state[:, h, :], scalar=dec[:, dcol:dcol + 1], in1=upd, op0=ALU.mult, op1=ALU.add),
                                reads=[("state", h), "dec", ures], writes=[("state", h)])
                            S.op("pool", lambda e, h=h, par=par: e.tensor_copy(out=stbf[:, par, h, :], in_=state[:, h, :]),
                                 reads=[("state", h)], writes=[("stbf", par, h)])
                    if KMODE == 0:
                        pull(4)
                    if j == 0 and pending:
                        pending.popleft()()
                    for h in range(4):
                        if not gla_on:
                            break
                        A("pe", lambda e, s=s, h=h, j=j, par=par, rows=rows: e.matmul(
                            PP[1][rows, :, :].rearrange("p a b -> p (a b)")[:, h * 256:(h + 1) * 256],
                            lhsT=qT[:, h, s * 128 + 64 * j:s * 128 + 64 * (j + 1)], rhs=stbf[:, par, h, :], start=True, stop=True),
                          reads=[("qT", h), ("stbf", par, h)], writes=[bres(2 + h // 2)], alias=[(G_QK, 0, 2, False)])
                    if KMODE == 2:
                        pull(4)
                if not gla_on:
                    continue
                yb = ytm[:, s % 2, :]
                c0s = []
                for h in range(4):
                    c0 = scol()
                    c0s.append(c0)
                    S.op("act", lambda e, h=h, c0=c0, opv=opv: e.activation(out=junk[:, h * 256:(h + 1) * 256], in_=opv[:, h * 256:(h + 1) * 256],
                                                               func=AF.Square, accum_out=stat[:, c0:c0 + 1]),
                         reads=[bres(2 + h // 2)], writes=[("stat", c0)])
                for h in range(4):
                    c2, r2 = rstd_from(c0s[h], 256, ("stat", c0s[h]))
                    S.op("dve", lambda e, h=h, c2=c2, yb=yb, opv=opv: e.scalar_tensor_tensor(
                        out=yb[:, h * 256:(h + 1) * 256], in0=opv[:, h * 256:(h + 1) * 256], scalar=stat[:, c2:c2 + 1],
                        in1=gnb[:, h * 256:(h + 1) * 256], op0=ALU.mult, op1=ALU.mult),
                        reads=[bres(2 + h // 2), r2, "gnb"], writes=[("ytm", s % 2)])
                S.op("dve", lambda e, s=s, yb=yb: e.tensor_tensor(out=ygbs[s % 2][:], in0=yb, in1=sr[:, s, :], op=ALU.mult),
                     reads=[("ytm", s % 2), ("sr", s, 0), ("sr", s, 1)], writes=[("ygb", s % 2)])
                def ytrans(s=s):
                    tb = fbank()
                    pv = bank(tb).bitcast(BF16).rearrange("p (k t) -> p k t", k=8)
                    for k in range(8):
                        S.op("pe", lambda e, k=k, pv=pv: e.transpose(out=pv[:, k, :], in_=ygbs[s % 2][:, k * 128:(k + 1) * 128], identity=ident[:]),
                             reads=[("ygb", s % 2), "ident"], writes=[bres(tb)])
                    A("act", lambda e, s=s, pv=pv: e.activation(out=yglaT[:, :, s * 128:(s + 1) * 128], in_=pv, func=AF.Copy),
                      reads=[bres(tb)], writes=[("ygla", s)], alias=[(G_GATE, 1, 2, True)])
                pending.append(ytrans)
            pull(4)
            while pending:
                pending.popleft()()
            pull(len(filler))

            S.phase(6)
            for s in range(4):
                pp = 1 + (s % 2)
                mv = PP[pp][:].rearrange("p a (c i) -> p (a c) i", c=4)
                for g in range(4):
                    for cc in range(2):
                        ch = g * 2 + cc
                        A("pe", lambda e, s=s, g=g, cc=cc, ch=ch, mv=mv: e.matmul(
                            mv[:, ch, :], lhsT=vn[:, s, g * 256 + cc * 128:g * 256 + (cc + 1) * 128], rhs=wsp[:, g, :], start=True, stop=True),
                          reads=[("vn", s, g // 2), "wsp"], writes=[bres(2 * pp + ch // 4)], alias=[(G_BIG, 0, 2, False)])
                yb4 = ytm[:, s % 2, :].rearrange("p (g c i) -> p g c i", g=4, c=2)
                S.op("dve", lambda e, mv=mv, yb4=yb4: e.tensor_tensor(
                    out=yb4, in0=mv.rearrange("p (g c) i -> p g c i", g=4),
                    in1=bspb[:].unsqueeze(2).broadcast_to([128, 4, 2, 128]), op=ALU.add),
                    reads=[bres(2 * pp), bres(2 * pp + 1), "bspb"], writes=[("ytm", s % 2)])
                A("dve", lambda e, s=s: e.tensor_tensor(out=ysguT[:, :, s * 128:(s + 1) * 128],
                                                        in0=ytm[:, s % 2, :].rearrange("p (c i) -> p c i", c=8),
                                                        in1=u[:, :, s * 128:(s + 1) * 128], op=ALU.mult),
                  reads=[("ytm", s % 2)] + [("u", ch) for ch in range(8)], writes=[("ysgu", s)],
                  alias=[(G_GATE, 1, 2, True), (G_BIG, 0, 2, False)])

            S.phase(7)
            YG = [("ygla", s) for s in range(4)]
            YS = [("ysgu", s) for s in range(4)]
            for half in range(2):
                wtg, wrg = w_acquire(wj)
                wts, wrs = w_acquire(wj + 1)
                for c4 in range(4):
                    ch = half * 4 + c4
                    pp = 1 + (ch % 3)
                    for kc in range(8):
                        A("pe", lambda e, kc=kc, c4=c4, pp=pp, wtg=wtg: e.matmul(
                            PP[pp][:, 0, :], lhsT=wtg[:, kc, c4 * 128:(c4 + 1) * 128], rhs=yglaT[:, kc, :], start=(kc == 0), stop=(kc == 7)),
                          reads=YG + [wrg], writes=[bres(2 * pp)], alias=[(G_GATE, 1, 2, False)])
                    for kc in range(8):
                        A("pe", lambda e, kc=kc, c4=c4, pp=pp, wts=wts: e.matmul(
                            PP[pp][:, 1, :], lhsT=wts[:, kc, c4 * 128:(c4 + 1) * 128], rhs=ysguT[:, kc, :], start=(kc == 0), stop=(kc == 7)),
                          reads=YS + [wrs], writes=[bres(2 * pp + 1)], alias=[(G_GATE, 1, 2, False)])
                    t1 = rtmp[:, (ch % 2) * 2, :]
                    t2 = rtmp[:, (ch % 2) * 2 + 1, :]
                    A("dve", lambda e, ch=ch, pp=pp, t1=t1: e.tensor_tensor(out=t1, in0=PP[pp][:, 0, :], in1=sgg[:, ch, :], op=ALU.mult),
                      reads=[bres(2 * pp), ("sgg", ch)], writes=[("t1", ch % 2)], alias=[(G_TMP, 1, 4, True)])
                    A("dve", lambda e, ch=ch, pp=pp, t2=t2: e.tensor_tensor(out=t2, in0=PP[pp][:, 1, :], in1=sgs[:, ch, :], op=ALU.mult),
                      reads=[bres(2 * pp + 1), ("sgs", ch)], writes=[("t2", ch % 2)], alias=[(G_TMP, 1, 4, True)])
                    A("pool", lambda e, ch=ch, t1=t1, t2=t2: e.tensor_tensor(out=mergedT[:, ch, :], in0=t1, in1=t2, op=ALU.add),
                      reads=[("t1", ch % 2), ("t2", ch % 2)], writes=[("merged", ch)], alias=[(G_TMP, 1, 4, False), (G_QK, 1, 2, True)])
                w_release(wj)
                w_release(wj + 1)
                wj += 2

            S.phase(8)
            MG = [("merged", ch) for ch in range(8)]
            wt0, wr0 = w_acquire(wj)
            wt1, wr1 = w_acquire(wj + 1)
            for s in range(4):
                pp = 1 + (s % 3)
                for half, (wt, wr) in enumerate(((wt0, wr0), (wt1, wr1))):
                    for kc in range(8):
                        A("pe", lambda e, kc=kc, s=s, pp=pp, half=half, wt=wt: e.matmul(
                            PP[pp][:, half, :], lhsT=mergedT[:, kc, s * 128:(s + 1) * 128], rhs=wt[:, kc, :], start=(kc == 0), stop=(kc == 7)),
                          reads=MG + [wr], writes=[bres(2 * pp + half)], alias=[(G_QK, 1, 2, False)])
                postnorm_residual(s, pp, gqm, "gqm", False, t)
                S.phase(9)
                prenorm_compute(s, gpf, "gpf")
                if s >= 1:
                    prenorm_pe(s - 1, (s - 1) % 2)
                S.phase(8)
            w_release(wj)
            w_release(wj + 1)
            wj += 2

            S.phase(9)
            prenorm_pe(3, 1)

            S.phase(10)
            fc = 0
            for tt in range(6):
                wtg, wrg = w_acquire(wj)
                wtu, wru = w_acquire(wj + 1)
                for c4 in range(4 if tt < 5 else 2):
                    pp = 1 + (fc % 3)
                    for kc in range(8):
                        S.op("pe", lambda e, kc=kc, c4=c4, pp=pp, wtg=wtg: e.matmul(
                            PP[pp][:, 0, :], lhsT=wtg[:, kc, c4 * 128:(c4 + 1) * 128], rhs=aT[:, kc, :], start=(kc == 0), stop=(kc == 7)),
                            reads=AT_ALL + [wrg], writes=[bres(2 * pp)])
                    for kc in range(8):
                        S.op("pe", lambda e, kc=kc, c4=c4, pp=pp, wtu=wtu: e.matmul(
                            PP[pp][:, 1, :], lhsT=wtu[:, kc, c4 * 128:(c4 + 1) * 128], rhs=aT[:, kc, :], start=(kc == 0), stop=(kc == 7)),
                            reads=AT_ALL + [wru], writes=[bres(2 * pp + 1)])
                    sgt = rtmp[:, fc % 2, :]
                    A("act", lambda e, pp=pp, sgt=sgt: e.activation(out=sgt, in_=PP[pp][:, 0, :], func=AF.Silu),
                      reads=[bres(2 * pp)], writes=[("sgt", fc % 2)], alias=[(G_TMP, 2, 4, True)])
                    A("dve", lambda e, pp=pp, sgt=sgt, fc=fc: e.tensor_tensor(out=actT[:, fc, :], in0=PP[pp][:, 1, :], in1=sgt, op=ALU.mult),
                      reads=[bres(2 * pp + 1), ("sgt", fc % 2)], writes=[("act", fc)], alias=[(G_TMP, 2, 4, False), (G_BIG, 1, 2, True)])
                    fc += 1
                w_release(wj)
                w_release(wj + 1)
                wj += 2

            S.phase(11)
            for half in range(2):
                for kg, nk in enumerate((8, 8, 6)):
                    wt, wr = w_acquire(wj)
                    for s in range(4):
                        for k in range(nk):
                            fcc = kg * 8 + k
                            A("pe", lambda e, s=s, k=k, fcc=fcc, half=half, wt=wt: e.matmul(
                                PP[s][:, half, :], lhsT=actT[:, fcc, s * 128:(s + 1) * 128], rhs=wt[:, k, :],
                                start=(fcc == 0), stop=(fcc == 21)),
                              reads=[("act", fcc), wr], writes=[bres(2 * s + half)], alias=[(G_BIG, 1, 2, False)])
                    w_release(wj)
                    wj += 1
            for s in range(4):
                postnorm_residual(s, s, gqf, "gqf", True, t)

        S.emit()
    return nc, S


_CACHE = {}


def _prep_consts(inp):
    f = lambda a: np.ascontiguousarray(np.asarray(a, dtype=np.float32))
    return {
        "w_in": f(inp["w_in"][0]),
        "w_bg": f(inp["w_branch_gla"][0]),
        "w_bs": f(inp["w_branch_sgu"][0]),
        "w_out": f(inp["w_out"][0]),
        "w_fi": f(inp["w_ffn_in"][0]),
        "w_fo": f(inp["w_ffn_out"][0]),
        "g_pm": f(inp["norm_pre_mix"][0]),
        "g_pf": f(inp["norm_pre_ffn"][0]),
        "g_qm": f(inp["norm_post_mix"][0]),
        "g_qf": f(inp["norm_post_ffn"][0]),
        "b_gate": f(inp["b_gate"][0]),
        "gla_norm": f(np.asarray(inp["gla_norm"][0]).reshape(-1)),
        "ln_g": f(np.asarray(inp["sgu_ln_g"][0]).reshape(-1)),
        "ln_b": f(np.asarray(inp["sgu_ln_b"][0]).reshape(-1)),
        "w_gu": f(inp["w_gate_up"][0]),
        "wspT": f(np.transpose(np.asarray(inp["w_spatial"][0]), (2, 0, 1))),
        "b_sp": f(np.asarray(inp["b_spatial"][0])),
    }


def kernel(**inputs):
    x = np.asarray(inputs["x"], dtype=np.float32)
    B, S_tok, _ = x.shape
    key = S_tok
    if key not in _CACHE:
        _CACHE[key] = build(S_tok)[0]
    nc = _CACHE[key]
    consts = _prep_consts(inputs)
    in_maps = []
    for b in range(B):
        m = dict(consts)
        m["x"] = np.ascontiguousarray(x[b])
        in_maps.append(m)
    res = run_bass_kernel_spmd(nc, in_maps, core_ids=list(range(B)))
    return np.stack([np.asarray(r["out"]) for r in res.results], axis=0).astype(np.float32)


def _simulate(S):
    ops = S.ops
    engs = ("pe", "act", "dve", "pool", "sp")
    streams = {e: [i for i, o in enumerate(ops) if o.eng == e] for e in engs}
    pos = {e: 0 for e in engs}
    done = set()
    progress = True
    while progress:
        progress = False
        for e in engs:
            while pos[e] < len(streams[e]):
                i = streams[e][pos[e]]
                if all(d in done for (_k, d) in ops[i].waits):
                    done.add(i)
                    pos[e] += 1
                    progress = True
                else:
                    break
    stuck = {e: (pos[e], len(streams[e])) for e in engs if pos[e] < len(streams[e])}
    return stuck
```

```python
import contextlib
import numpy as np
import concourse.bass as bass
import concourse.mybir as mybir
from concourse.bass_utils import run_bass_kernel_spmd

F32 = mybir.dt.float32
BF16 = mybir.dt.bfloat16
AF = mybir.ActivationFunctionType
ALU = mybir.AluOpType

D = 1024
DIN = 7184
DFF = 2816
T = 512
NB = 4
import os
KMODE = int(os.environ.get("KMODE", "0"))
STRICT = int(os.environ.get("STRICT", "0"))
EPS = 1e-6


class _Op:
    __slots__ = ("eng", "fn", "deps", "chan", "seq", "waits")

    def __init__(self, eng, fn, chan):
        self.eng = eng
        self.fn = fn
        self.chan = chan
        self.deps = {}
        self.seq = 0
        self.waits = []


class Sched:
    COMPUTE = ("pe", "act", "dve", "pool")

    def __init__(self, nc):
        self.nc = nc
        self.ops = []
        self.last_w = {}
        self.readers = {}
        self.upto = 99
        self.stopped = False

    def phase(self, k):
        self.stopped = k > self.upto

    def op(self, eng, fn, reads=(), writes=(), chan=None, force=False):
        if self.stopped and not force:
            return -1
        idx = len(self.ops)
        o = _Op(eng, fn, chan)
        for r in reads:
            w = self.last_w.get(r)
            if w is not None:
                o.deps[w] = True
        for r in writes:
            w = self.last_w.get(r)
            if w is not None and w not in o.deps:
                o.deps[w] = False
            for rd in self.readers.get(r, ()):
                if rd not in o.deps:
                    o.deps[rd] = False
        for r in reads:
            self.readers.setdefault(r, []).append(idx)
        for r in writes:
            self.last_w[r] = idx
            self.readers[r] = []
        self.ops.append(o)
        return idx

    def emit(self):
        nc = self.nc
        ops = self.ops
        engs = ("pe", "act", "dve", "pool", "sp")
        streams = {e: [] for e in engs}
        for i, o in enumerate(ops):
            streams[o.eng].append(i)
            o.seq = len(streams[o.eng])
        chan_cnt = {}
        dma_val = {}
        for i, o in enumerate(ops):
            if o.chan is not None:
                chan_cnt[o.chan] = chan_cnt.get(o.chan, 0) + 16
                dma_val[i] = chan_cnt[o.chan]
        need_sig = set()
        for e in engs:
            waited = {}
            for i in streams[e]:
                o = ops[i]
                for d, is_raw in sorted(o.deps.items()):
                    p = ops[d]
                    if p.chan is not None:
                        key = ("dma", p.chan)
                        val = dma_val[d]
                        if p.chan == "const":
                            val = chan_cnt[p.chan]
                            dma_val[d] = val
                    else:
                        if p.eng == e and o.chan is None:
                            if e == "pe" or (not is_raw and not STRICT):
                                continue
                        key = ("eng", p.eng)
                        val = p.seq
                    if waited.get(key, 0) >= val:
                        continue
                    waited[key] = val
                    o.waits.append((key, d))
                    if p.chan is None:
                        need_sig.add(d)
        rank = {}
        for e in self.COMPUTE:
            n = 0
            for i in streams[e]:
                if i in need_sig:
                    n += 1
                    rank[i] = n
        self.stats = {e: len(streams[e]) for e in engs}
        with contextlib.ExitStack() as st:
            esem = {e: st.enter_context(nc.semaphore("s_" + e)) for e in self.COMPUTE}
            csem = {}
            for n_, c in enumerate(chan_cnt):
                csem[c] = st.enter_context(nc.semaphore("c%d" % n_))
            block = st.enter_context(nc.Block())

            def run_stream(e, engobj):
                for i in streams[e]:
                    o = ops[i]
                    for key, d in o.waits:
                        if key[0] == "dma":
                            engobj.wait_ge(csem[key[1]], dma_val[d])
                        else:
                            engobj.wait_ge(esem[key[1]], rank[d])
                    ins = o.fn(engobj)
                    if o.chan is not None:
                        ins.then_inc(csem[o.chan], 16)
                    elif i in need_sig:
                        ins.then_inc(esem[e], 1)
                if e == "sp":
                    for c, v in chan_cnt.items():
                        engobj.wait_ge(csem[c], v)

            @block.tensor
            def _(eng):
                run_stream("pe", eng)

            @block.scalar
            def _(eng):
                run_stream("act", eng)

            @block.vector
            def _(eng):
                run_stream("dve", eng)

            @block.gpsimd
            def _(eng):
                run_stream("pool", eng)

            @block.sync
            def _(eng):
                run_stream("sp", eng)


def weight_tiles():
    tl = []
    tl.append(("w_in", 0, 8, 3072, 16))
    tl.append(("w_in", 0, 8, 0, 512))
    tl.append(("w_in", 0, 8, 1024, 512))
    tl.append(("w_in", 0, 8, 1536, 512))
    tl.append(("w_in", 0, 8, 512, 512))
    tl.append(("w_in", 0, 8, 2048, 512))
    tl.append(("w_in", 0, 8, 2560, 512))
    tl.append(("w_in", 0, 8, 3088, 512))
    tl.append(("w_in", 0, 8, 3600, 512))
    tl.append(("w_in", 0, 8, 4112, 512))
    tl.append(("w_in", 0, 8, 4624, 512))
    tl.append(("w_in", 0, 8, 5136, 512))
    tl.append(("w_in", 0, 8, 5648, 512))
    tl.append(("w_in", 0, 8, 6160, 512))
    tl.append(("w_in", 0, 8, 6672, 512))
    for half in range(2):
        tl.append(("w_bg", 0, 8, half * 512, 512))
        tl.append(("w_bs", 0, 8, half * 512, 512))
    tl.append(("w_out", 0, 8, 0, 512))
    tl.append(("w_out", 0, 8, 512, 512))
    for t in range(6):
        ncw = 512 if t < 5 else 256
        tl.append(("w_fi", 0, 8, t * 512, ncw))
        tl.append(("w_fi", 0, 8, DFF + t * 512, ncw))
    for half in range(2):
        for kg, nk in enumerate((8, 8, 6)):
            tl.append(("w_fo", kg * 8 * 128, nk, half * 512, 512))
    return tl


def build(S_tok, upto=99):
    NT = S_tok // T
    nc = bass.Bass("TRN2", target_bir_lowering=False)

    def din(name, shape, dt=F32):
        return nc.dram_tensor(name, shape, dt, kind="ExternalInput").ap()

    x = din("x", [S_tok, D])
    wsrc = {
        "w_in": din("w_in", [D, DIN]),
        "w_bg": din("w_bg", [D, D]),
        "w_bs": din("w_bs", [D, D]),
        "w_out": din("w_out", [D, D]),
        "w_fi": din("w_fi", [D, 2 * DFF]),
        "w_fo": din("w_fo", [DFF, D]),
    }
    wscr = {k: nc.dram_tensor(k + "_bf", list(v.shape), BF16, kind="Internal").ap() for k, v in wsrc.items()}
    c_gpm = din("g_pm", [D])
    c_gpf = din("g_pf", [D])
    c_gqm = din("g_qm", [D])
    c_gqf = din("g_qf", [D])
    c_bg = din("b_gate", [512])
    c_gn = din("gla_norm", [D])
    c_lng = din("ln_g", [D])
    c_lnb = din("ln_b", [D])
    c_wgu = din("w_gu", [16, 512])
    c_wsp = din("wspT", [128, 4, 128])
    c_bsp = din("b_sp", [4, 128])
    out = nc.dram_tensor("out", [S_tok, D], F32, kind="ExternalOutput").ap()

    S = Sched(nc)
    S.upto = upto
    with contextlib.ExitStack() as st:
        def sb(name, shape, dt):
            return st.enter_context(nc.sbuf_tensor(name, shape, dt))

        xt = sb("xt", [128, 4, D], F32)
        aT = sb("aT", [128, 8, T], BF16)
        xs = sb("xs", [128, D], BF16)
        xsB = sb("xsB", [128, D], BF16)
        xs2 = [xs, xsB]
        sr = sb("sr", [128, 4, D], BF16)
        alT = sb("alT", [16, T], F32)
        state = sb("state", [128, 4, 256], F32)
        stbf = sb("stbf", [128, 2, 4, 256], BF16)
        sgg = sb("sgg", [128, 8, T], BF16)
        sgs = sb("sgs", [128, 8, T], BF16)
        ytm = sb("ytm", [128, 2, D], F32)
        ygb = sb("ygb", [128, D], BF16)
        ygb2 = sb("ygb2", [128, D], BF16)
        ygbs = [ygb, ygb2]
        big = sb("big", [128, 24, T], BF16)
        rgate = sb("rgate", [128, 8192], BF16)
        rqk = sb("rqk", [128, 8, T], BF16)
        rtmp = sb("rtmp", [128, 4, T], F32)
        stat = sb("stat", [128, 64], F32)
        dec = sb("dec", [128, 32], F32)
        bnst = sb("bnst", [128, 4, 6], F32)
        bnag = sb("bnag", [128, 4, 2], F32)
        wring = [sb("wr%d" % i, [128, 8, 512], BF16) for i in range(NB)]
        gpm = sb("gpm", [128, D], F32)
        gpf = sb("gpf", [128, D], F32)
        gqm = sb("gqm", [128, D], F32)
        gqf = sb("gqf", [128, D], F32)
        gnb = sb("gnb", [128, D], F32)
        lng = sb("lng", [128, D], F32)
        lnb = sb("lnb", [128, D], F32)
        bgb = sb("bgb", [128, 512], F32)
        wgu = sb("wgu", [16, 512], F32)
        wspf = sb("wspf", [128, 4, 128], F32)
        wsp = sb("wsp", [128, 4, 128], BF16)
        bspb = sb("bspb", [128, 4, 128], F32)
        identf = sb("identf", [128, 128], F32)
        ident = sb("ident", [128, 128], BF16)
        mtri = sb("mtri", [128, 128], F32)
        ind = sb("ind", [128, 2], F32)
        mhalf = sb("mhalf", [128, 1], F32)
        junk = sb("junk", [128, D], BF16)
        fz = sb("fz", [128, 1], F32)
        PP = [st.enter_context(nc.psum_tensor("pp%d" % i, [128, 2, 512], F32)) for i in range(4)]

        def bank(b):
            return PP[b // 2][:, b % 2, :]

        def bres(b):
            return ("ps", b)

        vtm = big[:, 0:8, :].rearrange("p (s a) t -> p s (a t)", s=4)
        vn = big[:, 8:16, :].rearrange("p (s a) t -> p s (a t)", s=4)
        u = big[:, 16:24, :]
        actT = big
        loga = rgate[:, 0:4096].bitcast(F32).rearrange("p (s e) -> p s e", s=4)
        edec = rgate[:, 4096:8192].bitcast(F32).rearrange("p (s e) -> p s e", s=4)
        yglaT = rgate[:, 0:4096].rearrange("p (c t) -> p c t", c=8)
        ysguT = rgate[:, 4096:8192].rearrange("p (c t) -> p c t", c=8)
        qT = rqk[:, 0:4, :]
        kdec = rqk[:, 4:8, :]
        mergedT = rqk
        lg = rtmp[:, 0, :]
        e1 = rtmp[:, 1, :]

        def al(group, side, nsides, write):
            r = [("tok", group, side)]
            w = [("tok", group, j) for j in range(nsides) if j != side] if write else []
            return r, w

        def A(eng, fn, reads=(), writes=(), alias=(), chan=None):
            rr = list(reads)
            ww = list(writes)
            for (g, sd, n, wr) in alias:
                r_, w_ = al(g, sd, n, wr)
                rr += r_
                ww += w_
            return S.op(eng, fn, rr, ww, chan)

        G_BIG, G_GATE, G_QK, G_TMP = "big", "gate", "qk", "tmp"

        tiles = weight_tiles()
        NW = len(tiles)
        cstate = {"n": 0}

        def emit_casts(upto_):
            while cstate["n"] < min(NW, upto_):
                i = cstate["n"]
                wn, r0, nk, c0, ncw = tiles[i]
                S.op("pool", lambda e, wn=wn, r0=r0, nk=nk, c0=c0, ncw=ncw:
                     e.dma_start(out=wscr[wn][r0:r0 + nk * 128, c0:c0 + ncw], in_=wsrc[wn][r0:r0 + nk * 128, c0:c0 + ncw]),
                     writes=[("scr", i)], chan=("cast", i))
                cstate["n"] += 1

        def cload(dst, src, res):
            S.op("sp", lambda e: e.dma_start(out=dst, in_=src), writes=[res], chan="const")

        cload(gpm[:], c_gpm.partition_broadcast(128), "gpm")
        cload(gpf[:], c_gpf.partition_broadcast(128), "gpf")
        cload(gqm[:], c_gqm.partition_broadcast(128), "gqm")
        cload(gqf[:], c_gqf.partition_broadcast(128), "gqf")
        cload(gnb[:], c_gn.partition_broadcast(128), "gnb")
        cload(lng[:], c_lng.partition_broadcast(128), "lng")
        cload(lnb[:], c_lnb.partition_broadcast(128), "lnb")
        cload(bgb[:], c_bg.partition_broadcast(128), "bgb")
        cload(wgu[:], c_wgu, "wgu")
        cload(wspf[:], c_wsp, "wspf")
        cload(bspb[:], c_bsp.partition_broadcast(128), "bspb")

        S.op("pool", lambda e: e.memset(identf[:], 0.0), writes=["identf"])
        S.op("pool", lambda e: e.affine_select(out=identf[:], in_=identf[:], pattern=[[-1, 128]],
                                               compare_op=ALU.not_equal, fill=1.0, base=0, channel_multiplier=1),
             reads=["identf"], writes=["identf"])
        S.op("dve", lambda e: e.tensor_copy(out=ident[:], in_=identf[:]), reads=["identf"], writes=["ident"])
        S.op("pool", lambda e: e.memset(mhalf[:], -0.5), writes=["mhalf"])
        emit_casts(8)
        S.op("pool", lambda e: e.memset(mtri[:], -1.0 / 16), writes=["mtri"])
        S.op("pool", lambda e: e.affine_select(out=mtri[:], in_=mtri[:], pattern=[[-1, 128]],
                                               compare_op=ALU.is_gt, fill=0.0, base=0, channel_multiplier=1),
             reads=["mtri"], writes=["mtri"])
        S.op("pool", lambda e: e.memset(mtri[64:128, 0:64], 0.0), reads=["mtri"], writes=["mtri"])
        S.op("pool", lambda e: e.memset(ind[:], 0.0), writes=["ind"])
        S.op("pool", lambda e: e.memset(ind[0:64, 0:1], -1.0 / 16), reads=["ind"], writes=["ind"])
        S.op("pool", lambda e: e.memset(ind[64:128, 1:2], -1.0 / 16), reads=["ind"], writes=["ind"])
        S.op("pool", lambda e: e.memset(state[:], 0.0), writes=[("state", h) for h in range(4)])
        S.op("pool", lambda e: e.memset(wspf[64:128, :, 0:64], 0.0), reads=["wspf"], writes=["wspf"])
        S.op("dve", lambda e: e.tensor_copy(out=wsp[:], in_=wspf[:]), reads=["wspf"], writes=["wsp"])

        total_w = NT * NW
        wstate = {"loaded": 0}

        def emit_load(j):
            i = j % NW
            wn, r0, nk, c0, ncw = tiles[i]
            slot = j % NB
            S.op("sp", lambda e: e.dma_start(
                out=wring[slot][:, 0:nk, 0:ncw],
                in_=wscr[wn][r0:r0 + nk * 128, c0:c0 + ncw].rearrange("(k p) e -> p k e", p=128)),
                reads=[("scr", i)], writes=[("w", slot)], chan=("w", slot))

        def w_acquire(j):
            if j < NW:
                emit_casts(j + 10)
            while wstate["loaded"] <= j and wstate["loaded"] < total_w:
                assert wstate["loaded"] < j + NB
                emit_load(wstate["loaded"])
                wstate["loaded"] += 1
            return wring[j % NB], ("w", j % NB)

        def w_release(j):
            nxt = j + NB
            if nxt < total_w and wstate["loaded"] == nxt:
                emit_load(nxt)
                wstate["loaded"] += 1


        scnt = [0]

        def scol():
            c = scnt[0] % 64
            scnt[0] += 1
            return c

        def rstd_from(ss_col, n, res_in):
            c1 = scol()
            c2 = scol()
            S.op("dve", lambda e: e.tensor_scalar(out=stat[:, c1:c1 + 1], in0=stat[:, ss_col:ss_col + 1],
                                                  scalar1=1.0 / n, scalar2=EPS, op0=ALU.mult, op1=ALU.add),
                 reads=[res_in], writes=[("stat", c1)])
            S.op("pool", lambda e: e.tensor_tensor(out=stat[:, c2:c2 + 1], in0=stat[:, c1:c1 + 1], in1=mhalf[:], op=ALU.pow),
                 reads=[("stat", c1), "mhalf"], writes=[("stat", c2)])
            return c2, ("stat", c2)

        def prenorm_compute(s, gtile, gres):
            c0 = scol()
            xsb = xs2[s % 2]
            S.op("act", lambda e: e.activation(out=junk[:], in_=xt[:, s, :], func=AF.Square, accum_out=stat[:, c0:c0 + 1]),
                 reads=[("xt", s)], writes=[("stat", c0)])
            c2, r2 = rstd_from(c0, D, ("stat", c0))
            S.op("dve", lambda e: e.scalar_tensor_tensor(out=xsb[:], in0=xt[:, s, :], scalar=stat[:, c2:c2 + 1], in1=gtile[:],
                                                         op0=ALU.mult, op1=ALU.mult),
                 reads=[("xt", s), r2, gres], writes=[("xs", s % 2)])

        def prenorm_pe(s, tpb):
            xsb = xs2[s % 2]
            pv = bank(tpb).bitcast(BF16).rearrange("p (k t) -> p k t", k=8)
            for k in range(8):
                S.op("pe", lambda e, k=k: e.transpose(out=pv[:, k, :], in_=xsb[:, k * 128:(k + 1) * 128], identity=ident[:]),
                     reads=[("xs", s % 2), "ident"], writes=[bres(tpb)])
            S.op("act", lambda e: e.activation(out=aT[:, :, s * 128:(s + 1) * 128], in_=pv, func=AF.Copy),
                 reads=[bres(tpb)], writes=[("aT", s)])

        def prenorm_transposes(s, gtile, gres, tpb):
            prenorm_compute(s, gtile, gres)
            prenorm_pe(s, tpb)

        def postnorm_residual(s, pp, gtile, gres, final, t):
            src = PP[pp][:].rearrange("p a b -> p (a b)")
            c0 = scol()
            yb = ytm[:, s % 2, :]
            S.op("act", lambda e: e.activation(out=junk[:], in_=src, func=AF.Square, accum_out=stat[:, c0:c0 + 1]),
                 reads=[bres(2 * pp), bres(2 * pp + 1)], writes=[("stat", c0)])
            c2, r2 = rstd_from(c0, D, ("stat", c0))
            S.op("dve", lambda e: e.scalar_tensor_tensor(out=yb, in0=src, scalar=stat[:, c2:c2 + 1], in1=gtile[:],
                                                         op0=ALU.mult, op1=ALU.mult),
                 reads=[bres(2 * pp), bres(2 * pp + 1), r2, gres], writes=[("ytm", s % 2)])
            S.op("dve", lambda e: e.tensor_tensor(out=xt[:, s, :], in0=xt[:, s, :], in1=yb, op=ALU.add),
                 reads=[("xt", s), ("ytm", s % 2)], writes=[("xt", s)])
            if final:
                S.op("sp", lambda e: e.dma_start(out=out[t * T + s * 128:t * T + (s + 1) * 128, :], in_=xt[:, s, :]),
                     reads=[("xt", s)], chan=("st", s), force=True)

        def fence(res):
            S.op("dve", lambda e: e.memset(fz[:], 0.0), writes=list(res) + ["fz"])

        XT_ALL = [("xt", s) for s in range(4)]
        AT_ALL = [("aT", s) for s in range(4)]

        for t in range(NT):
            wj = t * NW
            S.phase(0)
            for s_ in range(4):
                S.op("sp", lambda e, t=t, s_=s_: e.dma_start(out=xt[:, s_, :], in_=x[t * T + s_ * 128:t * T + (s_ + 1) * 128, :]),
                     writes=[("xt", s_)], chan=("xld", s_))
            if t == 0:
                for j in range(min(NB, total_w)):
                    emit_load(j)
                    wstate["loaded"] += 1
            S.phase(1)
            for s in range(4):
                prenorm_transposes(s, gpm, "gpm", s % 2)

            S.phase(2)
            fb = [0]

            def fbank():
                b = 4 + (fb[0] % 4)
                fb[0] += 1
                return b

            def fm_items(evac):
                nonlocal wj
                j = wj
                wj += 1
                items = []
                for c4 in range(4):
                    def item(c4=c4, j=j):
                        wt, wr = w_acquire(j)
                        pb = fbank()
                        for kc in range(8):
                            S.op("pe", lambda e, kc=kc, c4=c4, pb=pb, wt=wt: e.matmul(
                                bank(pb), lhsT=wt[:, kc, c4 * 128:(c4 + 1) * 128], rhs=aT[:, kc, :],
                                start=(kc == 0), stop=(kc == 7)),
                                reads=AT_ALL + [wr], writes=[bres(pb)])
                        evac(c4, pb)
                        if c4 == 3:
                            w_release(j)
                    items.append(item)
                return items

            def tm_items(evac):
                nonlocal wj
                j = wj
                wj += 1
                items = []
                for s in range(4):
                    def item(s=s, j=j):
                        wt, wr = w_acquire(j)
                        pb = fbank()
                        for kc in range(8):
                            S.op("pe", lambda e, kc=kc, s=s, pb=pb, wt=wt: e.matmul(
                                bank(pb), lhsT=aT[:, kc, s * 128:(s + 1) * 128], rhs=wt[:, kc, :],
                                start=(kc == 0), stop=(kc == 7)),
                                reads=[("aT", s), wr], writes=[bres(pb)])
                        evac(s, pb)
                        if s == 3:
                            w_release(j)
                    items.append(item)
                return items

            wt, wr = w_acquire(wj)
            for kc in range(8):
                S.op("pe", lambda e, kc=kc, wt=wt: e.matmul(bank(2)[0:16, :], lhsT=wt[:, kc, 0:16], rhs=aT[:, kc, :],
                                                             start=(kc == 0), stop=(kc == 7)),
                     reads=AT_ALL + [wr], writes=[bres(2)])
            w_release(wj)
            wj += 1
            S.op("dve", lambda e: e.tensor_copy(out=alT[:], in_=bank(2)[0:16, :]), reads=[bres(2)], writes=["alT"])

            def gate_logits(s):
                pb = 2 + (s % 2)
                S.op("pe", lambda e, s=s, pb=pb: e.matmul(bank(pb), lhsT=alT[:, s * 128:(s + 1) * 128], rhs=wgu[:], start=True, stop=True),
                     reads=["alT", "wgu"], writes=[bres(pb)])
                lgs = rtmp[:, s % 2, :]
                e1s = rtmp[:, 2 + (s % 2), :]
                A("dve", lambda e, pb=pb, lgs=lgs: e.tensor_tensor(out=lgs, in0=bank(pb), in1=bgb[:], op=ALU.add),
                  reads=[bres(pb), "bgb"], writes=[("lg", s % 2)], alias=[(G_TMP, 0, 4, True)])
                A("act", lambda e, lgs=lgs, e1s=e1s: e.activation(out=e1s, in_=lgs, func=AF.Exp, scale=-1.0),
                  reads=[("lg", s % 2)], writes=[("e1", s % 2)], alias=[(G_TMP, 0, 4, True)])
                A("act", lambda e, s=s, e1s=e1s: e.activation(out=loga[:, s, :], in_=e1s, func=AF.Ln, bias=1.0),
                  reads=[("e1", s % 2)], writes=[("loga", s)], alias=[(G_TMP, 0, 4, False), (G_GATE, 0, 2, True)])

            def gate_cumsum(s):
                pb = 2 + (s % 2)
                S.op("pe", lambda e, s=s, pb=pb: e.matmul(bank(pb), lhsT=mtri[:], rhs=loga[:, s, :], start=True, stop=True),
                     reads=["mtri", ("loga", s), ("tok", G_GATE, 0)], writes=[bres(pb)])
                A("act", lambda e, s=s, pb=pb: e.activation(out=edec[:, s, :], in_=bank(pb), func=AF.Exp),
                  reads=[bres(pb)], writes=[("edec", s)], alias=[(G_GATE, 0, 2, True)])

            def ev_q(h, pb):
                A("act", lambda e: e.activation(out=qT[:, h, :], in_=bank(pb), func=AF.Copy, scale=float(128 ** -0.5)),
                  reads=[bres(pb)], writes=[("qT", h)], alias=[(G_QK, 0, 2, True)])

            def mk_ev_v(half):
                def ev(s, pb):
                    A("dve", lambda e: e.tensor_copy(out=vtm[:, s, half * 512:(half + 1) * 512], in_=bank(pb)),
                      reads=[bres(pb)], writes=[("vtm", s, half)], alias=[(G_BIG, 0, 2, True)])
                return ev

            def ev_k(s, pb):
                A("dve", lambda e: e.tensor_tensor(out=kdec[:, s, :], in0=bank(pb), in1=edec[:, s, :], op=ALU.mult),
                  reads=[bres(pb), ("edec", s)], writes=[("kdec", s)], alias=[(G_QK, 0, 2, True), (G_GATE, 0, 2, False)])

            def mk_ev_r(half):
                def ev(s, pb):
                    S.op("act", lambda e: e.activation(out=sr[:, s, half * 512:(half + 1) * 512], in_=bank(pb), func=AF.Silu),
                         reads=[bres(pb)], writes=[("sr", s, half)])
                return ev

            def mk_ev_su(half):
                def ev(c4, pb):
                    ch = half * 4 + c4
                    A("act", lambda e: e.activation(out=u[:, ch, :], in_=bank(pb), func=AF.Gelu_apprx_tanh),
                      reads=[bres(pb)], writes=[("u", ch)], alias=[(G_BIG, 0, 2, True)])
                return ev

            gvc = [0]

            def mk_ev_sv(half):
                def ev(s, pb):
                    gi = gvc[0] % 4
                    gvc[0] += 1
                    gv = rtmp[:, gi, :]
                    gres = ("gv", gi)
                    A("act", lambda e: e.activation(out=gv, in_=bank(pb), func=AF.Gelu_apprx_tanh),
                      reads=[bres(pb)], writes=[gres], alias=[(G_TMP, 3, 4, True)])
                    cs = []
                    for g2 in range(2):
                        S.op("dve", lambda e, g2=g2: e.bn_stats(out=bnst[:, g2, :], in_=gv[:, g2 * 256:(g2 + 1) * 256]),
                             reads=[gres, ("tok", G_TMP, 3)], writes=[("bnst", g2)])
                        S.op("dve", lambda e, g2=g2: e.bn_aggr(out=bnag[:, g2, :], in_=bnst[:, g2, :]),
                             reads=[("bnst", g2)], writes=[("bnag", g2)])
                        c1 = scol()
                        c2 = scol()
                        c3 = scol()
                        S.op("dve", lambda e, g2=g2, c1=c1, c3=c3: e.tensor_scalar(out=stat[:, c1:c1 + 1], in0=bnag[:, g2, 1:2], scalar1=EPS, scalar2=None, op0=ALU.add),
                             reads=[("bnag", g2)], writes=[("stat", c1)])
                        S.op("dve", lambda e, g2=g2, c3=c3: e.tensor_copy(out=stat[:, c3:c3 + 1], in_=bnag[:, g2, 0:1]),
                             reads=[("bnag", g2)], writes=[("stat", c3)])
                        S.op("pool", lambda e, c1=c1, c2=c2: e.tensor_tensor(out=stat[:, c2:c2 + 1], in0=stat[:, c1:c1 + 1], in1=mhalf[:], op=ALU.pow),
                             reads=[("stat", c1), "mhalf"], writes=[("stat", c2)])
                        cs.append((c2, c3))
                    for g2 in range(2):
                        c2, c3 = cs[g2]
                        A("dve", lambda e, g2=g2, c2=c2, c3=c3: e.tensor_scalar(out=gv[:, g2 * 256:(g2 + 1) * 256], in0=gv[:, g2 * 256:(g2 + 1) * 256],
                                                                             scalar1=stat[:, c3:c3 + 1], scalar2=stat[:, c2:c2 + 1],
                                                                             op0=ALU.subtract, op1=ALU.mult),
                          reads=[gres, ("stat", c3), ("stat", c2)], writes=[gres], alias=[(G_TMP, 3, 4, True)])
                    A("dve", lambda e: e.tensor_tensor(out=gv, in0=gv, in1=lng[:, half * 512:(half + 1) * 512], op=ALU.mult),
                      reads=[gres, "lng"], writes=[gres], alias=[(G_TMP, 3, 4, True)])
                    A("dve", lambda e: e.tensor_tensor(out=vn[:, s, half * 512:(half + 1) * 512], in0=gv, in1=lnb[:, half * 512:(half + 1) * 512], op=ALU.add),
                      reads=[gres, "lnb", ("tok", G_TMP, 3)], writes=[("vn", s, half)], alias=[(G_BIG, 0, 2, True)])
                return ev

            def mk_ev_sig(dst, name, half):
                def ev(c4, pb):
                    ch = half * 4 + c4
                    S.op("act", lambda e: e.activation(out=dst[:, ch, :], in_=bank(pb), func=AF.Sigmoid),
                         reads=[bres(pb)], writes=[(name, ch)])
                return ev

            gate_logits(0)
            gate_logits(1)
            S.phase(3)
            for it in fm_items(ev_q):
                it()
            S.phase(2)
            gate_cumsum(0)
            gate_cumsum(1)
            gate_logits(2)
            gate_logits(3)
            S.phase(3)
            for it in tm_items(mk_ev_v(0)):
                it()
            S.phase(2)
            gate_cumsum(2)
            gate_cumsum(3)
            for s in range(4):
                for h in range(4):
                    cc = (s * 4 + h) * 2
                    S.op("pe", lambda e, s=s, h=h, cc=cc: e.matmul(bank(2)[:, cc:cc + 2], lhsT=loga[:, s, h * 128:(h + 1) * 128], rhs=ind[:],
                                                                    start=True, stop=True),
                         reads=[("loga", s), "ind", ("tok", G_GATE, 0)], writes=[bres(2)])
            S.op("act", lambda e: e.activation(out=dec[:], in_=bank(2)[:, 0:32], func=AF.Exp), reads=[bres(2)], writes=["dec"])
            S.phase(3)
            for it in tm_items(mk_ev_v(1)):
                it()
            for it in tm_items(ev_k):
                it()

            S.phase(4)
            from collections import deque
            filler = deque()
            filler.extend(tm_items(mk_ev_r(0)))
            filler.extend(tm_items(mk_ev_r(1)))
            filler.extend(fm_items(mk_ev_su(0)))
            filler.extend(fm_items(mk_ev_su(1)))
            filler.extend(tm_items(mk_ev_sv(0)))
            filler.extend(tm_items(mk_ev_sv(1)))
            filler.extend(fm_items(mk_ev_sig(sgg, "sgg", 0)))
            filler.extend(fm_items(mk_ev_sig(sgg, "sgg", 1)))
            filler.extend(fm_items(mk_ev_sig(sgs, "sgs", 0)))
            filler.extend(fm_items(mk_ev_sig(sgs, "sgs", 1)))

            def pull(n):
                for _ in range(n):
                    if filler:
                        filler.popleft()()

            gla_on = S.upto >= 5
            if KMODE == 1:
                pull(len(filler))
            UQ = []
            OPS = []
            if gla_on:
                fence(UQ + [bres(0), bres(1)])
                fence(OPS + [bres(2), bres(3)])
            pending = deque()
            for s in range(4):
                opv = PP[1][:].rearrange("p a b -> p (a b)")
                for j in range(2):
                    par = j
                    rows = slice(64 * j, 64 * (j + 1))
                    if gla_on:
                        upds = []
                        for h in range(4):
                            ub = h // 2
                            uoff = (h % 2) * 256
                            upd = bank(ub)[:, uoff:uoff + 256]
                            upds.append((upd, bres(ub)))
                            A("pe", lambda e, s=s, h=h, rows=rows, upd=upd: e.matmul(
                                upd, lhsT=kdec[rows, s, h * 128:(h + 1) * 128], rhs=vtm[rows, s, h * 256:(h + 1) * 256], start=True, stop=True),
                              reads=[("kdec", s), ("vtm", s, h // 2)], writes=[bres(ub)], alias=[(G_QK, 0, 2, False), (G_BIG, 0, 2, False)])
                        for h in range(4):
                            upd, ures = upds[h]
                            dcol = (s * 4 + h) * 2 + j
                            S.op("dve", lambda e, h=h, upd=upd, dcol=dcol: e.scalar_tensor_tensor(
                                out=state[:, h, :], in0=state[:, h, :], scalar=dec[:, dcol:dcol + 1], in1=upd, op0=ALU.mult, op1=ALU.add),
                                reads=[("state", h), "dec", ures], writes=[("state", h)])
                            S.op("pool", lambda e, h=h, par=par: e.tensor_copy(out=stbf[:, par, h, :], in_=state[:, h, :]),
                                 reads=[("state", h)], writes=[("stbf", par, h)])
                    if KMODE == 0:
                        pull(4)
                    if j == 0 and pending:
                        pending.popleft()()
                    for h in range(4):
                        if not gla_on:
                            break
                        A("pe", lambda e, s=s, h=h, j=j, par=par, rows=rows: e.matmul(
                            PP[1][rows, :, :].rearrange("p a b -> p (a b)")[:, h * 256:(h + 1) * 256],
                            lhsT=qT[:, h, s * 128 + 64 * j:s * 128 + 64 * (j + 1)], rhs=stbf[:, par, h, :], start=True, stop=True),
                          reads=[("qT", h), ("stbf", par, h)], writes=[bres(2 + h // 2)], alias=[(G_QK, 0, 2, False)])
                    if KMODE == 2:
                        pull(4)
                if not gla_on:
                    continue
                yb = ytm[:, s % 2, :]
                c0s = []
                for h in range(4):
                    c0 = scol()
                    c0s.append(c0)
                    S.op("act", lambda e, h=h, c0=c0, opv=opv: e.activation(out=junk[:, h * 256:(h + 1) * 256], in_=opv[:, h * 256:(h + 1) * 256],
                                                               func=AF.Square, accum_out=stat[:, c0:c0 + 1]),
                         reads=[bres(2 + h // 2)], writes=[("stat", c0)])
                for h in range(4):
                    c2, r2 = rstd_from(c0s[h], 256, ("stat", c0s[h]))
                    S.op("dve", lambda e, h=h, c2=c2, yb=yb, opv=opv: e.scalar_tensor_tensor(
                        out=yb[:, h * 256:(h + 1) * 256], in0=opv[:, h * 256:(h + 1) * 256], scalar=stat[:, c2:c2 + 1],
                        in1=gnb[:, h * 256:(h + 1) * 256], op0=ALU.mult, op1=ALU.mult),
                        reads=[bres(2 + h // 2), r2, "gnb"], writes=[("ytm", s % 2)])
                S.op("dve", lambda e, s=s, yb=yb: e.tensor_tensor(out=ygbs[s % 2][:], in0=yb, in1=sr[:, s, :], op=ALU.mult),
                     reads=[("ytm", s % 2), ("sr", s, 0), ("sr", s, 1)], writes=[("ygb", s % 2)])
                def ytrans(s=s):
                    tb = fbank()
                    pv = bank(tb).bitcast(BF16).rearrange("p (k t) -> p k t", k=8)
                    for k in range(8):
                        S.op("pe", lambda e, k=k, pv=pv: e.transpose(out=pv[:, k, :], in_=ygbs[s % 2][:, k * 128:(k + 1) * 128], identity=ident[:]),
                             reads=[("ygb", s % 2), "ident"], writes=[bres(tb)])
                    A("act", lambda e, s=s, pv=pv: e.activation(out=yglaT[:, :, s * 128:(s + 1) * 128], in_=pv, func=AF.Copy),
                      reads=[bres(tb)], writes=[("ygla", s)], alias=[(G_GATE, 1, 2, True)])
                pending.append(ytrans)
            pull(4)
            while pending:
                pending.popleft()()
            pull(len(filler))

            S.phase(6)
            for s in range(4):
                pp = 1 + (s % 2)
                mv = PP[pp][:].rearrange("p a (c i) -> p (a c) i", c=4)
                for g in range(4):
                    for cc in range(2):
                        ch = g * 2 + cc
                        A("pe", lambda e, s=s, g=g, cc=cc, ch=ch, mv=mv: e.matmul(
                            mv[:, ch, :], lhsT=vn[:, s, g * 256 + cc * 128:g * 256 + (cc + 1) * 128], rhs=wsp[:, g, :], start=True, stop=True),
                          reads=[("vn", s, g // 2), "wsp"], writes=[bres(2 * pp + ch // 4)], alias=[(G_BIG, 0, 2, False)])
                yb4 = ytm[:, s % 2, :].rearrange("p (g c i) -> p g c i", g=4, c=2)
                S.op("dve", lambda e, mv=mv, yb4=yb4: e.tensor_tensor(
                    out=yb4, in0=mv.rearrange("p (g c) i -> p g c i", g=4),
                    in1=bspb[:].unsqueeze(2).broadcast_to([128, 4, 2, 128]), op=ALU.add),
                    reads=[bres(2 * pp), bres(2 * pp + 1), "bspb"], writes=[("ytm", s % 2)])
                A("dve", lambda e, s=s: e.tensor_tensor(out=ysguT[:, :, s * 128:(s + 1) * 128],
                                                        in0=ytm[:, s % 2, :].rearrange("p (c i) -> p c i", c=8),
                                                        in1=u[:, :, s * 128:(s + 1) * 128], op=ALU.mult),
                  reads=[("ytm", s % 2)] + [("u", ch) for ch in range(8)], writes=[("ysgu", s)],
                  alias=[(G_GATE, 1, 2, True), (G_BIG, 0, 2, False)])

            S.phase(7)
            YG = [("ygla", s) for s in range(4)]
            YS = [("ysgu", s) for s in range(4)]
            for half in range(2):
                wtg, wrg = w_acquire(wj)
                wts, wrs = w_acquire(wj + 1)
                for c4 in range(4):
                    ch = half * 4 + c4
                    pp = 1 + (ch % 3)
                    for kc in range(8):
                        A("pe", lambda e, kc=kc, c4=c4, pp=pp, wtg=wtg: e.matmul(
                            PP[pp][:, 0, :], lhsT=wtg[:, kc, c4 * 128:(c4 + 1) * 128], rhs=yglaT[:, kc, :], start=(kc == 0), stop=(kc == 7)),
                          reads=YG + [wrg], writes=[bres(2 * pp)], alias=[(G_GATE, 1, 2, False)])
                    for kc in range(8):
                        A("pe", lambda e, kc=kc, c4=c4, pp=pp, wts=wts: e.matmul(
                            PP[pp][:, 1, :], lhsT=wts[:, kc, c4 * 128:(c4 + 1) * 128], rhs=ysguT[:, kc, :], start=(kc == 0), stop=(kc == 7)),
                          reads=YS + [wrs], writes=[bres(2 * pp + 1)], alias=[(G_GATE, 1, 2, False)])
                    t1 = rtmp[:, (ch % 2) * 2, :]
                    t2 = rtmp[:, (ch % 2) * 2 + 1, :]
                    A("dve", lambda e, ch=ch, pp=pp, t1=t1: e.tensor_tensor(out=t1, in0=PP[pp][:, 0, :], in1=sgg[:, ch, :], op=ALU.mult),
                      reads=[bres(2 * pp), ("sgg", ch)], writes=[("t1", ch % 2)], alias=[(G_TMP, 1, 4, True)])
                    A("dve", lambda e, ch=ch, pp=pp, t2=t2: e.tensor_tensor(out=t2, in0=PP[pp][:, 1, :], in1=sgs[:, ch, :], op=ALU.mult),
                      reads=[bres(2 * pp + 1), ("sgs", ch)], writes=[("t2", ch % 2)], alias=[(G_TMP, 1, 4, True)])
                    A("pool", lambda e, ch=ch, t1=t1, t2=t2: e.tensor_tensor(out=mergedT[:, ch, :], in0=t1, in1=t2, op=ALU.add),
                      reads=[("t1", ch % 2), ("t2", ch % 2)], writes=[("merged", ch)], alias=[(G_TMP, 1, 4, False), (G_QK, 1, 2, True)])
                w_release(wj)
                w_release(wj + 1)
                wj += 2

            S.phase(8)
            MG = [("merged", ch) for ch in range(8)]
            wt0, wr0 = w_acquire(wj)
            wt1, wr1 = w_acquire(wj + 1)
            for s in range(4):
                pp = 1 + (s % 3)
                for half, (wt, wr) in enumerate(((wt0, wr0), (wt1, wr1))):
                    for kc in range(8):
                        A("pe", lambda e, kc=kc, s=s, pp=pp, half=half, wt=wt: e.matmul(
                            PP[pp][:, half, :], lhsT=mergedT[:, kc, s * 128:(s + 1) * 128], rhs=wt[:, kc, :], start=(kc == 0), stop=(kc == 7)),
                          reads=MG + [wr], writes=[bres(2 * pp + half)], alias=[(G_QK, 1, 2, False)])
                postnorm_residual(s, pp, gqm, "gqm", False, t)
                S.phase(9)
                prenorm_compute(s, gpf, "gpf")
                if s >= 1:
                    prenorm_pe(s - 1, (s - 1) % 2)
                S.phase(8)
            w_release(wj)
            w_release(wj + 1)
            wj += 2

            S.phase(9)
            prenorm_pe(3, 1)

            S.phase(10)
            fc = 0
            for tt in range(6):
                wtg, wrg = w_acquire(wj)
                wtu, wru = w_acquire(wj + 1)
                for c4 in range(4 if tt < 5 else 2):
                    pp = 1 + (fc % 3)
                    for kc in range(8):
                        S.op("pe", lambda e, kc=kc, c4=c4, pp=pp, wtg=wtg: e.matmul(
                            PP[pp][:, 0, :], lhsT=wtg[:, kc, c4 * 128:(c4 + 1) * 128], rhs=aT[:, kc, :], start=(kc == 0), stop=(kc == 7)),
                            reads=AT_ALL + [wrg], writes=[bres(2 * pp)])
                    for kc in range(8):
                        S.op("pe", lambda e, kc=kc, c4=c4, pp=pp, wtu=wtu: e.matmul(
                            PP[pp][:, 1, :], lhsT=wtu[:, kc, c4 * 128:(c4 + 1) * 128], rhs=aT[:, kc, :], start=(kc == 0), stop=(kc == 7)),
                            reads=AT_ALL + [wru], writes=[bres(2 * pp + 1)])
                    sgt = rtmp[:, fc % 2, :]
                    A("act", lambda e, pp=pp, sgt=sgt: e.activation(out=sgt, in_=PP[pp][:, 0, :], func=AF.Silu),
                      reads=[bres(2 * pp)], writes=[("sgt", fc % 2)], alias=[(G_TMP, 2, 4, True)])
                    A("dve", lambda e, pp=pp, sgt=sgt, fc=fc: e.tensor_tensor(out=actT[:, fc, :], in0=PP[pp][:, 1, :], in1=sgt, op=ALU.mult),
                      reads=[bres(2 * pp + 1), ("sgt", fc % 2)], writes=[("act", fc)], alias=[(G_TMP, 2, 4, False), (G_BIG, 1, 2, True)])
                    fc += 1
                w_release(wj)
                w_release(wj + 1)
                wj += 2

            S.phase(11)
            for half in range(2):
                for kg, nk in enumerate((8, 8, 6)):
                    wt, wr = w_acquire(wj)
                    for s in range(4):
                        for k in range(nk):
                            fcc = kg * 8 + k
                            A("pe", lambda e, s=s, k=k, fcc=fcc, half=half, wt=wt: e.matmul(
                                PP[s][:, half, :], lhsT=actT[:, fcc, s * 128:(s + 1) * 128], rhs=wt[:, k, :],
                                start=(fcc == 0), stop=(fcc == 21)),
                              reads=[("act", fcc), wr], writes=[bres(2 * s + half)], alias=[(G_BIG, 1, 2, False)])
                    w_release(wj)
                    wj += 1
            for s in range(4):
                postnorm_residual(s, s, gqf, "gqf", True, t)

        S.emit()
    return nc, S


_CACHE = {}


def _prep_consts(inp):
    f = lambda a: np.ascontiguousarray(np.asarray(a, dtype=np.float32))
    return {
        "w_in": f(inp["w_in"][0]),
        "w_bg": f(inp["w_branch_gla"][0]),
        "w_bs": f(inp["w_branch_sgu"][0]),
        "w_out": f(inp["w_out"][0]),
        "w_fi": f(inp["w_ffn_in"][0]),
        "w_fo": f(inp["w_ffn_out"][0]),
        "g_pm": f(inp["norm_pre_mix"][0]),
        "g_pf": f(inp["norm_pre_ffn"][0]),
        "g_qm": f(inp["norm_post_mix"][0]),
        "g_qf": f(inp["norm_post_ffn"][0]),
        "b_gate": f(inp["b_gate"][0]),
        "gla_norm": f(np.asarray(inp["gla_norm"][0]).reshape(-1)),
        "ln_g": f(np.asarray(inp["sgu_ln_g"][0]).reshape(-1)),
        "ln_b": f(np.asarray(inp["sgu_ln_b"][0]).reshape(-1)),
        "w_gu": f(inp["w_gate_up"][0]),
        "wspT": f(np.transpose(np.asarray(inp["w_spatial"][0]), (2, 0, 1))),
        "b_sp": f(np.asarray(inp["b_spatial"][0])),
    }


def kernel(**inputs):
    x = np.asarray(inputs["x"], dtype=np.float32)
    B, S_tok, _ = x.shape
    key = S_tok
    if key not in _CACHE:
        _CACHE[key] = build(S_tok)[0]
    nc = _CACHE[key]
    consts = _prep_consts(inputs)
    in_maps = []
    for b in range(B):
        m = dict(consts)
        m["x"] = np.ascontiguousarray(x[b])
        in_maps.append(m)
    res = run_bass_kernel_spmd(nc, in_maps, core_ids=list(range(B)))
    return np.stack([np.asarray(r["out"]) for r in res.results], axis=0).astype(np.float32)


def _simulate(S):
    ops = S.ops
    engs = ("pe", "act", "dve", "pool", "sp")
    streams = {e: [i for i, o in enumerate(ops) if o.eng == e] for e in engs}
    pos = {e: 0 for e in engs}
    done = set()
    progress = True
    while progress:
        progress = False
        for e in engs:
            while pos[e] < len(streams[e]):
                i = streams[e][pos[e]]
                if all(d in done for (_k, d) in ops[i].waits):
                    done.add(i)
                    pos[e] += 1
                    progress = True
                else:
                    break
    stuck = {e: (pos[e], len(streams[e])) for e in engs if pos[e] < len(streams[e])}
    return stuck
```

```python
import contextlib
import numpy as np
import concourse.bass as bass
import concourse.mybir as mybir
from concourse.bass_utils import run_bass_kernel_spmd

F32 = mybir.dt.float32
BF16 = mybir.dt.bfloat16
AF = mybir.ActivationFunctionType
ALU = mybir.AluOpType

D = 1024
DIN = 7184
DFF = 2816
T = 512
NB = 4
import os
KMODE = int(os.environ.get("KMODE", "0"))
STRICT = int(os.environ.get("STRICT", "1"))
EPS = 1e-6


class _Op:
    __slots__ = ("eng", "fn", "deps", "chan", "seq", "waits")

    def __init__(self, eng, fn, chan):
        self.eng = eng
        self.fn = fn
        self.chan = chan
        self.deps = {}
        self.seq = 0
        self.waits = []


class Sched:
    COMPUTE = ("pe", "act", "dve", "pool")

    def __init__(self, nc):
        self.nc = nc
        self.ops = []
        self.last_w = {}
        self.readers = {}
        self.upto = 99
        self.stopped = False

    def phase(self, k):
        self.stopped = k > self.upto

    def op(self, eng, fn, reads=(), writes=(), chan=None, force=False):
        if self.stopped and not force:
            return -1
        idx = len(self.ops)
        o = _Op(eng, fn, chan)
        for r in reads:
            w = self.last_w.get(r)
            if w is not None:
                o.deps[w] = True
        for r in writes:
            w = self.last_w.get(r)
            if w is not None and w not in o.deps:
                o.deps[w] = False
            for rd in self.readers.get(r, ()):
                if rd not in o.deps:
                    o.deps[rd] = False
        for r in reads:
            self.readers.setdefault(r, []).append(idx)
        for r in writes:
            self.last_w[r] = idx
            self.readers[r] = []
        self.ops.append(o)
        return idx

    def emit(self):
        nc = self.nc
        ops = self.ops
        engs = ("pe", "act", "dve", "pool", "sp")
        streams = {e: [] for e in engs}
        for i, o in enumerate(ops):
            streams[o.eng].append(i)
            o.seq = len(streams[o.eng])
        chan_cnt = {}
        dma_val = {}
        for i, o in enumerate(ops):
            if o.chan is not None:
                chan_cnt[o.chan] = chan_cnt.get(o.chan, 0) + 16
                dma_val[i] = chan_cnt[o.chan]
        need_sig = set()
        for e in engs:
            waited = {}
            for i in streams[e]:
                o = ops[i]
                for d, is_raw in sorted(o.deps.items()):
                    p = ops[d]
                    if p.chan is not None:
                        key = ("dma", p.chan)
                        val = dma_val[d]
                        if p.chan == "const":
                            val = chan_cnt[p.chan]
                            dma_val[d] = val
                    else:
                        if p.eng == e and o.chan is None:
                            if e == "pe" or (not is_raw and not STRICT):
                                continue
                        key = ("eng", p.eng)
                        val = p.seq
                    if waited.get(key, 0) >= val:
                        continue
                    waited[key] = val
                    o.waits.append((key, d))
                    if p.chan is None:
                        need_sig.add(d)
        rank = {}
        for e in self.COMPUTE:
            n = 0
            for i in streams[e]:
                if i in need_sig:
                    n += 1
                    rank[i] = n
        self.stats = {e: len(streams[e]) for e in engs}
        with contextlib.ExitStack() as st:
            esem = {e: st.enter_context(nc.semaphore("s_" + e)) for e in self.COMPUTE}
            csem = {}
            for n_, c in enumerate(chan_cnt):
                csem[c] = st.enter_context(nc.semaphore("c%d" % n_))
            block = st.enter_context(nc.Block())

            def run_stream(e, engobj):
                for i in streams[e]:
                    o = ops[i]
                    for key, d in o.waits:
                        if key[0] == "dma":
                            engobj.wait_ge(csem[key[1]], dma_val[d])
                        else:
                            engobj.wait_ge(esem[key[1]], rank[d])
                    ins = o.fn(engobj)
                    if o.chan is not None:
                        ins.then_inc(csem[o.chan], 16)
                    elif i in need_sig:
                        ins.then_inc(esem[e], 1)
                if e == "sp":
                    for c, v in chan_cnt.items():
                        engobj.wait_ge(csem[c], v)

            @block.tensor
            def _(eng):
                run_stream("pe", eng)

            @block.scalar
            def _(eng):
                run_stream("act", eng)

            @block.vector
            def _(eng):
                run_stream("dve", eng)

            @block.gpsimd
            def _(eng):
                run_stream("pool", eng)

            @block.sync
            def _(eng):
                run_stream("sp", eng)


def weight_tiles():
    tl = []
    tl.append(("w_in", 0, 8, 3072, 16))
    tl.append(("w_in", 0, 8, 0, 512))
    tl.append(("w_in", 0, 8, 1024, 512))
    tl.append(("w_in", 0, 8, 1536, 512))
    tl.append(("w_in", 0, 8, 512, 512))
    tl.append(("w_in", 0, 8, 2048, 512))
    tl.append(("w_in", 0, 8, 2560, 512))
    tl.append(("w_in", 0, 8, 3088, 512))
    tl.append(("w_in", 0, 8, 3600, 512))
    tl.append(("w_in", 0, 8, 4112, 512))
    tl.append(("w_in", 0, 8, 4624, 512))
    tl.append(("w_in", 0, 8, 5136, 512))
    tl.append(("w_in", 0, 8, 5648, 512))
    tl.append(("w_in", 0, 8, 6160, 512))
    tl.append(("w_in", 0, 8, 6672, 512))
    for half in range(2):
        tl.append(("w_bg", 0, 8, half * 512, 512))
        tl.append(("w_bs", 0, 8, half * 512, 512))
    tl.append(("w_out", 0, 8, 0, 512))
    tl.append(("w_out", 0, 8, 512, 512))
    for t in range(6):
        ncw = 512 if t < 5 else 256
        tl.append(("w_fi", 0, 8, t * 512, ncw))
        tl.append(("w_fi", 0, 8, DFF + t * 512, ncw))
    for half in range(2):
        for kg, nk in enumerate((8, 8, 6)):
            tl.append(("w_fo", kg * 8 * 128, nk, half * 512, 512))
    return tl


def build(S_tok, upto=99):
    NT = S_tok // T
    nc = bass.Bass("TRN2", target_bir_lowering=False)

    def din(name, shape, dt=F32):
        return nc.dram_tensor(name, shape, dt, kind="ExternalInput").ap()

    x = din("x", [S_tok, D])
    wsrc = {
        "w_in": din("w_in", [D, DIN]),
        "w_bg": din("w_bg", [D, D]),
        "w_bs": din("w_bs", [D, D]),
        "w_out": din("w_out", [D, D]),
        "w_fi": din("w_fi", [D, 2 * DFF]),
        "w_fo": din("w_fo", [DFF, D]),
    }
    wscr = {k: nc.dram_tensor(k + "_bf", list(v.shape), BF16, kind="Internal").ap() for k, v in wsrc.items()}
    c_gpm = din("g_pm", [D])
    c_gpf = din("g_pf", [D])
    c_gqm = din("g_qm", [D])
    c_gqf = din("g_qf", [D])
    c_bg = din("b_gate", [512])
    c_gn = din("gla_norm", [D])
    c_lng = din("ln_g", [D])
    c_lnb = din("ln_b", [D])
    c_wgu = din("w_gu", [16, 512])
    c_wsp = din("wspT", [128, 4, 128])
    c_bsp = din("b_sp", [4, 128])
    out = nc.dram_tensor("out", [S_tok, D], F32, kind="ExternalOutput").ap()

    S = Sched(nc)
    S.upto = upto
    with contextlib.ExitStack() as st:
        def sb(name, shape, dt):
            return st.enter_context(nc.sbuf_tensor(name, shape, dt))

        xt = sb("xt", [128, 4, D], F32)
        aT = sb("aT", [128, 8, T], BF16)
        xs = sb("xs", [128, D], BF16)
        xsB = sb("xsB", [128, D], BF16)
        xs2 = [xs, xsB]
        sr = sb("sr", [128, 4, D], BF16)
        alT = sb("alT", [16, T], F32)
        state = sb("state", [128, 4, 256], F32)
        stbf = sb("stbf", [128, 2, 4, 256], BF16)
        sgg = sb("sgg", [128, 8, T], BF16)
        sgs = sb("sgs", [128, 8, T], BF16)
        ytm = sb("ytm", [128, 2, D], F32)
        ygb = sb("ygb", [128, D], BF16)
        ygb2 = sb("ygb2", [128, D], BF16)
        ygbs = [ygb, ygb2]
        big = sb("big", [128, 24, T], BF16)
        rgate = sb("rgate", [128, 8192], BF16)
        rqk = sb("rqk", [128, 8, T], BF16)
        rtmp = sb("rtmp", [128, 4, T], F32)
        stat = sb("stat", [128, 64], F32)
        dec = sb("dec", [128, 32], F32)
        bnst = sb("bnst", [128, 4, 6], F32)
        bnag = sb("bnag", [128, 4, 2], F32)
        wring = [sb("wr%d" % i, [128, 8, 512], BF16) for i in range(NB)]
        gpm = sb("gpm", [128, D], F32)
        gpf = sb("gpf", [128, D], F32)
        gqm = sb("gqm", [128, D], F32)
        gqf = sb("gqf", [128, D], F32)
        gnb = sb("gnb", [128, D], F32)
        lng = sb("lng", [128, D], F32)
        lnb = sb("lnb", [128, D], F32)
        bgb = sb("bgb", [128, 512], F32)
        wgu = sb("wgu", [16, 512], F32)
        wspf = sb("wspf", [128, 4, 128], F32)
        wsp = sb("wsp", [128, 4, 128], BF16)
        bspb = sb("bspb", [128, 4, 128], F32)
        identf = sb("identf", [128, 128], F32)
        ident = sb("ident", [128, 128], BF16)
        mtri = sb("mtri", [128, 128], F32)
        ind = sb("ind", [128, 2], F32)
        mhalf = sb("mhalf", [128, 1], F32)
        junk = sb("junk", [128, D], BF16)
        fz = sb("fz", [128, 1], F32)
        PP = [st.enter_context(nc.psum_tensor("pp%d" % i, [128, 2, 512], F32)) for i in range(4)]

        def bank(b):
            return PP[b // 2][:, b % 2, :]

        def bres(b):
            return ("ps", b)

        vtm = big[:, 0:8, :].rearrange("p (s a) t -> p s (a t)", s=4)
        vn = big[:, 8:16, :].rearrange("p (s a) t -> p s (a t)", s=4)
        u = big[:, 16:24, :]
        actT = big
        loga = rgate[:, 0:4096].bitcast(F32).rearrange("p (s e) -> p s e", s=4)
        edec = rgate[:, 4096:8192].bitcast(F32).rearrange("p (s e) -> p s e", s=4)
        yglaT = rgate[:, 0:4096].rearrange("p (c t) -> p c t", c=8)
        ysguT = rgate[:, 4096:8192].rearrange("p (c t) -> p c t", c=8)
        qT = rqk[:, 0:4, :]
        kdec = rqk[:, 4:8, :]
        mergedT = rqk
        lg = rtmp[:, 0, :]
        e1 = rtmp[:, 1, :]

        def al(group, side, nsides, write):
            r = [("tok", group, side)]
            w = [("tok", group, j) for j in range(nsides) if j != side] if write else []
            return r, w

        def A(eng, fn, reads=(), writes=(), alias=(), chan=None):
            rr = list(reads)
            ww = list(writes)
            for (g, sd, n, wr) in alias:
                r_, w_ = al(g, sd, n, wr)
                rr += r_
                ww += w_
            return S.op(eng, fn, rr, ww, chan)

        G_BIG, G_GATE, G_QK, G_TMP = "big", "gate", "qk", "tmp"

        tiles = weight_tiles()
        NW = len(tiles)
        cstate = {"n": 0}

        def emit_casts(upto_):
            while cstate["n"] < min(NW, upto_):
                i = cstate["n"]
                wn, r0, nk, c0, ncw = tiles[i]
                S.op("pool", lambda e, wn=wn, r0=r0, nk=nk, c0=c0, ncw=ncw:
                     e.dma_start(out=wscr[wn][r0:r0 + nk * 128, c0:c0 + ncw], in_=wsrc[wn][r0:r0 + nk * 128, c0:c0 + ncw]),
                     writes=[("scr", i)], chan=("cast", i))
                cstate["n"] += 1

        def cload(dst, src, res):
            S.op("sp", lambda e: e.dma_start(out=dst, in_=src), writes=[res], chan="const")

        cload(gpm[:], c_gpm.partition_broadcast(128), "gpm")
        cload(gpf[:], c_gpf.partition_broadcast(128), "gpf")
        cload(gqm[:], c_gqm.partition_broadcast(128), "gqm")
        cload(gqf[:], c_gqf.partition_broadcast(128), "gqf")
        cload(gnb[:], c_gn.partition_broadcast(128), "gnb")
        cload(lng[:], c_lng.partition_broadcast(128), "lng")
        cload(lnb[:], c_lnb.partition_broadcast(128), "lnb")
        cload(bgb[:], c_bg.partition_broadcast(128), "bgb")
        cload(wgu[:], c_wgu, "wgu")
        cload(wspf[:], c_wsp, "wspf")
        cload(bspb[:], c_bsp.partition_broadcast(128), "bspb")

        S.op("pool", lambda e: e.memset(identf[:], 0.0), writes=["identf"])
        S.op("pool", lambda e: e.affine_select(out=identf[:], in_=identf[:], pattern=[[-1, 128]],
                                               compare_op=ALU.not_equal, fill=1.0, base=0, channel_multiplier=1),
             reads=["identf"], writes=["identf"])
        S.op("dve", lambda e: e.tensor_copy(out=ident[:], in_=identf[:]), reads=["identf"], writes=["ident"])
        S.op("pool", lambda e: e.memset(mhalf[:], -0.5), writes=["mhalf"])
        emit_casts(8)
        S.op("pool", lambda e: e.memset(mtri[:], -1.0 / 16), writes=["mtri"])
        S.op("pool", lambda e: e.affine_select(out=mtri[:], in_=mtri[:], pattern=[[-1, 128]],
                                               compare_op=ALU.is_gt, fill=0.0, base=0, channel_multiplier=1),
             reads=["mtri"], writes=["mtri"])
        S.op("pool", lambda e: e.memset(mtri[64:128, 0:64], 0.0), reads=["mtri"], writes=["mtri"])
        S.op("pool", lambda e: e.memset(ind[:], 0.0), writes=["ind"])
        S.op("pool", lambda e: e.memset(ind[0:64, 0:1], -1.0 / 16), reads=["ind"], writes=["ind"])
        S.op("pool", lambda e: e.memset(ind[64:128, 1:2], -1.0 / 16), reads=["ind"], writes=["ind"])
        S.op("pool", lambda e: e.memset(state[:], 0.0), writes=[("state", h) for h in range(4)])
        S.op("pool", lambda e: e.memset(wspf[64:128, :, 0:64], 0.0), reads=["wspf"], writes=["wspf"])
        S.op("dve", lambda e: e.tensor_copy(out=wsp[:], in_=wspf[:]), reads=["wspf"], writes=["wsp"])

        total_w = NT * NW
        wstate = {"loaded": 0}

        def emit_load(j):
            i = j % NW
            wn, r0, nk, c0, ncw = tiles[i]
            slot = j % NB
            S.op("sp", lambda e: e.dma_start(
                out=wring[slot][:, 0:nk, 0:ncw],
                in_=wscr[wn][r0:r0 + nk * 128, c0:c0 + ncw].rearrange("(k p) e -> p k e", p=128)),
                reads=[("scr", i)], writes=[("w", slot)], chan=("w", slot))

        def w_acquire(j):
            if j < NW:
                emit_casts(j + 10)
            while wstate["loaded"] <= j and wstate["loaded"] < total_w:
                assert wstate["loaded"] < j + NB
                emit_load(wstate["loaded"])
                wstate["loaded"] += 1
            return wring[j % NB], ("w", j % NB)

        def w_release(j):
            nxt = j + NB
            if nxt < total_w and wstate["loaded"] == nxt:
                emit_load(nxt)
                wstate["loaded"] += 1


        scnt = [0]

        def scol():
            c = scnt[0] % 64
            scnt[0] += 1
            return c

        def rstd_from(ss_col, n, res_in):
            c1 = scol()
            c2 = scol()
            S.op("dve", lambda e: e.tensor_scalar(out=stat[:, c1:c1 + 1], in0=stat[:, ss_col:ss_col + 1],
                                                  scalar1=1.0 / n, scalar2=EPS, op0=ALU.mult, op1=ALU.add),
                 reads=[res_in], writes=[("stat", c1)])
            S.op("pool", lambda e: e.tensor_tensor(out=stat[:, c2:c2 + 1], in0=stat[:, c1:c1 + 1], in1=mhalf[:], op=ALU.pow),
                 reads=[("stat", c1), "mhalf"], writes=[("stat", c2)])
            return c2, ("stat", c2)

        def prenorm_compute(s, gtile, gres):
            c0 = scol()
            xsb = xs2[s % 2]
            S.op("act", lambda e: e.activation(out=junk[:], in_=xt[:, s, :], func=AF.Square, accum_out=stat[:, c0:c0 + 1]),
                 reads=[("xt", s)], writes=[("stat", c0), "junk"])
            c2, r2 = rstd_from(c0, D, ("stat", c0))
            S.op("dve", lambda e: e.scalar_tensor_tensor(out=xsb[:], in0=xt[:, s, :], scalar=stat[:, c2:c2 + 1], in1=gtile[:],
                                                         op0=ALU.mult, op1=ALU.mult),
                 reads=[("xt", s), r2, gres], writes=[("xs", s % 2)])

        def prenorm_pe(s, tpb):
            xsb = xs2[s % 2]
            pv = bank(tpb).bitcast(BF16).rearrange("p (k t) -> p k t", k=8)
            for k in range(8):
                S.op("pe", lambda e, k=k: e.transpose(out=pv[:, k, :], in_=xsb[:, k * 128:(k + 1) * 128], identity=ident[:]),
                     reads=[("xs", s % 2), "ident"], writes=[bres(tpb)])
            S.op("act", lambda e: e.activation(out=aT[:, :, s * 128:(s + 1) * 128], in_=pv, func=AF.Copy),
                 reads=[bres(tpb)], writes=[("aT", s)])

        def prenorm_transposes(s, gtile, gres, tpb):
            prenorm_compute(s, gtile, gres)
            prenorm_pe(s, tpb)

        def postnorm_residual(s, pp, gtile, gres, final, t):
            src = PP[pp][:].rearrange("p a b -> p (a b)")
            c0 = scol()
            yb = ytm[:, s % 2, :]
            S.op("act", lambda e: e.activation(out=junk[:], in_=src, func=AF.Square, accum_out=stat[:, c0:c0 + 1]),
                 reads=[bres(2 * pp), bres(2 * pp + 1)], writes=[("stat", c0), "junk"])
            c2, r2 = rstd_from(c0, D, ("stat", c0))
            S.op("dve", lambda e: e.scalar_tensor_tensor(out=yb, in0=src, scalar=stat[:, c2:c2 + 1], in1=gtile[:],
                                                         op0=ALU.mult, op1=ALU.mult),
                 reads=[bres(2 * pp), bres(2 * pp + 1), r2, gres], writes=[("ytm", s % 2)])
            S.op("dve", lambda e: e.tensor_tensor(out=xt[:, s, :], in0=xt[:, s, :], in1=yb, op=ALU.add),
                 reads=[("xt", s), ("ytm", s % 2)], writes=[("xt", s)])
            if final:
                S.op("sp", lambda e: e.dma_start(out=out[t * T + s * 128:t * T + (s + 1) * 128, :], in_=xt[:, s, :]),
                     reads=[("xt", s)], chan=("st", s), force=True)

        def fence(res):
            S.op("dve", lambda e: e.memset(fz[:], 0.0), writes=list(res) + ["fz"])

        XT_ALL = [("xt", s) for s in range(4)]
        AT_ALL = [("aT", s) for s in range(4)]

        for t in range(NT):
            wj = t * NW
            S.phase(0)
            for s_ in range(4):
                S.op("sp", lambda e, t=t, s_=s_: e.dma_start(out=xt[:, s_, :], in_=x[t * T + s_ * 128:t * T + (s_ + 1) * 128, :]),
                     writes=[("xt", s_)], chan=("xld", s_))
            if t == 0:
                for j in range(min(NB, total_w)):
                    emit_load(j)
                    wstate["loaded"] += 1
            S.phase(1)
            for s in range(4):
                prenorm_transposes(s, gpm, "gpm", s % 2)

            S.phase(2)
            fb = [0]

            def fbank():
                b = 4 + (fb[0] % 4)
                fb[0] += 1
                return b

            def fm_items(evac):
                nonlocal wj
                j = wj
                wj += 1
                items = []
                for c4 in range(4):
                    def item(c4=c4, j=j):
                        wt, wr = w_acquire(j)
                        pb = fbank()
                        for kc in range(8):
                            S.op("pe", lambda e, kc=kc, c4=c4, pb=pb, wt=wt: e.matmul(
                                bank(pb), lhsT=wt[:, kc, c4 * 128:(c4 + 1) * 128], rhs=aT[:, kc, :],
                                start=(kc == 0), stop=(kc == 7)),
                                reads=AT_ALL + [wr], writes=[bres(pb)])
                        evac(c4, pb)
                        if c4 == 3:
                            w_release(j)
                    items.append(item)
                return items

            def tm_items(evac):
                nonlocal wj
                j = wj
                wj += 1
                items = []
                for s in range(4):
                    def item(s=s, j=j):
                        wt, wr = w_acquire(j)
                        pb = fbank()
                        for kc in range(8):
                            S.op("pe", lambda e, kc=kc, s=s, pb=pb, wt=wt: e.matmul(
                                bank(pb), lhsT=aT[:, kc, s * 128:(s + 1) * 128], rhs=wt[:, kc, :],
                                start=(kc == 0), stop=(kc == 7)),
                                reads=[("aT", s), wr], writes=[bres(pb)])
                        evac(s, pb)
                        if s == 3:
                            w_release(j)
                    items.append(item)
                return items

            wt, wr = w_acquire(wj)
            for kc in range(8):
                S.op("pe", lambda e, kc=kc, wt=wt: e.matmul(bank(2)[0:16, :], lhsT=wt[:, kc, 0:16], rhs=aT[:, kc, :],
                                                             start=(kc == 0), stop=(kc == 7)),
                     reads=AT_ALL + [wr], writes=[bres(2)])
            w_release(wj)
            wj += 1
            S.op("dve", lambda e: e.tensor_copy(out=alT[:], in_=bank(2)[0:16, :]), reads=[bres(2)], writes=["alT"])

            def gate_logits(s):
                pb = 2 + (s % 2)
                S.op("pe", lambda e, s=s, pb=pb: e.matmul(bank(pb), lhsT=alT[:, s * 128:(s + 1) * 128], rhs=wgu[:], start=True, stop=True),
                     reads=["alT", "wgu"], writes=[bres(pb)])
                lgs = rtmp[:, s % 2, :]
                e1s = rtmp[:, 2 + (s % 2), :]
                A("dve", lambda e, pb=pb, lgs=lgs: e.tensor_tensor(out=lgs, in0=bank(pb), in1=bgb[:], op=ALU.add),
                  reads=[bres(pb), "bgb"], writes=[("lg", s % 2)], alias=[(G_TMP, 0, 4, True)])
                A("act", lambda e, lgs=lgs, e1s=e1s: e.activation(out=e1s, in_=lgs, func=AF.Exp, scale=-1.0),
                  reads=[("lg", s % 2)], writes=[("e1", s % 2)], alias=[(G_TMP, 0, 4, True)])
                A("act", lambda e, s=s, e1s=e1s: e.activation(out=loga[:, s, :], in_=e1s, func=AF.Ln, bias=1.0),
                  reads=[("e1", s % 2)], writes=[("loga", s)], alias=[(G_TMP, 0, 4, False), (G_GATE, 0, 2, True)])

            def gate_cumsum(s):
                pb = 2 + (s % 2)
                S.op("pe", lambda e, s=s, pb=pb: e.matmul(bank(pb), lhsT=mtri[:], rhs=loga[:, s, :], start=True, stop=True),
                     reads=["mtri", ("loga", s), ("tok", G_GATE, 0)], writes=[bres(pb)])
                A("act", lambda e, s=s, pb=pb: e.activation(out=edec[:, s, :], in_=bank(pb), func=AF.Exp),
                  reads=[bres(pb)], writes=[("edec", s)], alias=[(G_GATE, 0, 2, True)])

            def ev_q(h, pb):
                A("act", lambda e: e.activation(out=qT[:, h, :], in_=bank(pb), func=AF.Copy, scale=float(128 ** -0.5)),
                  reads=[bres(pb)], writes=[("qT", h)], alias=[(G_QK, 0, 2, True)])

            def mk_ev_v(half):
                def ev(s, pb):
                    A("dve", lambda e: e.tensor_copy(out=vtm[:, s, half * 512:(half + 1) * 512], in_=bank(pb)),
                      reads=[bres(pb)], writes=[("vtm", s, half)], alias=[(G_BIG, 0, 2, True)])
                return ev

            def ev_k(s, pb):
                A("dve", lambda e: e.tensor_tensor(out=kdec[:, s, :], in0=bank(pb), in1=edec[:, s, :], op=ALU.mult),
                  reads=[bres(pb), ("edec", s)], writes=[("kdec", s)], alias=[(G_QK, 0, 2, True), (G_GATE, 0, 2, False)])

            def mk_ev_r(half):
                def ev(s, pb):
                    S.op("act", lambda e: e.activation(out=sr[:, s, half * 512:(half + 1) * 512], in_=bank(pb), func=AF.Silu),
                         reads=[bres(pb)], writes=[("sr", s, half)])
                return ev

            def mk_ev_su(half):
                def ev(c4, pb):
                    ch = half * 4 + c4
                    A("act", lambda e: e.activation(out=u[:, ch, :], in_=bank(pb), func=AF.Gelu_apprx_tanh),
                      reads=[bres(pb)], writes=[("u", ch)], alias=[(G_BIG, 0, 2, True)])
                return ev

            gvc = [0]

            def mk_ev_sv(half):
                def ev(s, pb):
                    gi = gvc[0] % 4
                    gvc[0] += 1
                    gv = rtmp[:, gi, :]
                    gres = ("gv", gi)
                    A("act", lambda e: e.activation(out=gv, in_=bank(pb), func=AF.Gelu_apprx_tanh),
                      reads=[bres(pb)], writes=[gres], alias=[(G_TMP, 3, 4, True)])
                    cs = []
                    for g2 in range(2):
                        S.op("dve", lambda e, g2=g2: e.bn_stats(out=bnst[:, g2, :], in_=gv[:, g2 * 256:(g2 + 1) * 256]),
                             reads=[gres, ("tok", G_TMP, 3)], writes=[("bnst", g2)])
                        S.op("dve", lambda e, g2=g2: e.bn_aggr(out=bnag[:, g2, :], in_=bnst[:, g2, :]),
                             reads=[("bnst", g2)], writes=[("bnag", g2)])
                        c1 = scol()
                        c2 = scol()
                        c3 = scol()
                        S.op("dve", lambda e, g2=g2, c1=c1, c3=c3: e.tensor_scalar(out=stat[:, c1:c1 + 1], in0=bnag[:, g2, 1:2], scalar1=EPS, scalar2=None, op0=ALU.add),
                             reads=[("bnag", g2)], writes=[("stat", c1)])
                        S.op("dve", lambda e, g2=g2, c3=c3: e.tensor_copy(out=stat[:, c3:c3 + 1], in_=bnag[:, g2, 0:1]),
                             reads=[("bnag", g2)], writes=[("stat", c3)])
                        S.op("pool", lambda e, c1=c1, c2=c2: e.tensor_tensor(out=stat[:, c2:c2 + 1], in0=stat[:, c1:c1 + 1], in1=mhalf[:], op=ALU.pow),
                             reads=[("stat", c1), "mhalf"], writes=[("stat", c2)])
                        cs.append((c2, c3))
                    for g2 in range(2):
                        c2, c3 = cs[g2]
                        A("dve", lambda e, g2=g2, c2=c2, c3=c3: e.tensor_scalar(out=gv[:, g2 * 256:(g2 + 1) * 256], in0=gv[:, g2 * 256:(g2 + 1) * 256],
                                                                             scalar1=stat[:, c3:c3 + 1], scalar2=stat[:, c2:c2 + 1],
                                                                             op0=ALU.subtract, op1=ALU.mult),
                          reads=[gres, ("stat", c3), ("stat", c2)], writes=[gres], alias=[(G_TMP, 3, 4, True)])
                    A("dve", lambda e: e.tensor_tensor(out=gv, in0=gv, in1=lng[:, half * 512:(half + 1) * 512], op=ALU.mult),
                      reads=[gres, "lng"], writes=[gres], alias=[(G_TMP, 3, 4, True)])
                    A("dve", lambda e: e.tensor_tensor(out=vn[:, s, half * 512:(half + 1) * 512], in0=gv, in1=lnb[:, half * 512:(half + 1) * 512], op=ALU.add),
                      reads=[gres, "lnb", ("tok", G_TMP, 3)], writes=[("vn", s, half)], alias=[(G_BIG, 0, 2, True)])
                return ev

            def mk_ev_sig(dst, name, half):
                def ev(c4, pb):
                    ch = half * 4 + c4
                    S.op("act", lambda e: e.activation(out=dst[:, ch, :], in_=bank(pb), func=AF.Sigmoid),
                         reads=[bres(pb)], writes=[(name, ch)])
                return ev

            gate_logits(0)
            gate_logits(1)
            S.phase(3)
            for it in fm_items(ev_q):
                it()
            S.phase(2)
            gate_cumsum(0)
            gate_cumsum(1)
            gate_logits(2)
            gate_logits(3)
            S.phase(3)
            for it in tm_items(mk_ev_v(0)):
                it()
            S.phase(2)
            gate_cumsum(2)
            gate_cumsum(3)
            for s in range(4):
                for h in range(4):
                    cc = (s * 4 + h) * 2
                    S.op("pe", lambda e, s=s, h=h, cc=cc: e.matmul(bank(2)[:, cc:cc + 2], lhsT=loga[:, s, h * 128:(h + 1) * 128], rhs=ind[:],
                                                                    start=True, stop=True),
                         reads=[("loga", s), "ind", ("tok", G_GATE, 0)], writes=[bres(2)])
            S.op("act", lambda e: e.activation(out=dec[:], in_=bank(2)[:, 0:32], func=AF.Exp), reads=[bres(2)], writes=["dec"])
            S.phase(3)
            for it in tm_items(mk_ev_v(1)):
                it()
            for it in tm_items(ev_k):
                it()

            S.phase(4)
            from collections import deque
            filler = deque()
            filler.extend(tm_items(mk_ev_r(0)))
            filler.extend(tm_items(mk_ev_r(1)))
            filler.extend(fm_items(mk_ev_su(0)))
            filler.extend(fm_items(mk_ev_su(1)))
            filler.extend(tm_items(mk_ev_sv(0)))
            filler.extend(tm_items(mk_ev_sv(1)))
            filler.extend(fm_items(mk_ev_sig(sgg, "sgg", 0)))
            filler.extend(fm_items(mk_ev_sig(sgg, "sgg", 1)))
            filler.extend(fm_items(mk_ev_sig(sgs, "sgs", 0)))
            filler.extend(fm_items(mk_ev_sig(sgs, "sgs", 1)))

            def pull(n):
                for _ in range(n):
                    if filler:
                        filler.popleft()()

            gla_on = S.upto >= 5
            if KMODE == 1:
                pull(len(filler))
            UQ = []
            OPS = []
            pending = deque()
            for s in range(4):
                opv = PP[1][:].rearrange("p a b -> p (a b)")
                for j in range(2):
                    par = j
                    rows = slice(64 * j, 64 * (j + 1))
                    if gla_on:
                        upds = []
                        for h in range(4):
                            ub = h // 2
                            uoff = (h % 2) * 256
                            upd = bank(ub)[:, uoff:uoff + 256]
                            upds.append((upd, bres(ub)))
                            A("pe", lambda e, s=s, h=h, rows=rows, upd=upd: e.matmul(
                                upd, lhsT=kdec[rows, s, h * 128:(h + 1) * 128], rhs=vtm[rows, s, h * 256:(h + 1) * 256], start=True, stop=True),
                              reads=[("kdec", s), ("vtm", s, h // 2)], writes=[bres(ub)], alias=[(G_QK, 0, 2, False), (G_BIG, 0, 2, False)])
                        for h in range(4):
                            upd, ures = upds[h]
                            dcol = (s * 4 + h) * 2 + j
                            S.op("dve", lambda e, h=h, upd=upd, dcol=dcol: e.scalar_tensor_tensor(
                                out=state[:, h, :], in0=state[:, h, :], scalar=dec[:, dcol:dcol + 1], in1=upd, op0=ALU.mult, op1=ALU.add),
                                reads=[("state", h), "dec", ures], writes=[("state", h)])
                            S.op("pool", lambda e, h=h, par=par: e.tensor_copy(out=stbf[:, par, h, :], in_=state[:, h, :]),
                                 reads=[("state", h)], writes=[("stbf", par, h)])
                    if KMODE == 0:
                        pull(4)
                    if j == 0 and pending:
                        pending.popleft()()
                    for h in range(4):
                        if not gla_on:
                            break
                        A("pe", lambda e, s=s, h=h, j=j, par=par, rows=rows: e.matmul(
                            PP[1][rows, :, :].rearrange("p a b -> p (a b)")[:, h * 256:(h + 1) * 256],
                            lhsT=qT[:, h, s * 128 + 64 * j:s * 128 + 64 * (j + 1)], rhs=stbf[:, par, h, :], start=True, stop=True),
                          reads=[("qT", h), ("stbf", par, h)], writes=[bres(2 + h // 2)], alias=[(G_QK, 0, 2, False)])
                    if KMODE == 2:
                        pull(4)
                if not gla_on:
                    continue
                yb = ytm[:, s % 2, :]
                c0s = []
                for h in range(4):
                    c0 = scol()
                    c0s.append(c0)
                    S.op("act", lambda e, h=h, c0=c0, opv=opv: e.activation(out=junk[:, h * 256:(h + 1) * 256], in_=opv[:, h * 256:(h + 1) * 256],
                                                               func=AF.Square, accum_out=stat[:, c0:c0 + 1]),
                         reads=[bres(2 + h // 2)], writes=[("stat", c0), "junk"])
                for h in range(4):
                    c2, r2 = rstd_from(c0s[h], 256, ("stat", c0s[h]))
                    S.op("dve", lambda e, h=h, c2=c2, yb=yb, opv=opv: e.scalar_tensor_tensor(
                        out=yb[:, h * 256:(h + 1) * 256], in0=opv[:, h * 256:(h + 1) * 256], scalar=stat[:, c2:c2 + 1],
                        in1=gnb[:, h * 256:(h + 1) * 256], op0=ALU.mult, op1=ALU.mult),
                        reads=[bres(2 + h // 2), r2, "gnb"], writes=[("ytm", s % 2)])
                S.op("dve", lambda e, s=s, yb=yb: e.tensor_tensor(out=ygbs[s % 2][:], in0=yb, in1=sr[:, s, :], op=ALU.mult),
                     reads=[("ytm", s % 2), ("sr", s, 0), ("sr", s, 1)], writes=[("ygb", s % 2)])
                def ytrans(s=s):
                    tb = fbank()
                    pv = bank(tb).bitcast(BF16).rearrange("p (k t) -> p k t", k=8)
                    for k in range(8):
                        S.op("pe", lambda e, k=k, pv=pv: e.transpose(out=pv[:, k, :], in_=ygbs[s % 2][:, k * 128:(k + 1) * 128], identity=ident[:]),
                             reads=[("ygb", s % 2), "ident"], writes=[bres(tb)])
                    A("act", lambda e, s=s, pv=pv: e.activation(out=yglaT[:, :, s * 128:(s + 1) * 128], in_=pv, func=AF.Copy),
                      reads=[bres(tb)], writes=[("ygla", s)], alias=[(G_GATE, 1, 2, True)])
                pending.append(ytrans)
            pull(4)
            while pending:
                pending.popleft()()
            pull(len(filler))

            S.phase(6)
            for s in range(4):
                pp = 1 + (s % 2)
                mv = PP[pp][:].rearrange("p a (c i) -> p (a c) i", c=4)
                for g in range(4):
                    for cc in range(2):
                        ch = g * 2 + cc
                        A("pe", lambda e, s=s, g=g, cc=cc, ch=ch, mv=mv: e.matmul(
                            mv[:, ch, :], lhsT=vn[:, s, g * 256 + cc * 128:g * 256 + (cc + 1) * 128], rhs=wsp[:, g, :], start=True, stop=True),
                          reads=[("vn", s, g // 2), "wsp"], writes=[bres(2 * pp + ch // 4)], alias=[(G_BIG, 0, 2, False)])
                yb4 = ytm[:, s % 2, :].rearrange("p (g c i) -> p g c i", g=4, c=2)
                S.op("dve", lambda e, mv=mv, yb4=yb4: e.tensor_tensor(
                    out=yb4, in0=mv.rearrange("p (g c) i -> p g c i", g=4),
                    in1=bspb[:].unsqueeze(2).broadcast_to([128, 4, 2, 128]), op=ALU.add),
                    reads=[bres(2 * pp), bres(2 * pp + 1), "bspb"], writes=[("ytm", s % 2)])
                A("dve", lambda e, s=s: e.tensor_tensor(out=ysguT[:, :, s * 128:(s + 1) * 128],
                                                        in0=ytm[:, s % 2, :].rearrange("p (c i) -> p c i", c=8),
                                                        in1=u[:, :, s * 128:(s + 1) * 128], op=ALU.mult),
                  reads=[("ytm", s % 2)] + [("u", ch) for ch in range(8)], writes=[("ysgu", s)],
                  alias=[(G_GATE, 1, 2, True), (G_BIG, 0, 2, False)])

            S.phase(7)
            YG = [("ygla", s) for s in range(4)]
            YS = [("ysgu", s) for s in range(4)]
            for half in range(2):
                wtg, wrg = w_acquire(wj)
                wts, wrs = w_acquire(wj + 1)
                for c4 in range(4):
                    ch = half * 4 + c4
                    pp = 1 + (ch % 3)
                    for kc in range(8):
                        A("pe", lambda e, kc=kc, c4=c4, pp=pp, wtg=wtg: e.matmul(
                            PP[pp][:, 0, :], lhsT=wtg[:, kc, c4 * 128:(c4 + 1) * 128], rhs=yglaT[:, kc, :], start=(kc == 0), stop=(kc == 7)),
                          reads=YG + [wrg], writes=[bres(2 * pp)], alias=[(G_GATE, 1, 2, False)])
                    for kc in range(8):
                        A("pe", lambda e, kc=kc, c4=c4, pp=pp, wts=wts: e.matmul(
                            PP[pp][:, 1, :], lhsT=wts[:, kc, c4 * 128:(c4 + 1) * 128], rhs=ysguT[:, kc, :], start=(kc == 0), stop=(kc == 7)),
                          reads=YS + [wrs], writes=[bres(2 * pp + 1)], alias=[(G_GATE, 1, 2, False)])
                    t1 = rtmp[:, (ch % 2) * 2, :]
                    t2 = rtmp[:, (ch % 2) * 2 + 1, :]
                    A("dve", lambda e, ch=ch, pp=pp, t1=t1: e.tensor_tensor(out=t1, in0=PP[pp][:, 0, :], in1=sgg[:, ch, :], op=ALU.mult),
                      reads=[bres(2 * pp), ("sgg", ch)], writes=[("t1", ch % 2)], alias=[(G_TMP, 1, 4, True)])
                    A("dve", lambda e, ch=ch, pp=pp, t2=t2: e.tensor_tensor(out=t2, in0=PP[pp][:, 1, :], in1=sgs[:, ch, :], op=ALU.mult),
                      reads=[bres(2 * pp + 1), ("sgs", ch)], writes=[("t2", ch % 2)], alias=[(G_TMP, 1, 4, True)])
                    A("pool", lambda e, ch=ch, t1=t1, t2=t2: e.tensor_tensor(out=mergedT[:, ch, :], in0=t1, in1=t2, op=ALU.add),
                      reads=[("t1", ch % 2), ("t2", ch % 2)], writes=[("merged", ch)], alias=[(G_TMP, 1, 4, False), (G_QK, 1, 2, True)])
                w_release(wj)
                w_release(wj + 1)
                wj += 2

            S.phase(8)
            MG = [("merged", ch) for ch in range(8)]
            wt0, wr0 = w_acquire(wj)
            wt1, wr1 = w_acquire(wj + 1)
            for s in range(4):
                pp = 1 + (s % 3)
                for half, (wt, wr) in enumerate(((wt0, wr0), (wt1, wr1))):
                    for kc in range(8):
                        A("pe", lambda e, kc=kc, s=s, pp=pp, half=half, wt=wt: e.matmul(
                            PP[pp][:, half, :], lhsT=mergedT[:, kc, s * 128:(s + 1) * 128], rhs=wt[:, kc, :], start=(kc == 0), stop=(kc == 7)),
                          reads=MG + [wr], writes=[bres(2 * pp + half)], alias=[(G_QK, 1, 2, False)])
                postnorm_residual(s, pp, gqm, "gqm", False, t)
                S.phase(9)
                prenorm_compute(s, gpf, "gpf")
                if s >= 1:
                    prenorm_pe(s - 1, (s - 1) % 2)
                S.phase(8)
            w_release(wj)
            w_release(wj + 1)
            wj += 2

            S.phase(9)
            prenorm_pe(3, 1)

            S.phase(10)
            fc = 0
            for tt in range(6):
                wtg, wrg = w_acquire(wj)
                wtu, wru = w_acquire(wj + 1)
                for c4 in range(4 if tt < 5 else 2):
                    pp = 1 + (fc % 3)
                    for kc in range(8):
                        S.op("pe", lambda e, kc=kc, c4=c4, pp=pp, wtg=wtg: e.matmul(
                            PP[pp][:, 0, :], lhsT=wtg[:, kc, c4 * 128:(c4 + 1) * 128], rhs=aT[:, kc, :], start=(kc == 0), stop=(kc == 7)),
                            reads=AT_ALL + [wrg], writes=[bres(2 * pp)])
                    for kc in range(8):
                        S.op("pe", lambda e, kc=kc, c4=c4, pp=pp, wtu=wtu: e.matmul(
                            PP[pp][:, 1, :], lhsT=wtu[:, kc, c4 * 128:(c4 + 1) * 128], rhs=aT[:, kc, :], start=(kc == 0), stop=(kc == 7)),
                            reads=AT_ALL + [wru], writes=[bres(2 * pp + 1)])
                    sgt = rtmp[:, fc % 2, :]
                    A("act", lambda e, pp=pp, sgt=sgt: e.activation(out=sgt, in_=PP[pp][:, 0, :], func=AF.Silu),
                      reads=[bres(2 * pp)], writes=[("sgt", fc % 2)], alias=[(G_TMP, 2, 4, True)])
                    A("dve", lambda e, pp=pp, sgt=sgt, fc=fc: e.tensor_tensor(out=actT[:, fc, :], in0=PP[pp][:, 1, :], in1=sgt, op=ALU.mult),
                      reads=[bres(2 * pp + 1), ("sgt", fc % 2)], writes=[("act", fc)], alias=[(G_TMP, 2, 4, False), (G_BIG, 1, 2, True)])
                    fc += 1
                w_release(wj)
                w_release(wj + 1)
                wj += 2

            S.phase(11)
            for half in range(2):
                for kg, nk in enumerate((8, 8, 6)):
                    wt, wr = w_acquire(wj)
                    for s in range(4):
                        for k in range(nk):
                            fcc = kg * 8 + k
                            A("pe", lambda e, s=s, k=k, fcc=fcc, half=half, wt=wt: e.matmul(
                                PP[s][:, half, :], lhsT=actT[:, fcc, s * 128:(s + 1) * 128], rhs=wt[:, k, :],
                                start=(fcc == 0), stop=(fcc == 21)),
                              reads=[("act", fcc), wr], writes=[bres(2 * s + half)], alias=[(G_BIG, 1, 2, False)])
                    w_release(wj)
                    wj += 1
            for s in range(4):
                postnorm_residual(s, s, gqf, "gqf", True, t)

        S.emit()
    return nc, S


_CACHE = {}


def _prep_consts(inp):
    f = lambda a: np.ascontiguousarray(np.asarray(a, dtype=np.float32))
    return {
        "w_in": f(inp["w_in"][0]),
        "w_bg": f(inp["w_branch_gla"][0]),
        "w_bs": f(inp["w_branch_sgu"][0]),
        "w_out": f(inp["w_out"][0]),
        "w_fi": f(inp["w_ffn_in"][0]),
        "w_fo": f(inp["w_ffn_out"][0]),
        "g_pm": f(inp["norm_pre_mix"][0]),
        "g_pf": f(inp["norm_pre_ffn"][0]),
        "g_qm": f(inp["norm_post_mix"][0]),
        "g_qf": f(inp["norm_post_ffn"][0]),
        "b_gate": f(inp["b_gate"][0]),
        "gla_norm": f(np.asarray(inp["gla_norm"][0]).reshape(-1)),
        "ln_g": f(np.asarray(inp["sgu_ln_g"][0]).reshape(-1)),
        "ln_b": f(np.asarray(inp["sgu_ln_b"][0]).reshape(-1)),
        "w_gu": f(inp["w_gate_up"][0]),
        "wspT": f(np.transpose(np.asarray(inp["w_spatial"][0]), (2, 0, 1))),
        "b_sp": f(np.asarray(inp["b_spatial"][0])),
    }


def kernel(**inputs):
    x = np.asarray(inputs["x"], dtype=np.float32)
    B, S_tok, _ = x.shape
    key = S_tok
    if key not in _CACHE:
        _CACHE[key] = build(S_tok)[0]
    nc = _CACHE[key]
    consts = _prep_consts(inputs)
    in_maps = []
    for b in range(B):
        m = dict(consts)
        m["x"] = np.ascontiguousarray(x[b])
        in_maps.append(m)
    res = run_bass_kernel_spmd(nc, in_maps, core_ids=list(range(B)))
    return np.stack([np.asarray(r["out"]) for r in res.results], axis=0).astype(np.float32)


def _simulate(S):
    ops = S.ops
    engs = ("pe", "act", "dve", "pool", "sp")
    streams = {e: [i for i, o in enumerate(ops) if o.eng == e] for e in engs}
    pos = {e: 0 for e in engs}
    done = set()
    progress = True
    while progress:
        progress = False
        for e in engs:
            while pos[e] < len(streams[e]):
                i = streams[e][pos[e]]
                if all(d in done for (_k, d) in ops[i].waits):
                    done.add(i)
                    pos[e] += 1
                    progress = True
                else:
                    break
    stuck = {e: (pos[e], len(streams[e])) for e in engs if pos[e] < len(streams[e])}
    return stuck
```
